# Optimizing a Trainium2 kernel written in Bass

```python
import jax, jax.numpy as jnp
from jax import lax
import numpy as np

D_MODEL = 2048
BATCH = 1
SEQ = 16384
DEPTH = 4
DEC_BATCH = 32
DEC_SEQ = 16
PAST_LEN = 1024

CHUNK = 64
N_MIXERS = 3
N_A_LAYERS = (DEPTH + 2) // 3
N_B_LAYERS = (DEPTH + 1) // 3
N_C_LAYERS = DEPTH // 3
NORM_EPS = 1e-6

A_WIDTH = 2 * D_MODEL
A_GROUPS = 16
A_GROUP_DIM = A_WIDTH // A_GROUPS
A_CHUNK = 128

B_D_INNER = 2 * D_MODEL
B_HEAD_DIM = 64
B_HEADS = B_D_INNER // B_HEAD_DIM
B_GROUPS = 8
B_HEADS_PER_GROUP = B_HEADS // B_GROUPS
B_STATE = 128
B_CONV = 4
B_CONV_DIM = B_D_INNER + 2 * B_GROUPS * B_STATE
B_IN_DIM = B_D_INNER + B_CONV_DIM + B_HEADS
B_SCAN_CHUNK = CHUNK

C_HEADS = 16
C_HEAD_DIM = D_MODEL // C_HEADS
C_WIDTH = C_HEADS * C_HEAD_DIM
C_BLOCK = 128

kernel_name = 'hybrid_gmlp_ssd_stickbreak_stream_step'


def rms_norm(x, w):
    xf = x.astype(jnp.float32)
    y = xf * lax.rsqrt(jnp.mean(xf * xf, axis=-1, keepdims=True) + NORM_EPS)
    return (y * w.astype(jnp.float32)).astype(x.dtype)


def gmlp_mixer(h, w_in, ln_g, ln_b, w_s, b_s, w_out):
    bsz, L, _ = h.shape
    u, v, z = jnp.split(h @ w_in, 3, axis=-1)
    u = jax.nn.gelu(u, approximate=False)
    vf = jax.nn.gelu(v, approximate=False).astype(jnp.float32)
    vc = vf - jnp.mean(vf, axis=-1, keepdims=True)
    var = jnp.mean(vc * vc, axis=-1, keepdims=True)
    v = (vc * lax.rsqrt(var + NORM_EPS) * ln_g + ln_b).astype(h.dtype)
    lc = min(L, A_CHUNK)
    pos = jnp.arange(lc)
    mask = (pos[None, :] // CHUNK) <= (pos[:, None] // CHUNK)
    w_pos = jnp.where(mask[None], w_s[:, :lc, :lc], 0.0)
    vg = v.reshape(bsz, L // lc, lc, A_GROUPS, A_GROUP_DIM)
    s = jnp.einsum('gts,bcsgk->bctgk', w_pos, vg) + b_s[:, :lc].T[None, None, :, :, None]
    y = u * s.reshape(bsz, L, A_WIDTH) * jax.nn.silu(z)
    return y @ w_out, v


def ssd_scan(x, dt, a, bm, cm, s0):
    bsz, L = x.shape[:2]
    lc = min(L, B_SCAN_CHUNK)
    nc = L // lc

    def to_chunks(t):
        return jnp.moveaxis(t.reshape(bsz, nc, lc, *t.shape[2:]), 1, 0)

    tri = jnp.tril(jnp.ones((lc, lc), dtype=bool))

    def step(s, inp):
        xc, dtc, bc, cc = inp
        cum = jnp.cumsum(dtc * a, axis=1)
        seg = cum[:, :, None] - cum[:, None, :]
        decay = jnp.exp(jnp.where(tri[None, :, :, None, None], seg, -jnp.inf))
        cb = jnp.einsum('btgn,bsgn->btsg', cc, bc)
        y = jnp.einsum('btsg,btsgr,bsgr,bsgrp->btgrp', cb, decay, dtc, xc)
        y = y + jnp.einsum('btgn,bgrpn,btgr->btgrp', cc, s, jnp.exp(cum))
        w_end = jnp.exp(cum[:, -1:] - cum) * dtc
        s = s * jnp.exp(cum[:, -1])[..., None, None] + jnp.einsum('bsgn,bsgr,bsgrp->bgrpn', bc, w_end, xc)
        return s, y

    s_final, ys = lax.scan(step, s0, (to_chunks(x), to_chunks(dt), to_chunks(bm), to_chunks(cm)))
    y = jnp.moveaxis(ys, 0, 1).reshape(x.shape)
    return y, s_final


def mamba_mixer(h, ssm_state, conv_state, w_in, conv_w, conv_b, dt_bias, a_log, d_skip, norm_w, w_out):
    bsz, L, _ = h.shape
    proj = h @ w_in
    z = proj[..., :B_D_INNER]
    xbc = proj[..., B_D_INNER:B_D_INNER + B_CONV_DIM]
    dt = proj[..., B_D_INNER + B_CONV_DIM:]
    xpad = jnp.concatenate([conv_state.astype(xbc.dtype), xbc], axis=1)
    new_conv = xpad[:, -(B_CONV - 1):]
    conv = conv_b
    for k in range(B_CONV):
        conv = conv + xpad[:, k:k + L] * conv_w[k]
    xbc = jax.nn.silu(conv)
    x = xbc[..., :B_D_INNER]
    bm = xbc[..., B_D_INNER:B_D_INNER + B_GROUPS * B_STATE]
    cm = xbc[..., B_D_INNER + B_GROUPS * B_STATE:]
    f32 = jnp.float32
    dt = jax.nn.softplus(dt.astype(f32) + dt_bias.astype(f32))
    a = -jnp.exp(a_log.astype(f32)).reshape(B_GROUPS, B_HEADS_PER_GROUP)
    xh = x.astype(f32).reshape(bsz, L, B_GROUPS, B_HEADS_PER_GROUP, B_HEAD_DIM)
    y, s_new = ssd_scan(
        xh,
        dt.reshape(bsz, L, B_GROUPS, B_HEADS_PER_GROUP),
        a,
        bm.astype(f32).reshape(bsz, L, B_GROUPS, B_STATE),
        cm.astype(f32).reshape(bsz, L, B_GROUPS, B_STATE),
        ssm_state.astype(f32).reshape(bsz, B_GROUPS, B_HEADS_PER_GROUP, B_HEAD_DIM, B_STATE))
    y = y + d_skip.astype(f32).reshape(B_GROUPS, B_HEADS_PER_GROUP)[..., None] * xh
    y = y.reshape(bsz, L, B_D_INNER) * jax.nn.silu(z.astype(f32))
    yg = y.reshape(bsz, L, B_GROUPS, B_D_INNER // B_GROUPS)
    yg = yg * lax.rsqrt(jnp.mean(yg * yg, axis=-1, keepdims=True) + NORM_EPS)
    y = (yg.reshape(bsz, L, B_D_INNER) * norm_w.astype(f32)).astype(h.dtype)
    s_new = s_new.reshape(bsz, B_HEADS, B_HEAD_DIM, B_STATE).astype(h.dtype)
    return y @ w_out, s_new, new_conv


def sb_attend(q, k, v, q_pos, k_pos):
    z = jnp.einsum('bhtd,bhsd->bhts', q.astype(jnp.float32), k.astype(jnp.float32)) * (C_HEAD_DIM ** -0.5)
    visible = k_pos[None, :] < q_pos[:, None]
    log_keep = jnp.where(visible, jax.nn.log_sigmoid(-z), 0.0)
    log_surv = lax.cumsum(log_keep, axis=3, reverse=True) - log_keep
    w = jnp.where(visible, jnp.exp(jax.nn.log_sigmoid(z) + log_surv), 0.0)
    return jnp.einsum('bhts,bhsd->bhtd', w, v.astype(jnp.float32)).astype(q.dtype)


def sb_mixer(h, k_past, v_past, w_in, w_out):
    bsz, L, _ = h.shape
    q, k, v, z = jnp.split(h @ w_in, 4, axis=-1)

    def heads(t):
        return t.reshape(bsz, L, C_HEADS, C_HEAD_DIM).transpose(0, 2, 1, 3)

    q, k, v = heads(q), heads(k), heads(v)
    past = k_past.shape[2]
    k_all = jnp.concatenate([k_past.astype(k.dtype), k], axis=2)
    v_all = jnp.concatenate([v_past.astype(v.dtype), v], axis=2)
    k_pos = jnp.arange(past + L)
    blk = min(L, C_BLOCK)
    nb = L // blk
    qb = jnp.moveaxis(q.reshape(bsz, C_HEADS, nb, blk, C_HEAD_DIM), 2, 0)
    starts = past + jnp.arange(nb) * blk
    o = lax.map(lambda a: sb_attend(a[0], k_all, v_all, a[1] + jnp.arange(blk), k_pos), (qb, starts))
    o = jnp.moveaxis(o, 0, 2).reshape(bsz, C_HEADS, L, C_HEAD_DIM).transpose(0, 2, 1, 3).reshape(bsz, L, C_WIDTH)
    y = (o * jax.nn.silu(z)) @ w_out
    return y, k, v


def setup_inputs(seed: int = 0) -> dict:
    key = jax.random.key(seed)
    ks = jax.random.split(key, 26)
    f32 = jnp.float32

    def nrm(k, shape, scale):
        return jax.random.normal(k, shape, f32) * scale

    dt0 = jnp.exp(jax.random.uniform(ks[16], (N_B_LAYERS, B_HEADS), f32, np.log(1e-3), np.log(1e-1)))
    return {
        'x_prompt': nrm(ks[0], (BATCH, SEQ, D_MODEL), 1.0),
        'x_sample': nrm(ks[1], (DEC_BATCH, DEC_SEQ, D_MODEL), 1.0),
        'state_ssm': nrm(ks[2], (N_B_LAYERS, DEC_BATCH, B_HEADS, B_HEAD_DIM, B_STATE), 0.5),
        'state_conv': nrm(ks[3], (N_B_LAYERS, DEC_BATCH, B_CONV - 1, B_CONV_DIM), 1.0),
        'cache_k': nrm(ks[4], (N_C_LAYERS, DEC_BATCH, C_HEADS, PAST_LEN, C_HEAD_DIM), 1.0),
        'cache_v': nrm(ks[5], (N_C_LAYERS, DEC_BATCH, C_HEADS, PAST_LEN, C_HEAD_DIM), 1.0),
        'norm_w': 1.0 + nrm(ks[6], (DEPTH, D_MODEL), 0.02),
        'final_norm_w': 1.0 + nrm(ks[7], (D_MODEL,), 0.02),
        'a_w_in': nrm(ks[8], (N_A_LAYERS, D_MODEL, 3 * A_WIDTH), D_MODEL ** -0.5),
        'a_ln_g': 1.0 + nrm(ks[9], (N_A_LAYERS, A_WIDTH), 0.02),
        'a_ln_b': nrm(ks[10], (N_A_LAYERS, A_WIDTH), 0.02),
        'a_w_s': nrm(ks[11], (N_A_LAYERS, A_GROUPS, A_CHUNK, A_CHUNK), A_CHUNK ** -0.5),
        'a_b_s': 1.0 + nrm(ks[12], (N_A_LAYERS, A_GROUPS, A_CHUNK), 0.01),
        'a_w_out': nrm(ks[13], (N_A_LAYERS, A_WIDTH, D_MODEL), A_WIDTH ** -0.5),
        'b_w_in': nrm(ks[14], (N_B_LAYERS, D_MODEL, B_IN_DIM), D_MODEL ** -0.5),
        'b_conv_w': nrm(ks[15], (N_B_LAYERS, B_CONV, B_CONV_DIM), B_CONV ** -0.5),
        'b_conv_b': nrm(ks[17], (N_B_LAYERS, B_CONV_DIM), 0.02),
        'b_dt_bias': dt0 + jnp.log(-jnp.expm1(-dt0)),
        'b_a_log': jnp.log(jax.random.uniform(ks[18], (N_B_LAYERS, B_HEADS), f32, 1.0, 16.0)),
        'b_d': 1.0 + nrm(ks[19], (N_B_LAYERS, B_HEADS), 0.02),
        'b_norm_w': 1.0 + nrm(ks[20], (N_B_LAYERS, B_D_INNER), 0.02),
        'b_w_out': nrm(ks[21], (N_B_LAYERS, B_D_INNER, D_MODEL), B_D_INNER ** -0.5),
        'c_w_in': nrm(ks[22], (N_C_LAYERS, D_MODEL, 4 * C_WIDTH), D_MODEL ** -0.5),
        'c_w_out': nrm(ks[23], (N_C_LAYERS, C_WIDTH, D_MODEL), C_WIDTH ** -0.5),
    }


def reference(x_prompt, x_sample, state_ssm, state_conv, cache_k, cache_v,
              norm_w, final_norm_w,
              a_w_in, a_ln_g, a_ln_b, a_w_s, a_b_s, a_w_out,
              b_w_in, b_conv_w, b_conv_b, b_dt_bias, b_a_log, b_d, b_norm_w, b_w_out,
              c_w_in, c_w_out):
    xp, xs = x_prompt, x_sample
    gmlp_v_s, ssm_p, conv_p, ssm_s, conv_s = [], [], [], [], []
    k_p, v_p, k_s, v_s = [], [], [], []
    bp = xp.shape[0]
    for i in range(DEPTH):
        kind, j = i % N_MIXERS, i // N_MIXERS
        hp = rms_norm(xp, norm_w[i])
        hs = rms_norm(xs, norm_w[i])
        if kind == 0:
            a_par = (a_w_in[j], a_ln_g[j], a_ln_b[j], a_w_s[j], a_b_s[j], a_w_out[j])
            dp, _ = gmlp_mixer(hp, *a_par)
            ds, v_new = gmlp_mixer(hs, *a_par)
            gmlp_v_s.append(v_new)
        elif kind == 1:
            b_par = (b_w_in[j], b_conv_w[j], b_conv_b[j], b_dt_bias[j], b_a_log[j], b_d[j], b_norm_w[j], b_w_out[j])
            s0 = jnp.zeros((bp, B_HEADS, B_HEAD_DIM, B_STATE), xp.dtype)
            c0 = jnp.zeros((bp, B_CONV - 1, B_CONV_DIM), xp.dtype)
            dp, s1, c1 = mamba_mixer(hp, s0, c0, *b_par)
            ds, s2, c2 = mamba_mixer(hs, state_ssm[j], state_conv[j], *b_par)
            ssm_p.append(s1)
            conv_p.append(c1)
            ssm_s.append(s2)
            conv_s.append(c2)
        else:
            empty = jnp.zeros((bp, C_HEADS, 0, C_HEAD_DIM), xp.dtype)
            dp, k1, v1 = sb_mixer(hp, empty, empty, c_w_in[j], c_w_out[j])
            ds, k2, v2 = sb_mixer(hs, cache_k[j], cache_v[j], c_w_in[j], c_w_out[j])
            k_p.append(k1)
            v_p.append(v1)
            k_s.append(k2)
            v_s.append(v2)
        xp = xp + dp
        xs = xs + ds
    y_prompt = rms_norm(xp, final_norm_w)
    y_sample = rms_norm(xs, final_norm_w)
    return (y_prompt, y_sample, jnp.stack(gmlp_v_s), jnp.stack(ssm_p), jnp.stack(conv_p),
            jnp.stack(ssm_s), jnp.stack(conv_s), jnp.stack(k_p), jnp.stack(v_p),
            jnp.stack(k_s), jnp.stack(v_s))
```

```python
import numpy as np
from contextlib import ExitStack
import concourse.bass as bass
import concourse.mybir as mybir
from concourse.bass_utils import run_bass_kernel_spmd

F32 = mybir.dt.float32
BF16 = mybir.dt.bfloat16
AF = mybir.ActivationFunctionType
ALU = mybir.AluOpType
AX = mybir.AxisListType

NCORES = 8
D = 2048
SEQ = 16384
NSB = 32
NST = 16
TP = SEQ // NCORES
TS = NSB * NST // NCORES
TT = TP + TS
AW = 4096
EPS = 1e-6
DMA_RING = 12
EPOCH = 6000


class Buf:
    __slots__ = ("name", "w", "r", "excl")

    def __init__(self, name, excl=False):
        self.name = name
        self.w = None
        self.r = []
        self.excl = excl


class K:
    def __init__(self, nc, stack):
        self.nc = nc
        self.eng = {"pe": nc.tensor, "act": nc.scalar, "dve": nc.vector,
                    "pool": nc.gpsimd, "sp": nc.sync}
        self.sems = {}
        self.cnt = {}
        self.stack = stack
        self.epoch = {}
        for e in ("pe", "act", "dve", "pool"):
            self.sems[(e, 0)] = stack.enter_context(nc.semaphore("s_" + e + "_0"))
            self.cnt[e] = 0
            self.epoch[e] = 0
        self.dq = {}
        for q, e in (("sp", "sp"), ("poolq", "pool")):
            ring = [stack.enter_context(nc.semaphore(f"d_{q}_{i}")) for i in range(DMA_RING)]
            self.dq[q] = {"eng": e, "ring": ring, "n": 0}
            for i in range(DMA_RING):
                self.sems[(q, i)] = ring[i]
        self.waited = {}
        self.pending = {e: [] for e in ("pe", "act", "dve", "pool")}
        self.nb = 0
        self.ninstr = 0
        self.allbufs = []

    def sb(self, name, shape, dt):
        return self.nc.alloc_sbuf_tensor(name, list(shape), dt)

    def ps(self, name, shape, dt=F32):
        return self.nc.alloc_psum_tensor(name, list(shape), dt)

    def buf(self, name=None, excl=False):
        self.nb += 1
        b = Buf(name or f"b{self.nb}", excl)
        self.allbufs.append(b)
        return b

    def pbuf(self, name=None):
        return self.buf(name, excl=True)

    def _wait(self, engkey, ev):
        semkey, val, src = ev
        if src == engkey and engkey == "pe":
            return
        kk = (engkey, semkey)
        if self.waited.get(kk, 0) >= val:
            return
        self.waited[kk] = val
        self.eng[engkey].wait_ge(self.sems[semkey], val)
        self.ninstr += 1

    def _deps(self, engkey, reads, writes):
        for b in reads:
            if b.w is not None:
                self._wait(engkey, b.w)
            if b.excl:
                for ev in b.r:
                    if ev[2] != engkey:
                        self._wait(engkey, ev)
        for b in writes:
            if b.w is not None:
                self._wait(engkey, b.w)
            for ev in b.r:
                self._wait(engkey, ev)

    def _record(self, ev, reads, writes):
        for b in reads:
            b.r.append(ev)
            if len(b.r) > 16:
                d = {}
                for e in b.r:
                    if e[0] not in d or d[e[0]][1] < e[1]:
                        d[e[0]] = e
                b.r = list(d.values())
        for b in writes:
            b.w = ev
            b.r = []

    def op(self, engkey, fn, reads=(), writes=(), sig=True):
        reads = [b for b in reads if b is not None]
        writes = [b for b in writes if b is not None]
        self._deps(engkey, reads, writes)
        ins = fn()
        self.ninstr += 1
        if sig:
            if self.cnt[engkey] >= EPOCH:
                self.epoch[engkey] += 1
                self.cnt[engkey] = 0
                self.sems[(engkey, self.epoch[engkey])] = self.stack.enter_context(
                    self.nc.semaphore(f"s_{engkey}_{self.epoch[engkey]}"))
            self.cnt[engkey] += 1
            sk = (engkey, self.epoch[engkey])
            ins.then_inc(self.sems[sk], 1)
            ev = (sk, self.cnt[engkey], engkey)
            pr = self.pending[engkey]
            if pr:
                for b in pr:
                    b.r.append(ev)
                self.pending[engkey] = []
            self._record(ev, reads, writes)
        else:
            self.pending[engkey].extend(reads)
            self.pending[engkey].extend(writes)
        return ins

    def dma(self, q, out, in_, reads=(), writes=(), **kw):
        reads = [b for b in reads if b is not None]
        writes = [b for b in writes if b is not None]
        d = self.dq[q]
        engkey = d["eng"]
        i = d["n"]
        slot = i % DMA_RING
        rnd = i // DMA_RING
        if rnd > 0:
            self._wait(engkey, ((q, slot), 16 * rnd, q))
        self._deps(engkey, reads, writes)
        ins = self.eng[engkey].dma_start(out=out, in_=in_, **kw)
        ins.then_inc(d["ring"][slot], 16)
        self.ninstr += 1
        d["n"] = i + 1
        ev = ((q, slot), 16 * (rnd + 1), q)
        self._record(ev, reads, writes)
        return ev

    def finish(self):
        for b in self.allbufs:
            if b.w is not None:
                self._wait("sp", b.w)
            for ev in b.r:
                self._wait("sp", ev)
        for q, d in self.dq.items():
            n = d["n"]
            for i in range(max(0, n - DMA_RING), n):
                self._wait("sp", ((q, i % DMA_RING), 16 * (i // DMA_RING + 1), q))


def make_ident(k, nc):
    identf = k.sb("identf", [128, 128], F32)
    ident = k.sb("identb", [128, 128], BF16)
    bi = k.buf("ident")
    k.op("pool", lambda: nc.gpsimd.memset(identf[:], 0.0), writes=[bi])
    k.op("pool", lambda: nc.gpsimd.affine_select(out=identf[:], in_=identf[:], pattern=[[-1, 128]],
                                                 compare_op=ALU.not_equal, fill=1.0, base=0,
                                                 channel_multiplier=1), reads=[bi], writes=[bi])
    k.op("dve", lambda: nc.vector.tensor_copy(out=ident[:], in_=identf[:]), reads=[bi], writes=[bi])
    return identf, ident, bi


def rstd_from_sumsq(k, nc, out, ssq, n, rb, wb, np_=128):
    k.op("dve", lambda: nc.vector.tensor_scalar(out=out, in0=ssq, scalar1=1.0 / n, scalar2=EPS,
                                               op0=ALU.mult, op1=ALU.add), reads=rb, writes=wb)
    k.op("act", lambda: nc.scalar.activation(out=out, in_=out, func=AF.Sqrt), reads=wb, writes=wb)
    k.op("dve", lambda: nc.vector.reciprocal(out=out, in_=out), reads=wb, writes=wb)


def build_gmlp(final_norm):
    nc = bass.Bass("TRN2", target_bir_lowering=False)
    x = nc.dram_tensor("x", [TT, D], F32, kind="ExternalInput").ap()
    normw = nc.dram_tensor("normw", [D], F32, kind="ExternalInput").ap()
    w_in = nc.dram_tensor("w_in", [D, 3 * AW], F32, kind="ExternalInput").ap()
    ln_g = nc.dram_tensor("ln_g", [AW], F32, kind="ExternalInput").ap()
    ln_b = nc.dram_tensor("ln_b", [AW], F32, kind="ExternalInput").ap()
    w_s = nc.dram_tensor("w_s", [16, 128, 128], F32, kind="ExternalInput").ap()
    b_s = nc.dram_tensor("b_s", [16, 128], F32, kind="ExternalInput").ap()
    w_out = nc.dram_tensor("w_out", [AW, D], F32, kind="ExternalInput").ap()
    xo = nc.dram_tensor("xo", [TT, D], F32, kind="ExternalOutput").ap()
    vsT = nc.dram_tensor("vsT", [AW, TS], F32, kind="ExternalOutput").ap()
    if final_norm:
        fnw = nc.dram_tensor("fnw", [D], F32, kind="ExternalInput").ap()
        yo = nc.dram_tensor("yo", [TT, D], F32, kind="ExternalOutput").ap()

    with ExitStack() as st:
        k = K(nc, st)
        identf, ident, b_id = make_ident(k, nc)
        normw_bc = k.sb("normw_bc", [128, D], F32)
        b_const = k.buf("const")
        k.dma("sp", normw_bc[:], normw.partition_broadcast(128), writes=[b_const])
        if final_norm:
            fnw_bc = k.sb("fnw_bc", [128, D], F32)
            k.dma("sp", fnw_bc[:], fnw.partition_broadcast(128), writes=[b_const])
        ones_b = k.sb("ones_b", [128, 128], BF16)
        k.op("dve", lambda: nc.vector.memset(ones_b[:], 1.0), writes=[b_const])
        lnrow = k.sb("lnrow", [32, 2, 128], F32)
        k.dma("sp", lnrow[:, 0, :], ln_g.rearrange("(a p) -> a p", p=128), writes=[b_const])
        k.dma("sp", lnrow[:, 1, :], ln_b.rearrange("(a p) -> a p", p=128), writes=[b_const])
        lnT = k.sb("lnT", [128, 2, 32], F32)
        pA = k.ps("pA", [128, 512], F32)
        pB = k.ps("pB", [128, 512], F32)
        pC = k.ps("pC", [128, 512], F32)
        pD = k.ps("pD", [128, 512], F32)
        pT = k.ps("pT", [128, 1024], BF16)
        bpA, bpB, bpC, bpD, bpT = (k.pbuf(n) for n in ("pA", "pB", "pC", "pD", "pT"))
        for j in range(2):
            k.op("pe", lambda: nc.tensor.transpose(out=pA[:, j * 32:(j + 1) * 32], in_=lnrow[:, j, :],
                                                   identity=identf[0:32, 0:32]),
                 reads=[b_const, b_id], writes=[bpA])
        k.op("dve", lambda: nc.vector.tensor_copy(out=lnT[:].rearrange("p a b -> p (a b)"), in_=pA[:, 0:64]),
             reads=[bpA], writes=[b_const])
        wnat = [k.sb(f"wnat{i}", [128, 128], F32) for i in range(2)]
        b_wnat = [k.buf() for _ in range(2)]
        wposT = k.sb("wposT", [128, 16, 128], BF16)
        for g in range(16):
            pp = pA if g % 2 == 0 else pB
            bpp = bpA if g % 2 == 0 else bpB
            k.dma("sp", wnat[g % 2][:], w_s[g, :, :], writes=[b_wnat[g % 2]])
            k.op("pe", lambda: nc.tensor.transpose(out=pp[:, 0:128], in_=wnat[g % 2][:], identity=identf[:]),
                 reads=[b_wnat[g % 2], b_id], writes=[bpp])
            k.op("act", lambda: nc.scalar.copy(out=wposT[:, g, :], in_=pp[:, 0:128]), reads=[bpp], writes=[b_const])
        k.op("dve", lambda: nc.vector.memset(wposT[64:128, :, 0:64], 0.0), reads=[b_const], writes=[b_const])
        wposT_s = k.sb("wposT_s", [64, 16, 64], BF16)
        for g in range(16):
            pp = pA if g % 2 == 0 else pB
            bpp = bpA if g % 2 == 0 else bpB
            k.op("dve", lambda: nc.vector.memset(wnat[g % 2][:], 0.0), writes=[b_wnat[g % 2]])
            for b in range(4):
                k.dma("sp", wnat[g % 2][16 * b:16 * b + 16, 16 * b:16 * b + 16], w_s[g, 0:16, 0:16],
                      writes=[b_wnat[g % 2]])
            k.op("pe", lambda: nc.tensor.transpose(out=pp[0:64, 0:64], in_=wnat[g % 2][0:64, 0:64], identity=identf[0:64, 0:64]),
                 reads=[b_wnat[g % 2], b_id], writes=[bpp])
            k.op("act", lambda: nc.scalar.copy(out=wposT_s[:, g, :], in_=pp[0:64, 0:64]), reads=[bpp], writes=[b_const])
        bsbt = [k.sb(f"bsb{i}", [128, 128], F32) for i in range(2)]
        b_bsb = [k.buf() for _ in range(2)]
        biasT = k.sb("biasT", [128, 32, 128], F32)
        biasT_s = k.sb("biasT_s", [128, 32, 64], F32)
        for g in range(16):
            k.dma("sp", bsbt[g % 2][:], b_s[g, :].partition_broadcast(128), writes=[b_bsb[g % 2]])
            k.op("pe", lambda: nc.tensor.matmul(pC[:, 0:128], lhsT=ones_b[:], rhs=wposT[:, g, :], start=True, stop=True),
                 reads=[b_const], writes=[bpC])
            k.op("pe", lambda: nc.tensor.matmul(pC[:, 128:192], lhsT=ones_b[0:64, :], rhs=wposT_s[:, g, :], start=True, stop=True),
                 reads=[b_const], writes=[bpC])
            for j in range(2):
                ft = 2 * g + j
                k.op("dve", lambda: nc.vector.scalar_tensor_tensor(out=biasT[:, ft, :], in0=pC[:, 0:128],
                                                                   scalar=lnT[:, 1, ft:ft + 1], in1=bsbt[g % 2][:],
                                                                   op0=ALU.mult, op1=ALU.add),
                     reads=[bpC, b_const, b_bsb[g % 2]], writes=[b_const])
                for b in range(4):
                    k.op("dve", lambda: nc.vector.scalar_tensor_tensor(out=biasT_s[:, ft, 16 * b:16 * b + 16],
                                                                       in0=pC[:, 128 + 16 * b:128 + 16 * b + 16],
                                                                       scalar=lnT[:, 1, ft:ft + 1], in1=bsbt[g % 2][:, 0:16],
                                                                       op0=ALU.mult, op1=ALU.add),
                         reads=[bpC, b_const, b_bsb[g % 2]], writes=[b_const])

        TB = 512
        xt = k.sb("xt", [128, D], F32)
        hb = k.sb("hb", [128, D], BF16)
        hT = k.sb("hT", [128, 16, TB], BF16)
        gv = k.sb("gv", [128, 4, AW], BF16)
        yT = k.sb("yT", [128, 32, TB], BF16)
        wsl = [k.sb(f"wsl{i}", [128, 16, 512], BF16) for i in range(2)]
        wo = [k.sb(f"wo{i}", [128, 32, 256], BF16) for i in range(1)] * 2
        gvf = [k.sb(f"gvf{i}", [128, 512], F32) for i in range(2)]
        sqf = [k.sb(f"sqf{i}", [128, 512], F32) for i in range(2)]
        guf = gvf
        szf = sqf
        ssf = [k.sb(f"ssf{i}", [128, 512], F32) for i in range(2)]
        stats = k.sb("stats", [128, 4, 2, 8], F32)
        st2 = k.sb("st2", [128, 4, 8], F32)
        small = k.sb("small", [128, 8], F32)
        xc = [k.sb(f"xc{i}", [128, 256], F32) for i in range(2)]
        xn = [k.sb(f"xn{i}", [128, 256], F32) for i in range(2)]
        b_xt, b_hb, b_hT, b_gv, b_yT, b_stats, b_small = (k.buf(n) for n in
                                                          ("xt", "hb", "hT", "gv", "yT", "stats", "small"))
        b_wsl = [k.buf(f"wsl{i}") for i in range(2)]
        b_wo = [k.buf("wo0")] * 2
        b_gvf = [k.buf() for _ in range(2)]
        b_sqf = [k.buf() for _ in range(2)]
        b_guf = b_gvf
        b_szf = b_sqf
        b_ssf = [k.buf() for _ in range(2)]
        b_xc = [k.buf() for _ in range(2)]
        b_xn = [k.buf() for _ in range(2)]
        b_xo = k.buf("xo_dram")
        cnt = {"w": 0, "z": 0, "o": 0, "t": 0, "c": 0}

        def load_slab(dst, bdst, col0):
            for c4 in range(4):
                k.dma("poolq", dst[:, c4 * 4:(c4 + 1) * 4, :],
                      w_in[c4 * 512:(c4 + 1) * 512, col0:col0 + 512].rearrange("(kt p) c -> p kt c", p=128),
                      writes=[bdst])

        blocks = [(i * 512, 512) for i in range(TP // 512)] + [(TP, TS)]
        for (t0, nt) in blocks:
            samp = nt == TS
            ntile = max(1, nt // 128)
            P = min(nt, 128)
            for ti in range(ntile):
                r0 = t0 + ti * 128
                k.dma("sp", xt[:P, :], x[r0:r0 + P, :], writes=[b_xt])
                for q4 in range(4):
                    k.op("act", lambda: nc.scalar.activation(out=gvf[0][:P, :], in_=xt[:P, q4 * 512:(q4 + 1) * 512],
                                                             func=AF.Square), reads=[b_xt], writes=[b_gvf[0]])
                    k.op("dve", lambda: nc.vector.reduce_sum(out=small[:P, q4:q4 + 1], in_=gvf[0][:P, :], axis=AX.X),
                         reads=[b_gvf[0]], writes=[b_small])
                k.op("dve", lambda: nc.vector.reduce_sum(out=small[:P, 4:5], in_=small[:P, 0:4], axis=AX.X),
                     reads=[b_small], writes=[b_small])
                rstd_from_sumsq(k, nc, small[:P, 5:6], small[:P, 4:5], D, [b_small], [b_small])
                k.op("dve", lambda: nc.vector.scalar_tensor_tensor(out=hb[:P, :], in0=xt[:P, :], scalar=small[:P, 5:6],
                                                                   in1=normw_bc[:P, :], op0=ALU.mult, op1=ALU.mult),
                     reads=[b_xt, b_small, b_const], writes=[b_hb])
                for half in range(2):
                    for j in range(8):
                        kt = half * 8 + j
                        k.op("pe", lambda: nc.tensor.transpose(out=pT[:, j * 128:j * 128 + P], in_=hb[:P, kt * 128:(kt + 1) * 128],
                                                               identity=ident[:P, :P]),
                             reads=[b_hb, b_id], writes=[bpT])
                    k.op("act", lambda: nc.scalar.copy(
                        out=hT[:, half * 8:half * 8 + 8, ti * 128:ti * 128 + P],
                        in_=pT[:].rearrange("p (a b) -> p a b", b=128)[:, :, 0:P]),
                        reads=[bpT], writes=[b_hT])
            for vc in range(8):
                i = cnt["w"] % 2
                cnt["w"] += 1
                load_slab(wsl[i], b_wsl[i], AW + vc * 512)
                for ti in range(ntile):
                    pp, bpp = (pA, bpA) if (cnt["t"] % 2 == 0) else (pB, bpB)
                    j = cnt["t"] % 2
                    cnt["t"] += 1
                    for kt in range(16):
                        k.op("pe", lambda: nc.tensor.matmul(pp[:P, :], lhsT=hT[:, kt, ti * 128:ti * 128 + P],
                                                            rhs=wsl[i][:, kt, :], start=(kt == 0), stop=(kt == 15)),
                             reads=[b_hT, b_wsl[i]], writes=[bpp], sig=(kt == 15))
                    k.op("act", lambda: nc.scalar.activation(out=gvf[j][:P, :], in_=pp[:P, :], func=AF.Gelu),
                         reads=[bpp], writes=[b_gvf[j]])
                    k.op("act", lambda: nc.scalar.activation(out=sqf[j][:P, :], in_=gvf[j][:P, :], func=AF.Square),
                         reads=[b_gvf[j]], writes=[b_sqf[j]])
                    k.op("dve", lambda: nc.vector.reduce_sum(out=stats[:P, ti, 0, vc:vc + 1], in_=gvf[j][:P, :], axis=AX.X),
                         reads=[b_gvf[j]], writes=[b_stats])
                    k.op("dve", lambda: nc.vector.reduce_sum(out=stats[:P, ti, 1, vc:vc + 1], in_=sqf[j][:P, :], axis=AX.X),
                         reads=[b_sqf[j]], writes=[b_stats])
                    k.op("dve", lambda: nc.vector.tensor_copy(out=gv[:P, ti, vc * 512:(vc + 1) * 512], in_=gvf[j][:P, :]),
                         reads=[b_gvf[j]], writes=[b_gv])
            for ti in range(ntile):
                k.op("dve", lambda: nc.vector.reduce_sum(out=st2[:P, ti, 0:1], in_=stats[:P, ti, 0, :], axis=AX.X),
                     reads=[b_stats], writes=[b_stats])
                k.op("dve", lambda: nc.vector.reduce_sum(out=st2[:P, ti, 1:2], in_=stats[:P, ti, 1, :], axis=AX.X),
                     reads=[b_stats], writes=[b_stats])
                k.op("dve", lambda: nc.vector.tensor_scalar(out=st2[:P, ti, 2:3], in0=st2[:P, ti, 0:1], scalar1=1.0 / AW,
                                                           scalar2=None, op0=ALU.mult), reads=[b_stats], writes=[b_stats])
                k.op("dve", lambda: nc.vector.tensor_scalar(out=st2[:P, ti, 3:4], in0=st2[:P, ti, 1:2], scalar1=1.0 / AW,
                                                           scalar2=None, op0=ALU.mult), reads=[b_stats], writes=[b_stats])
                k.op("dve", lambda: nc.vector.tensor_tensor(out=st2[:P, ti, 4:5], in0=st2[:P, ti, 2:3], in1=st2[:P, ti, 2:3],
                                                           op=ALU.mult), reads=[b_stats], writes=[b_stats])
                k.op("dve", lambda: nc.vector.tensor_tensor(out=st2[:P, ti, 5:6], in0=st2[:P, ti, 3:4], in1=st2[:P, ti, 4:5],
                                                           op=ALU.subtract), reads=[b_stats], writes=[b_stats])
                k.op("dve", lambda: nc.vector.tensor_scalar(out=st2[:P, ti, 6:7], in0=st2[:P, ti, 5:6], scalar1=EPS,
                                                           scalar2=None, op0=ALU.add),
                     reads=[b_stats], writes=[b_stats])
                k.op("act", lambda: nc.scalar.activation(out=st2[:P, ti, 6:7], in_=st2[:P, ti, 6:7], func=AF.Sqrt),
                     reads=[b_stats], writes=[b_stats])
                k.op("dve", lambda: nc.vector.reciprocal(out=st2[:P, ti, 6:7], in_=st2[:P, ti, 6:7]),
                     reads=[b_stats], writes=[b_stats])
                k.op("dve", lambda: nc.vector.tensor_scalar(out=gv[:P, ti, :], in0=gv[:P, ti, :], scalar1=st2[:P, ti, 2:3],
                                                           scalar2=st2[:P, ti, 6:7], op0=ALU.subtract, op1=ALU.mult),
                     reads=[b_stats, b_gv], writes=[b_gv])
            if samp:
                for ft in range(32):
                    k.op("pe", lambda: nc.tensor.transpose(out=pT[:, 0:P], in_=gv[:P, 0, ft * 128:(ft + 1) * 128],
                                                           identity=ident[:P, :P]), reads=[b_gv, b_id], writes=[bpT])
                    k.op("act", lambda: nc.scalar.activation(out=ssf[ft % 2][:, 0:P], in_=pT[:, 0:P], func=AF.Identity,
                                                             bias=lnT[:, 1, ft:ft + 1], scale=lnT[:, 0, ft:ft + 1]),
                         reads=[bpT, b_const], writes=[b_ssf[ft % 2]])
                    k.dma("sp", vsT[ft * 128:(ft + 1) * 128, :], ssf[ft % 2][:, 0:P], reads=[b_ssf[ft % 2]], writes=[b_xo])
            for fc in range(8):
                load_slab(wsl[0], b_wsl[0], fc * 512)
                load_slab(wsl[1], b_wsl[1], 2 * AW + fc * 512)
                for fj in range(4):
                    ft = fc * 4 + fj
                    g = ft // 2
                    j = cnt["c"] % 2
                    cnt["c"] += 1
                    for kt in range(16):
                        k.op("pe", lambda: nc.tensor.matmul(pA[:, 0:nt], lhsT=wsl[0][:, kt, fj * 128:(fj + 1) * 128],
                                                            rhs=hT[:, kt, 0:nt], start=(kt == 0), stop=(kt == 15)),
                             reads=[b_hT, b_wsl[0]], writes=[bpA], sig=(kt == 15))
                    for kt in range(16):
                        k.op("pe", lambda: nc.tensor.matmul(pB[:, 0:nt], lhsT=wsl[1][:, kt, fj * 128:(fj + 1) * 128],
                                                            rhs=hT[:, kt, 0:nt], start=(kt == 0), stop=(kt == 15)),
                             reads=[b_hT, b_wsl[1]], writes=[bpB], sig=(kt == 15))
                    for ti in range(ntile):
                        if samp:
                            k.op("pe", lambda: nc.tensor.matmul(pC[:, 0:P], lhsT=gv[:P, 0, ft * 128:(ft + 1) * 128],
                                                                rhs=wposT_s[:, g, :], start=True, stop=True),
                                 reads=[b_gv, b_const], writes=[bpC])
                        else:
                            k.op("pe", lambda: nc.tensor.matmul(pC[:, ti * 128:(ti + 1) * 128],
                                                                lhsT=gv[:, ti, ft * 128:(ft + 1) * 128],
                                                                rhs=wposT[:, g, :], start=True, stop=True),
                                 reads=[b_gv, b_const], writes=[bpC])
                    k.op("act", lambda: nc.scalar.activation(out=guf[j][:, 0:nt], in_=pA[:, 0:nt], func=AF.Gelu),
                         reads=[bpA], writes=[b_guf[j]])
                    k.op("act", lambda: nc.scalar.activation(out=szf[j][:, 0:nt], in_=pB[:, 0:nt], func=AF.Silu),
                         reads=[bpB], writes=[b_szf[j]])
                    for ti in range(ntile):
                        bt = biasT_s[:, ft, :] if samp else biasT[:, ft, :]
                        k.op("dve", lambda: nc.vector.scalar_tensor_tensor(
                            out=ssf[j][:, ti * 128:ti * 128 + P], in0=pC[:, ti * 128:ti * 128 + P],
                            scalar=lnT[:, 0, ft:ft + 1], in1=bt, op0=ALU.mult, op1=ALU.add),
                            reads=[bpC, b_const], writes=[b_ssf[j]])
                    k.op("dve", lambda: nc.vector.tensor_tensor(out=ssf[j][:, 0:nt], in0=ssf[j][:, 0:nt], in1=guf[j][:, 0:nt],
                                                               op=ALU.mult), reads=[b_ssf[j], b_guf[j]], writes=[b_ssf[j]])
                    k.op("dve", lambda: nc.vector.tensor_tensor(out=yT[:, ft, 0:nt], in0=ssf[j][:, 0:nt], in1=szf[j][:, 0:nt],
                                                               op=ALU.mult), reads=[b_ssf[j], b_szf[j]], writes=[b_yT])
            for oc in range(8):
                i = cnt["o"] % 2
                cnt["o"] += 1
                for c4 in range(4):
                    k.dma("poolq", wo[i][:, c4 * 8:(c4 + 1) * 8, :],
                          w_out[c4 * 1024:(c4 + 1) * 1024, oc * 256:(oc + 1) * 256].rearrange("(kt p) c -> p kt c", p=128),
                          writes=[b_wo[i]])
                for ti in range(ntile):
                    r0 = t0 + ti * 128
                    j = cnt["t"] % 2
                    cnt["t"] += 1
                    pp, bpp = (pD, bpD) if j == 0 else (pB, bpB)
                    k.dma("sp", xc[j][:P, :], x[r0:r0 + P, oc * 256:(oc + 1) * 256], writes=[b_xc[j]])
                    for ft in range(32):
                        k.op("pe", lambda: nc.tensor.matmul(pp[:P, 0:256], lhsT=yT[:, ft, ti * 128:ti * 128 + P],
                                                            rhs=wo[i][:, ft, :], start=(ft == 0), stop=(ft == 31)),
                             reads=[b_yT, b_wo[i]], writes=[bpp], sig=(ft == 31))
                    k.op("dve", lambda: nc.vector.tensor_tensor(out=xn[j][:P, :], in0=pp[:P, 0:256], in1=xc[j][:P, :],
                                                               op=ALU.add), reads=[bpp, b_xc[j]], writes=[b_xn[j]])
                    k.dma("sp", xo[r0:r0 + P, oc * 256:(oc + 1) * 256], xn[j][:P, :], reads=[b_xn[j]], writes=[b_xo])
            if final_norm:
                for ti in range(ntile):
                    r0 = t0 + ti * 128
                    k.dma("sp", hT[:P, :, :].rearrange("p a b -> p (a b)").bitcast(F32)[:, 0:D], xo[r0:r0 + P, :],
                          reads=[b_xo], writes=[b_hT])
                    xfl = hT[:P, :, :].rearrange("p a b -> p (a b)").bitcast(F32)
                    for q4 in range(4):
                        k.op("act", lambda: nc.scalar.activation(out=gvf[0][:P, :], in_=xfl[:, q4 * 512:(q4 + 1) * 512],
                                                                 func=AF.Square), reads=[b_hT], writes=[b_gvf[0]])
                        k.op("dve", lambda: nc.vector.reduce_sum(out=small[:P, q4:q4 + 1], in_=gvf[0][:P, :], axis=AX.X),
                             reads=[b_gvf[0]], writes=[b_small])
                    k.op("dve", lambda: nc.vector.reduce_sum(out=small[:P, 4:5], in_=small[:P, 0:4], axis=AX.X),
                         reads=[b_small], writes=[b_small])
                    rstd_from_sumsq(k, nc, small[:P, 5:6], small[:P, 4:5], D, [b_small], [b_small])
                    k.op("dve", lambda: nc.vector.scalar_tensor_tensor(out=xt[:P, :], in0=xfl[:, 0:D], scalar=small[:P, 5:6],
                                                                       in1=fnw_bc[:P, :], op0=ALU.mult, op1=ALU.mult),
                         reads=[b_hT, b_small, b_const], writes=[b_xt])
                    k.dma("sp", yo[r0:r0 + P, :], xt[:P, :], reads=[b_xt], writes=[b_xo])
        k.finish()
    return nc


def _run(nc, in_maps):
    res = run_bass_kernel_spmd(nc, in_maps, core_ids=list(range(NCORES)))
    return res.results


def tok_shard(xp, xs, c):
    return np.ascontiguousarray(np.concatenate(
        [xp[c * TP:(c + 1) * TP], xs[c * 4:(c + 1) * 4].reshape(TS, -1)], axis=0))


def run_gmlp(xp, xs, normw, w_in, ln_g, ln_b, w_s, b_s, w_out, fnw=None):
    nc = build_gmlp(fnw is not None)
    in_maps = []
    for c in range(NCORES):
        m = {"x": tok_shard(xp, xs, c), "normw": normw, "w_in": w_in, "ln_g": ln_g, "ln_b": ln_b,
             "w_s": w_s, "b_s": b_s, "w_out": w_out}
        if fnw is not None:
            m["fnw"] = fnw
        in_maps.append(m)
    return _run(nc, in_maps)


NTOK = SEQ + NSB * NST
GC = 768


def build_mamba():
    nc = bass.Bass("TRN2", target_bir_lowering=False)
    xT = nc.dram_tensor("xT", [D, NTOK], F32, kind="ExternalInput").ap()
    normwT = nc.dram_tensor("normwT", [128, 16], F32, kind="ExternalInput").ap()
    w_z = nc.dram_tensor("w_z", [D, 512], F32, kind="ExternalInput").ap()
    w_xbc = nc.dram_tensor("w_xbc", [D, GC], F32, kind="ExternalInput").ap()
    w_dt = nc.dram_tensor("w_dt", [D, 8], F32, kind="ExternalInput").ap()
    convw = nc.dram_tensor("convw", [128, 6, 4], F32, kind="ExternalInput").ap()
    convb = nc.dram_tensor("convb", [128, 6], F32, kind="ExternalInput").ap()
    hp = nc.dram_tensor("hp", [3, 8], F32, kind="ExternalInput").ap()
    dexp = nc.dram_tensor("dexp", [512], F32, kind="ExternalInput").ap()
    nw = nc.dram_tensor("nw", [512], F32, kind="ExternalInput").ap()
    sT_in = nc.dram_tensor("sT_in", [NSB, 128, 512], F32, kind="ExternalInput").ap()
    cs_in = nc.dram_tensor("cs_in", [128, 6, NSB, 3], F32, kind="ExternalInput").ap()
    yn = nc.dram_tensor("yn", [NTOK, 512], F32, kind="ExternalOutput").ap()
    sT_p = nc.dram_tensor("sT_p", [128, 512], F32, kind="ExternalOutput").ap()
    cs_p = nc.dram_tensor("cs_p", [128, 6, 3], F32, kind="ExternalOutput").ap()
    sT_s = nc.dram_tensor("sT_s", [NSB, 128, 512], F32, kind="ExternalOutput").ap()
    cs_s = nc.dram_tensor("cs_s", [128, 6, NSB, 3], F32, kind="ExternalOutput").ap()

    with ExitStack() as st:
        k = K(nc, st)
        identf, ident, b_id = make_ident(k, nc)
        bc = k.buf("const")
        nwT = k.sb("nwT", [128, 16], F32)
        k.dma("sp", nwT[:], normwT[:, :], writes=[bc])
        cw = k.sb("cw", [128, 6, 4], F32)
        k.dma("sp", cw[:], convw[:, :, :], writes=[bc])
        cbias = k.sb("cbias", [128, 6], F32)
        k.dma("sp", cbias[:], convb[:, :], writes=[bc])
        hpb = k.sb("hpb", [128, 3, 8], F32)
        k.dma("sp", hpb[:].rearrange("p a b -> p (a b)"), hp.rearrange("a b -> (a b)").partition_broadcast(128), writes=[bc])
        a_bc = k.sb("a_bc", [128, 8], F32)
        k.op("act", lambda: nc.scalar.activation(out=a_bc[:], in_=hpb[:, 1, :], func=AF.Exp), reads=[bc], writes=[bc])
        k.op("dve", lambda: nc.vector.tensor_scalar(out=a_bc[:], in0=a_bc[:], scalar1=-1.0, scalar2=None, op0=ALU.mult),
             reads=[bc], writes=[bc])
        d_bc = k.sb("d_bc", [128, 512], F32)
        k.dma("sp", d_bc[:], dexp.partition_broadcast(128), writes=[bc])
        nw_bc = k.sb("nw_bc", [128, 512], F32)
        k.dma("sp", nw_bc[:], nw.partition_broadcast(128), writes=[bc])
        ones_b = k.sb("ones_b", [128, 128], BF16)
        k.op("dve", lambda: nc.vector.memset(ones_b[:], 1.0), writes=[bc])
        ones_f = k.sb("ones_f", [128, 128], F32)
        k.op("dve", lambda: nc.vector.memset(ones_f[:], 1.0), writes=[bc])
        tri_le = k.sb("tri_le", [128, 128], F32)
        k.op("pool", lambda: nc.gpsimd.affine_select(out=tri_le[:], in_=ones_f[:], pattern=[[1, 128]],
                                                     compare_op=ALU.is_ge, fill=0.0, base=0, channel_multiplier=-1),
             reads=[bc], writes=[bc])
        mgt = k.sb("mgt", [128, 128], F32)
        k.op("pool", lambda: nc.gpsimd.affine_select(out=mgt[:], in_=ones_f[:], pattern=[[-1, 128]],
                                                     compare_op=ALU.is_gt, fill=0.0, base=0, channel_multiplier=1),
             reads=[bc], writes=[bc])
        Wx = k.sb("Wx", [128, 16, GC], BF16)
        Wz = k.sb("Wz", [128, 16, 512], BF16)
        Wd = k.sb("Wd", [128, 16, 8], BF16)
        for c4 in range(4):
            k.dma("poolq", Wx[:, c4 * 4:(c4 + 1) * 4, :], w_xbc[c4 * 512:(c4 + 1) * 512, :].rearrange("(kt p) c -> p kt c", p=128), writes=[bc])
            k.dma("poolq", Wz[:, c4 * 4:(c4 + 1) * 4, :], w_z[c4 * 512:(c4 + 1) * 512, :].rearrange("(kt p) c -> p kt c", p=128), writes=[bc])
        k.dma("poolq", Wd[:], w_dt.rearrange("(kt p) c -> p kt c", p=128), writes=[bc])

        pX = k.ps("pX", [128, 1024], F32)
        pZ = k.ps("pZ", [128, 512], F32)
        pSeg = k.ps("pSeg", [128, 1024], F32)
        pY = k.ps("pY", [128, 512], F32)
        pYi = k.ps("pYi", [128, 512], F32)
        pM = k.ps("pM", [128, 512], F32)
        bpX, bpZ, bpSeg, bpY, bpYi, bpM = (k.pbuf(n) for n in ("pX", "pZ", "pSeg", "pY", "pYi", "pM"))
        pYi_b = pYi[:].bitcast(BF16)

        xTb = k.sb("xTb", [128, 16, 512], F32)
        sqb = [k.sb(f"sqb{i}", [128, 512], BF16) for i in range(2)]
        hT = k.sb("hT", [128, 16, 512], BF16)
        rstd = k.sb("rstd", [128, 512], F32)
        cbuf = [k.sb(f"cbuf{i}", [128, 6, 131], F32) for i in range(2)]
        cv = k.sb("cv", [128, 6, 128], F32)
        xbc = k.sb("xbc", [128, 6, 128], BF16)
        Et = k.sb("Et", [128, 8, 128], F32)
        MT = k.sb("MT", [128, 8, 128], BF16)
        lh = k.sb("lh", [128, 8, 128], F32)
        x_tok = k.sb("x_tok", [128, 512], BF16)
        B_tok = k.sb("B_tok", [128, 128], BF16)
        xdt = k.sb("xdt", [128, 512], BF16)
        xw = k.sb("xw", [128, 512], BF16)
        y_sb = k.sb("y_sb", [128, 512], F32)
        zs = k.sb("zs", [128, 512], F32)
        yg = k.sb("yg", [128, 512], F32)
        ysq = k.sb("ysq", [128, 512], F32)
        yo = [k.sb(f"yo{i}", [128, 512], F32) for i in range(2)]
        ST = k.sb("ST", [128, 512], F32)
        STb = k.sb("STb", [128, 512], BF16)
        dtt = k.sb("dtt", [128, 8], F32)
        da = k.sb("da", [128, 8], F32)
        ecum = k.sb("ecum", [128, 8], F32)
        dec = k.sb("dec", [128, 8], F32)
        cbm = k.sb("cbm", [128, 128], F32)
        sm = k.sb("sm", [128, 4], F32)
        cso = k.sb("cso", [128, 6, NSB, 3], F32)
        (b_xTb, b_hT, b_rstd, b_cv, b_xbc, b_Et, b_MT, b_lh, b_xtok, b_Btok, b_xdt, b_xw, b_ysb, b_zs, b_yg, b_ysq,
         b_ST, b_STb, b_dt, b_da, b_ecum, b_dec, b_cbm, b_sm, b_cso, b_out) = (k.buf() for _ in range(26))
        b_sqb = [k.buf() for _ in range(2)]
        b_cbuf = [k.buf() for _ in range(2)]
        b_yo = [k.buf() for _ in range(2)]
        k.dma("sp", cso[:], cs_in[:, :, :, :], writes=[b_cso])
        k.op("dve", lambda: nc.vector.memset(ST[:], 0.0), writes=[b_ST])
        k.op("dve", lambda: nc.vector.memset(STb[:], 0.0), writes=[b_STb])
        k.op("dve", lambda: nc.vector.memset(cbuf[0][:], 0.0), writes=[b_cbuf[0]])
        k.op("dve", lambda: nc.vector.memset(cbuf[1][:], 0.0), writes=[b_cbuf[1]])
        ucount = [0]

        def unit(u0, L, tok0, sample_b):
            ci = ucount[0] % 2
            ucount[0] += 1
            cb_, bcb = cbuf[ci], b_cbuf[ci]
            cbn, bcbn = cbuf[1 - ci], b_cbuf[1 - ci]
            if sample_b is not None:
                k.op("dve", lambda: nc.vector.tensor_copy(out=cb_[:, :, 0:3], in_=cso[:, :, sample_b, :]),
                     reads=[b_cso], writes=[bcb])
                k.dma("sp", ST[:], sT_in[sample_b, :, :], writes=[b_ST])
                k.op("act", lambda: nc.scalar.copy(out=STb[:], in_=ST[:]), reads=[b_ST], writes=[b_STb])
            for ct in range(6):
                for kt in range(16):
                    k.op("pe", lambda: nc.tensor.matmul(pX[:, ct * 128:ct * 128 + L], lhsT=Wx[:, kt, ct * 128:(ct + 1) * 128],
                                                        rhs=hT[:, kt, u0:u0 + L], start=(kt == 0), stop=(kt == 15)),
                         reads=[b_hT, bc], writes=[bpX], sig=(kt == 15))
            k.op("act", lambda: nc.scalar.copy(out=cb_[:, 0:4, 3:3 + L],
                                               in_=pX[:, 0:512].rearrange("p (a b) -> p a b", b=128)[:, :, 0:L]),
                 reads=[bpX], writes=[bcb])
            k.op("act", lambda: nc.scalar.copy(out=cb_[:, 4:6, 3:3 + L],
                                               in_=pX[:, 512:768].rearrange("p (a b) -> p a b", b=128)[:, :, 0:L]),
                 reads=[bpX], writes=[bcb])
            if sample_b is not None:
                k.op("dve", lambda: nc.vector.tensor_copy(out=cso[:, :, sample_b, :], in_=cb_[:, :, L:L + 3]),
                     reads=[bcb], writes=[b_cso])
            else:
                k.op("dve", lambda: nc.vector.tensor_copy(out=cbn[:, :, 0:3], in_=cb_[:, :, L:L + 3]),
                     reads=[bcb], writes=[bcbn])
            for ct in range(6):
                k.op("act", lambda: nc.scalar.activation(out=cv[:, ct, 0:L], in_=cb_[:, ct, 0:L], func=AF.Identity,
                                                         bias=cbias[:, ct:ct + 1], scale=cw[:, ct, 0:1]),
                     reads=[bcb, bc], writes=[b_cv])
                for kk in range(1, 4):
                    k.op("dve", lambda: nc.vector.scalar_tensor_tensor(out=cv[:, ct, 0:L], in0=cb_[:, ct, kk:kk + L],
                                                                       scalar=cw[:, ct, kk:kk + 1], in1=cv[:, ct, 0:L],
                                                                       op0=ALU.mult, op1=ALU.add),
                         reads=[bcb, bc, b_cv], writes=[b_cv])
            k.op("act", lambda: nc.scalar.activation(out=xbc[:, :, 0:L], in_=cv[:, :, 0:L], func=AF.Silu),
                 reads=[b_cv], writes=[b_xbc])
            for kt in range(16):
                k.op("pe", lambda: nc.tensor.matmul(pZ[:L, :], lhsT=hT[:, kt, u0:u0 + L], rhs=Wz[:, kt, :],
                                                    start=(kt == 0), stop=(kt == 15)),
                     reads=[b_hT, bc], writes=[bpZ], sig=(kt == 15))
            for kt in range(16):
                k.op("pe", lambda: nc.tensor.matmul(pM[:L, 0:8], lhsT=hT[:, kt, u0:u0 + L], rhs=Wd[:, kt, :],
                                                    start=(kt == 0), stop=(kt == 15)),
                     reads=[b_hT, bc], writes=[bpM], sig=(kt == 15))
            k.op("dve", lambda: nc.vector.tensor_tensor(out=dtt[:L, :], in0=pM[:L, 0:8], in1=hpb[:L, 0, :], op=ALU.add),
                 reads=[bpM, bc], writes=[b_dt])
            k.op("act", lambda: nc.scalar.activation(out=dtt[:L, :], in_=dtt[:L, :], func=AF.Exp), reads=[b_dt], writes=[b_dt])
            k.op("act", lambda: nc.scalar.activation(out=dtt[:L, :], in_=dtt[:L, :], func=AF.Ln, bias=1.0), reads=[b_dt], writes=[b_dt])
            k.op("dve", lambda: nc.vector.tensor_tensor(out=da[:L, :], in0=dtt[:L, :], in1=a_bc[:L, :], op=ALU.mult),
                 reads=[b_dt, bc], writes=[b_da])
            for ct in range(5):
                k.op("pe", lambda: nc.tensor.transpose(out=pYi_b[:L, ct * 128:(ct + 1) * 128], in_=xbc[:, ct, 0:L], identity=ident[:]),
                     reads=[b_xbc, b_id], writes=[bpYi])
            k.op("act", lambda: nc.scalar.copy(out=x_tok[:L, :], in_=pYi_b[:L, 0:512]), reads=[bpYi], writes=[b_xtok])
            k.op("act", lambda: nc.scalar.copy(out=B_tok[:L, :], in_=pYi_b[:L, 512:640]), reads=[bpYi], writes=[b_Btok])
            for h in range(8):
                k.op("dve", lambda: nc.vector.tensor_scalar(out=xdt[:L, h * 64:(h + 1) * 64], in0=x_tok[:L, h * 64:(h + 1) * 64],
                                                           scalar1=dtt[:L, h:h + 1], scalar2=None, op0=ALU.mult),
                     reads=[b_xtok, b_dt], writes=[b_xdt])
            for h in range(8):
                k.op("dve", lambda: nc.vector.tensor_scalar(out=lh[:L, h, 0:L], in0=mgt[:L, 0:L], scalar1=da[:L, h:h + 1],
                                                           scalar2=None, op0=ALU.mult), reads=[b_da, bc], writes=[b_lh])
            for h in range(8):
                k.op("pe", lambda: nc.tensor.matmul(pSeg[:L, h * 128:h * 128 + L], lhsT=lh[:L, h, 0:L], rhs=tri_le[:L, 0:L],
                                                    start=True, stop=True), reads=[b_lh, bc], writes=[bpSeg])
            for hb_ in range(2):
                k.op("act", lambda: nc.scalar.activation(
                    out=Et[:L, hb_ * 4:hb_ * 4 + 4, 0:L],
                    in_=pSeg[:L, hb_ * 512:(hb_ + 1) * 512].rearrange("p (a b) -> p a b", b=128)[:, :, 0:L], func=AF.Exp),
                    reads=[bpSeg], writes=[b_Et])
            k.op("pe", lambda: nc.tensor.matmul(pM[:L, 8:16], lhsT=tri_le[:L, 0:L], rhs=da[:L, :], start=True, stop=True),
                 reads=[b_da, bc], writes=[bpM])
            k.op("pe", lambda: nc.tensor.matmul(pM[:, 16:24], lhsT=ones_f[:L, :], rhs=da[:L, :], start=True, stop=True),
                 reads=[b_da, bc], writes=[bpM])
            k.op("act", lambda: nc.scalar.activation(out=ecum[:L, :], in_=pM[:L, 8:16], func=AF.Exp), reads=[bpM], writes=[b_ecum])
            k.op("act", lambda: nc.scalar.activation(out=dec[:, :], in_=pM[:, 16:24], func=AF.Exp), reads=[bpM], writes=[b_dec])
            k.op("pe", lambda: nc.tensor.matmul(pM[:L, 128:128 + L], lhsT=xbc[:, 4, 0:L], rhs=xbc[:, 5, 0:L], start=True, stop=True),
                 reads=[b_xbc], writes=[bpM])
            k.op("dve", lambda: nc.vector.tensor_tensor(out=cbm[:L, 0:L], in0=pM[:L, 128:128 + L], in1=tri_le[:L, 0:L], op=ALU.mult),
                 reads=[bpM, bc], writes=[b_cbm])
            for h in range(8):
                k.op("dve", lambda: nc.vector.tensor_tensor(out=MT[:L, h, 0:L], in0=Et[:L, h, 0:L], in1=cbm[:L, 0:L], op=ALU.mult),
                     reads=[b_Et, b_cbm], writes=[b_MT])
            for h in range(8):
                k.op("pe", lambda: nc.tensor.matmul(pY[:L, h * 64:(h + 1) * 64], lhsT=MT[:L, h, 0:L], rhs=xdt[:L, h * 64:(h + 1) * 64],
                                                    start=True, stop=True), reads=[b_MT, b_xdt], writes=[bpY])
            k.op("pe", lambda: nc.tensor.matmul(pYi[:L, :], lhsT=xbc[:, 5, 0:L], rhs=STb[:], start=True, stop=True),
                 reads=[b_xbc, b_STb], writes=[bpYi])
            k.op("act", lambda: nc.scalar.copy(out=y_sb[:L, :], in_=pY[:L, :]), reads=[bpY], writes=[b_ysb])
            for h in range(8):
                k.op("dve", lambda: nc.vector.scalar_tensor_tensor(out=y_sb[:L, h * 64:(h + 1) * 64], in0=pYi[:L, h * 64:(h + 1) * 64],
                                                                   scalar=ecum[:L, h:h + 1], in1=y_sb[:L, h * 64:(h + 1) * 64],
                                                                   op0=ALU.mult, op1=ALU.add),
                     reads=[bpYi, b_ecum, b_ysb], writes=[b_ysb])
            k.op("dve", lambda: nc.vector.tensor_tensor(out=yg[:L, :], in0=x_tok[:L, :], in1=d_bc[:L, :], op=ALU.mult),
                 reads=[b_xtok, bc], writes=[b_yg])
            k.op("dve", lambda: nc.vector.tensor_tensor(out=y_sb[:L, :], in0=y_sb[:L, :], in1=yg[:L, :], op=ALU.add),
                 reads=[b_ysb, b_yg], writes=[b_ysb])
            for h in range(8):
                k.op("dve", lambda: nc.vector.tensor_scalar(out=xw[:L, h * 64:(h + 1) * 64], in0=xdt[:L, h * 64:(h + 1) * 64],
                                                           scalar1=Et[:L, h, L - 1:L], scalar2=None, op0=ALU.mult),
                     reads=[b_xdt, b_Et], writes=[b_xw])
            k.op("pe", lambda: nc.tensor.matmul(pX[:, 0:512], lhsT=B_tok[:L, :], rhs=xw[:L, :], start=True, stop=True),
                 reads=[b_Btok, b_xw], writes=[bpX])
            for h in range(8):
                k.op("dve", lambda: nc.vector.scalar_tensor_tensor(out=ST[:, h * 64:(h + 1) * 64], in0=ST[:, h * 64:(h + 1) * 64],
                                                                   scalar=dec[:, h:h + 1], in1=pX[:, h * 64:(h + 1) * 64],
                                                                   op0=ALU.mult, op1=ALU.add),
                     reads=[b_ST, b_dec, bpX], writes=[b_ST])
            if sample_b is not None:
                k.dma("sp", sT_s[sample_b, :, :], ST[:], reads=[b_ST], writes=[b_out])
            else:
                k.op("act", lambda: nc.scalar.copy(out=STb[:], in_=ST[:]), reads=[b_ST], writes=[b_STb])
            k.op("act", lambda: nc.scalar.activation(out=zs[:L, :], in_=pZ[:L, :], func=AF.Silu), reads=[bpZ], writes=[b_zs])
            k.op("dve", lambda: nc.vector.tensor_tensor(out=yg[:L, :], in0=y_sb[:L, :], in1=zs[:L, :], op=ALU.mult),
                 reads=[b_ysb, b_zs], writes=[b_yg])
            k.op("act", lambda: nc.scalar.activation(out=ysq[:L, :], in_=yg[:L, :], func=AF.Square), reads=[b_yg], writes=[b_ysq])
            k.op("dve", lambda: nc.vector.reduce_sum(out=sm[:L, 0:1], in_=ysq[:L, :], axis=AX.X), reads=[b_ysq], writes=[b_sm])
            rstd_from_sumsq(k, nc, sm[:L, 1:2], sm[:L, 0:1], 512, [b_sm], [b_sm])
            oi = ucount[0] % 2
            k.op("dve", lambda: nc.vector.scalar_tensor_tensor(out=yo[oi][:L, :], in0=yg[:L, :], scalar=sm[:L, 1:2], in1=nw_bc[:L, :],
                                                               op0=ALU.mult, op1=ALU.mult),
                 reads=[b_yg, b_sm, bc], writes=[b_yo[oi]])
            k.dma("sp", yn[tok0:tok0 + L, :], yo[oi][:L, :], reads=[b_yo[oi]], writes=[b_out])
            return cbn, bcbn

        nblk = NTOK // 512
        last = None
        for bi in range(nblk):
            c0 = bi * 512
            for c4 in range(4):
                k.dma("sp", xTb[:, c4 * 4:(c4 + 1) * 4, :],
                      xT[c4 * 512:(c4 + 1) * 512, c0:c0 + 512].rearrange("(kt p) t -> p kt t", p=128), writes=[b_xTb])
            for kt in range(16):
                j = kt % 2
                k.op("act", lambda: nc.scalar.activation(out=sqb[j][:], in_=xTb[:, kt, :], func=AF.Square),
                     reads=[b_xTb], writes=[b_sqb[j]])
                k.op("pe", lambda: nc.tensor.matmul(pZ[:, :], lhsT=ones_b[:], rhs=sqb[j][:], start=(kt == 0), stop=(kt == 15)),
                     reads=[b_sqb[j], bc], writes=[bpZ])
            k.op("dve", lambda: nc.vector.tensor_scalar(out=rstd[:], in0=pZ[:, :], scalar1=1.0 / D, scalar2=EPS,
                                                       op0=ALU.mult, op1=ALU.add), reads=[bpZ], writes=[b_rstd])
            k.op("act", lambda: nc.scalar.activation(out=rstd[:], in_=rstd[:], func=AF.Sqrt), reads=[b_rstd], writes=[b_rstd])
            k.op("dve", lambda: nc.vector.reciprocal(out=rstd[:], in_=rstd[:]), reads=[b_rstd], writes=[b_rstd])
            for kt in range(16):
                k.op("dve", lambda: nc.vector.scalar_tensor_tensor(out=hT[:, kt, :], in0=xTb[:, kt, :], scalar=nwT[:, kt:kt + 1],
                                                                   in1=rstd[:], op0=ALU.mult, op1=ALU.mult),
                     reads=[b_xTb, b_rstd, bc], writes=[b_hT])
            if bi < SEQ // 512:
                for u in range(4):
                    last = unit(u * 128, 128, c0 + u * 128, None)
                if bi == SEQ // 512 - 1:
                    k.dma("sp", sT_p[:, :], ST[:], reads=[b_ST], writes=[b_out])
                    k.dma("sp", cs_p[:, :, :], last[0][:, :, 0:3], reads=[last[1]], writes=[b_out])
            else:
                for b in range(NSB):
                    unit(b * 16, 16, c0 + b * 16, b)
        k.dma("sp", cs_s[:, :, :, :], cso[:], reads=[b_cso], writes=[b_out])
        k.finish()
    return nc


PAST = 1024


def build_attn(dbg_heads=2, dbg_blocks=None, dbg_attend=True):
    nc = bass.Bass("TRN2", target_bir_lowering=False)
    xT = nc.dram_tensor("xT", [D, NTOK], F32, kind="ExternalInput").ap()
    normwT = nc.dram_tensor("normwT", [128, 16], F32, kind="ExternalInput").ap()
    w_att = nc.dram_tensor("w_att", [2, D, 512], F32, kind="ExternalInput").ap()
    ckT = nc.dram_tensor("ckT", [NSB, 2, 128, PAST], F32, kind="ExternalInput").ap()
    cv_ = nc.dram_tensor("cv", [NSB, 2, PAST, 128], F32, kind="ExternalInput").ap()
    ogT = nc.dram_tensor("ogT", [256, NTOK], F32, kind="ExternalOutput").ap()
    kTo = nc.dram_tensor("kTo", [2, 128, NTOK], F32, kind="ExternalOutput").ap()
    vo = nc.dram_tensor("vo", [2, NTOK, 128], F32, kind="ExternalOutput").ap()
    SCALE = 128 ** -0.5

    with ExitStack() as st:
        k = K(nc, st)
        identf, ident, b_id = make_ident(k, nc)
        bc = k.buf("const")
        nwT = k.sb("nwT", [128, 16], F32)
        k.dma("sp", nwT[:], normwT[:, :], writes=[bc])
        ones_b = k.sb("ones_b", [128, 128], BF16)
        k.op("dve", lambda: nc.vector.memset(ones_b[:], 1.0), writes=[bc])
        ones_f = k.sb("ones_f", [128, 128], F32)
        k.op("dve", lambda: nc.vector.memset(ones_f[:], 1.0), writes=[bc])
        ones_w = k.sb("ones_w", [128, 512], F32)
        k.op("dve", lambda: nc.vector.memset(ones_w[:], 1.0), writes=[bc])
        mgt = k.sb("mgt", [128, 128], F32)
        k.op("pool", lambda: nc.gpsimd.affine_select(out=mgt[:], in_=ones_f[:], pattern=[[-1, 128]],
                                                     compare_op=ALU.is_gt, fill=0.0, base=0, channel_multiplier=1),
             reads=[bc], writes=[bc])
        m01 = k.sb("m01", [128, 4, 512], F32)
        negm = k.sb("negm", [128, 4, 512], F32)
        for jl in range(4):
            k.op("pool", lambda: nc.gpsimd.affine_select(out=m01[:, jl, :], in_=ones_w[:], pattern=[[1, 512]],
                                                         compare_op=ALU.is_gt, fill=0.0, base=-jl * 128, channel_multiplier=-1),
                 reads=[bc], writes=[bc])
        k.op("dve", lambda: nc.vector.tensor_scalar(out=negm[:].rearrange("p a b -> p (a b)"), in0=m01[:].rearrange("p a b -> p (a b)"),
                                                   scalar1=-1.0, scalar2=1e30, op0=ALU.add, op1=ALU.mult), reads=[bc], writes=[bc])

        pQ = k.ps("pQ", [128, 512], F32)
        pV = k.ps("pV", [128, 512], F32)
        pS = [k.ps(f"pS{i}", [128, 512], F32) for i in range(2)]
        pL = k.ps("pL", [128, 512], F32)
        pC = k.ps("pC", [128, 512], F32)
        pO = k.ps("pO", [128, 512], F32)
        bpQ, bpV, bpL, bpC, bpO = (k.pbuf(n) for n in ("pQ", "pV", "pL", "pC", "pO"))
        bpS = [k.pbuf() for _ in range(2)]

        xTb = k.sb("xTb", [128, 16, 512], F32)
        sqb = [k.sb(f"sqb{i}", [128, 512], BF16) for i in range(2)]
        hT = k.sb("hT", [128, 16, 512], BF16)
        rstd = k.sb("rstd", [128, 512], F32)
        W = k.sb("W", [128, 16, 512], BF16)
        kT_all = k.sb("kT_all", [128, SEQ], BF16)
        v_all = k.sb("v_all", [128, SEQ // 128, 128], BF16)
        qT = k.sb("qT", [128, 512], BF16)
        kT_s = k.sb("kT_s", [128, 512], BF16)
        v_s = k.sb("v_s", [16, 128], BF16)
        szT = k.sb("szT", [128, 512], F32)
        kf = k.sb("kf", [128, 512], F32)
        vf = [k.sb(f"vf{i}", [128, 128], F32) for i in range(2)]
        kc = k.sb("kc", [128, PAST], BF16)
        vc = k.sb("vc", [128, PAST // 128, 128], BF16)
        ef = [k.sb(f"ef{i}", [128, 512], F32) for i in range(2)]
        spf = [k.sb(f"spf{i}", [128, 512], F32) for i in range(2)]
        t1 = [k.sb(f"t1{i}", [128, 512], F32) for i in range(2)]
        t2 = [k.sb(f"t2{i}", [128, 512], F32) for i in range(2)]
        wT = [k.sb(f"wT{i}", [128, 512], BF16) for i in range(2)]
        carry = [k.sb(f"carry{i}", [128, 512], F32) for i in range(2)]
        ogf = [k.sb(f"ogf{i}", [128, 512], F32) for i in range(2)]
        (b_xTb, b_hT, b_rstd, b_W, b_kT, b_v, b_qT, b_kTs, b_vs, b_szT, b_kf, b_kc, b_vc, b_out) = (k.buf() for _ in range(14))
        b_sqb = [k.buf() for _ in range(2)]
        b_vf = [k.buf() for _ in range(2)]
        b_ef = [k.buf() for _ in range(2)]
        b_spf = [k.buf() for _ in range(2)]
        b_t1 = [k.buf() for _ in range(2)]
        b_t2 = [k.buf() for _ in range(2)]
        b_wT = [k.buf() for _ in range(2)]
        b_carry = [k.buf() for _ in range(2)]
        b_ogf = [k.buf() for _ in range(2)]
        tc = [0]
        cc = [0]
        oc = [0]

        def attend(q_ap, N, tiles):
            nt_ = len(tiles)
            ci = cc[0] % 2
            k.op("pool", lambda: nc.gpsimd.memset(carry[ci][:, 0:N], 0.0), writes=[b_carry[ci]])
            for idx, (k_ap, v_ap, S, m_ap, n_ap, rds) in enumerate(tiles):
                i = tc[0] % 2
                tc[0] += 1
                ci = cc[0] % 2
                cn = 1 - ci
                cc[0] += 1
                k.op("pe", lambda: nc.tensor.matmul(pS[i][:S, 0:N], lhsT=k_ap, rhs=q_ap, start=True, stop=True),
                     reads=rds + [b_qT], writes=[bpS[i]])
                k.op("act", lambda: nc.scalar.activation(out=ef[i][:S, 0:N], in_=pS[i][:S, 0:N], func=AF.Exp),
                     reads=[bpS[i]], writes=[b_ef[i]])
                k.op("act", lambda: nc.scalar.activation(out=spf[i][:S, 0:N], in_=ef[i][:S, 0:N], func=AF.Ln, bias=1.0),
                     reads=[b_ef[i]], writes=[b_spf[i]])
                if m_ap is not None:
                    k.op("dve", lambda: nc.vector.tensor_tensor(out=spf[i][:S, 0:N], in0=spf[i][:S, 0:N], in1=m_ap, op=ALU.mult),
                         reads=[b_spf[i], bc], writes=[b_spf[i]])
                k.op("pe", lambda: nc.tensor.matmul(pL[:S, 0:N], lhsT=mgt[:S, 0:S], rhs=spf[i][:S, 0:N], start=True, stop=True),
                     reads=[b_spf[i], bc], writes=[bpL])
                k.op("pe", lambda: nc.tensor.matmul(pC[:, 0:N], lhsT=ones_f[:S, :], rhs=spf[i][:S, 0:N], start=True, stop=True),
                     reads=[b_spf[i], bc], writes=[bpC])
                k.op("dve", lambda: nc.vector.tensor_tensor(out=t1[i][:S, 0:N], in0=pS[i][:S, 0:N], in1=spf[i][:S, 0:N], op=ALU.subtract),
                     reads=[bpS[i], b_spf[i]], writes=[b_t1[i]])
                k.op("dve", lambda: nc.vector.tensor_tensor(out=t2[i][:S, 0:N], in0=t1[i][:S, 0:N], in1=pL[:S, 0:N], op=ALU.subtract),
                     reads=[b_t1[i], bpL], writes=[b_t2[i]])
                k.op("pool", lambda: nc.gpsimd.tensor_tensor(out=t2[i][:S, 0:N], in0=t2[i][:S, 0:N], in1=carry[ci][:S, 0:N], op=ALU.subtract),
                     reads=[b_t2[i], b_carry[ci]], writes=[b_t2[i]])
                if n_ap is not None:
                    k.op("pool", lambda: nc.gpsimd.tensor_tensor(out=t2[i][:S, 0:N], in0=t2[i][:S, 0:N], in1=n_ap, op=ALU.add),
                         reads=[b_t2[i], bc], writes=[b_t2[i]])
                if idx < nt_ - 1:
                    k.op("dve", lambda: nc.vector.tensor_tensor(out=carry[cn][:, 0:N], in0=carry[ci][:, 0:N], in1=pC[:, 0:N], op=ALU.add),
                         reads=[b_carry[ci], bpC], writes=[b_carry[cn]])
                k.op("act", lambda: nc.scalar.activation(out=wT[i][:S, 0:N], in_=t2[i][:S, 0:N], func=AF.Exp),
                     reads=[b_t2[i]], writes=[b_wT[i]])
                k.op("pe", lambda: nc.tensor.matmul(pO[:, 0:N], lhsT=v_ap, rhs=wT[i][:S, 0:N], start=(idx == 0), stop=(idx == nt_ - 1)),
                     reads=rds + [b_wT[i]], writes=[bpO])

        def emit_og(hh, N, col0, sz_ap):
            j = oc[0] % 2
            oc[0] += 1
            k.op("dve", lambda: nc.vector.tensor_tensor(out=ogf[j][:, 0:N], in0=pO[:, 0:N], in1=sz_ap, op=ALU.mult),
                 reads=[bpO, b_szT], writes=[b_ogf[j]])
            k.dma("sp", ogT[hh * 128:(hh + 1) * 128, col0:col0 + N], ogf[j][:, 0:N], reads=[b_ogf[j]], writes=[b_out])

        nblk = NTOK // 512
        for hh in range(dbg_heads):
            for c4 in range(4):
                k.dma("poolq", W[:, c4 * 4:(c4 + 1) * 4, :], w_att[hh, c4 * 512:(c4 + 1) * 512, :].rearrange("(kt p) c -> p kt c", p=128),
                      writes=[b_W])
            for bi in (range(nblk) if dbg_blocks is None else dbg_blocks):
                c0 = bi * 512
                samp = bi >= SEQ // 512
                for c4 in range(4):
                    k.dma("sp", xTb[:, c4 * 4:(c4 + 1) * 4, :],
                          xT[c4 * 512:(c4 + 1) * 512, c0:c0 + 512].rearrange("(kt p) t -> p kt t", p=128), writes=[b_xTb])
                for kt in range(16):
                    j = kt % 2
                    k.op("act", lambda: nc.scalar.activation(out=sqb[j][:], in_=xTb[:, kt, :], func=AF.Square),
                         reads=[b_xTb], writes=[b_sqb[j]])
                    k.op("pe", lambda: nc.tensor.matmul(pQ[:, :], lhsT=ones_b[:], rhs=sqb[j][:], start=(kt == 0), stop=(kt == 15)),
                         reads=[b_sqb[j], bc], writes=[bpQ])
                k.op("dve", lambda: nc.vector.tensor_scalar(out=rstd[:], in0=pQ[:, :], scalar1=1.0 / D, scalar2=EPS,
                                                           op0=ALU.mult, op1=ALU.add), reads=[bpQ], writes=[b_rstd])
                k.op("act", lambda: nc.scalar.activation(out=rstd[:], in_=rstd[:], func=AF.Sqrt), reads=[b_rstd], writes=[b_rstd])
                k.op("dve", lambda: nc.vector.reciprocal(out=rstd[:], in_=rstd[:]), reads=[b_rstd], writes=[b_rstd])
                for kt in range(16):
                    k.op("dve", lambda: nc.vector.scalar_tensor_tensor(out=hT[:, kt, :], in0=xTb[:, kt, :], scalar=nwT[:, kt:kt + 1],
                                                                       in1=rstd[:], op0=ALU.mult, op1=ALU.mult),
                         reads=[b_xTb, b_rstd, bc], writes=[b_hT])
                import os
                LVL = int(os.environ.get("ATT_LVL", "9"))
                if LVL < 2:
                    continue
                for kt in range(16):
                    k.op("pe", lambda: nc.tensor.matmul(pQ[:, :], lhsT=W[:, kt, 0:128], rhs=hT[:, kt, :], start=(kt == 0), stop=(kt == 15)),
                         reads=[b_W, b_hT], writes=[bpQ], sig=(kt == 15))
                k.op("act", lambda: nc.scalar.activation(out=qT[:], in_=pQ[:, :], func=AF.Identity, scale=SCALE), reads=[bpQ], writes=[b_qT])
                if LVL < 3:
                    continue
                for kt in range(16):
                    k.op("pe", lambda: nc.tensor.matmul(pV[:, :], lhsT=W[:, kt, 128:256], rhs=hT[:, kt, :], start=(kt == 0), stop=(kt == 15)),
                         reads=[b_W, b_hT], writes=[bpV], sig=(kt == 15))
                if samp:
                    k.op("act", lambda: nc.scalar.copy(out=kT_s[:], in_=pV[:, :]), reads=[bpV], writes=[b_kTs])
                else:
                    k.op("act", lambda: nc.scalar.copy(out=kT_all[:, c0:c0 + 512], in_=pV[:, :]), reads=[bpV], writes=[b_kT])
                k.op("dve", lambda: nc.vector.tensor_copy(out=kf[:], in_=pV[:, :]), reads=[bpV], writes=[b_kf])
                if os.environ.get("ATT_NOKDMA") is None:
                    k.dma("sp", kTo[hh, :, c0:c0 + 512], kf[:], reads=[b_kf], writes=[b_out])
                if LVL < 4:
                    continue
                for kt in range(16):
                    k.op("pe", lambda: nc.tensor.matmul(pQ[:, :], lhsT=W[:, kt, 384:512], rhs=hT[:, kt, :], start=(kt == 0), stop=(kt == 15)),
                         reads=[b_W, b_hT], writes=[bpQ], sig=(kt == 15))
                k.op("act", lambda: nc.scalar.activation(out=szT[:], in_=pQ[:, :], func=AF.Silu), reads=[bpQ], writes=[b_szT])
                if LVL < 5:
                    continue
                if not samp:
                    for ti in range(4):
                        j = ti % 2
                        for kt in range(16):
                            k.op("pe", lambda: nc.tensor.matmul(pV[:, 0:128], lhsT=hT[:, kt, ti * 128:(ti + 1) * 128], rhs=W[:, kt, 256:384],
                                                                start=(kt == 0), stop=(kt == 15)),
                                 reads=[b_W, b_hT], writes=[bpV], sig=(kt == 15))
                        k.op("act", lambda: nc.scalar.copy(out=v_all[:, bi * 4 + ti, :], in_=pV[:, 0:128]), reads=[bpV], writes=[b_v])
                        k.op("dve", lambda: nc.vector.tensor_copy(out=vf[j][:], in_=pV[:, 0:128]), reads=[bpV], writes=[b_vf[j]])
                        k.dma("sp", vo[hh, c0 + ti * 128:c0 + (ti + 1) * 128, :], vf[j][:], reads=[b_vf[j]], writes=[b_out])
                    tiles = []
                    for jl in range(3, -1, -1):
                        kb = bi * 4 + jl
                        tiles.append((kT_all[:, kb * 128:(kb + 1) * 128], v_all[:, kb, :], 128, m01[:, jl, :], negm[:, jl, :], [b_kT, b_v]))
                    for kb in range(bi * 4 - 1, -1, -1):
                        tiles.append((kT_all[:, kb * 128:(kb + 1) * 128], v_all[:, kb, :], 128, None, None, [b_kT, b_v]))
                    if dbg_attend:
                        attend(qT[:], 512, tiles)
                        emit_og(hh, 512, c0, szT[:])
                else:
                    for b in range(NSB):
                        j = b % 2
                        for kt in range(16):
                            k.op("pe", lambda: nc.tensor.matmul(pV[:16, 0:128], lhsT=hT[:, kt, b * 16:(b + 1) * 16], rhs=W[:, kt, 256:384],
                                                                start=(kt == 0), stop=(kt == 15)),
                                 reads=[b_W, b_hT], writes=[bpV], sig=(kt == 15))
                        k.op("act", lambda: nc.scalar.copy(out=v_s[:, :], in_=pV[:16, 0:128]), reads=[bpV], writes=[b_vs])
                        k.op("dve", lambda: nc.vector.tensor_copy(out=vf[j][:16, :], in_=pV[:16, 0:128]), reads=[bpV], writes=[b_vf[j]])
                        k.dma("sp", vo[hh, c0 + b * 16:c0 + (b + 1) * 16, :], vf[j][:16, :], reads=[b_vf[j]], writes=[b_out])
                        k.dma("poolq", kc[:], ckT[b, hh, :, :], writes=[b_kc])
                        k.dma("poolq", vc[:], cv_[b, hh, :, :].rearrange("(a p) d -> p a d", p=128), writes=[b_vc])
                        tiles = [(kT_s[:, b * 16:(b + 1) * 16], v_s[:, :], 16, m01[:16, 0, 0:16], negm[:16, 0, 0:16], [b_kTs, b_vs])]
                        for kb in range(PAST // 128 - 1, -1, -1):
                            tiles.append((kc[:, kb * 128:(kb + 1) * 128], vc[:, kb, :], 128, None, None, [b_kc, b_vc]))
                        if dbg_attend:
                            attend(qT[:, b * 16:(b + 1) * 16], 16, tiles)
                            emit_og(hh, 16, c0 + b * 16, szT[:, b * 16:(b + 1) * 16])
        k.finish()
    return nc


def build_oproj(KD):
    nc = bass.Bass("TRN2", target_bir_lowering=False)
    nk = KD // 128
    aT = nc.dram_tensor("aT", [KD, TT], F32, kind="ExternalInput").ap()
    w = nc.dram_tensor("w", [KD, D], F32, kind="ExternalInput").ap()
    x = nc.dram_tensor("x", [TT, D], F32, kind="ExternalInput").ap()
    xo = nc.dram_tensor("xo", [TT, D], F32, kind="ExternalOutput").ap()
    with ExitStack() as st:
        k = K(nc, st)
        pp = [k.ps(f"pp{i}", [128, 512], F32) for i in range(2)]
        bpp = [k.pbuf() for _ in range(2)]
        yT = k.sb("yT", [128, nk, 512], BF16)
        wo = [k.sb(f"wo{i}", [128, nk, 256], BF16) for i in range(2)]
        xc = [k.sb(f"xc{i}", [128, 256], F32) for i in range(2)]
        xn = [k.sb(f"xn{i}", [128, 256], F32) for i in range(2)]
        b_yT = k.buf()
        b_wo = [k.buf() for _ in range(2)]
        b_xc = [k.buf() for _ in range(2)]
        b_xn = [k.buf() for _ in range(2)]
        b_out = k.buf()
        cnt = [0, 0]
        blocks = [(i * 512, 512) for i in range(TP // 512)] + [(TP, TS)]
        for (t0, nt) in blocks:
            ntile = max(1, nt // 128)
            P = min(nt, 128)
            for c8 in range(nk // 8):
                k.dma("poolq", yT[:, c8 * 8:(c8 + 1) * 8, 0:nt],
                      aT[c8 * 1024:(c8 + 1) * 1024, t0:t0 + nt].rearrange("(kt p) t -> p kt t", p=128), writes=[b_yT])
            for oc in range(8):
                i = cnt[0] % 2
                cnt[0] += 1
                for c8 in range(nk // 8):
                    k.dma("poolq", wo[i][:, c8 * 8:(c8 + 1) * 8, :],
                          w[c8 * 1024:(c8 + 1) * 1024, oc * 256:(oc + 1) * 256].rearrange("(kt p) c -> p kt c", p=128),
                          writes=[b_wo[i]])
                for ti in range(ntile):
                    r0 = t0 + ti * 128
                    j = cnt[1] % 2
                    cnt[1] += 1
                    k.dma("sp", xc[j][:P, :], x[r0:r0 + P, oc * 256:(oc + 1) * 256], writes=[b_xc[j]])
                    for ft in range(nk):
                        k.op("pe", lambda: nc.tensor.matmul(pp[j][:P, 0:256], lhsT=yT[:, ft, ti * 128:ti * 128 + P],
                                                            rhs=wo[i][:, ft, :], start=(ft == 0), stop=(ft == nk - 1)),
                             reads=[b_yT, b_wo[i]], writes=[bpp[j]], sig=(ft == nk - 1))
                    k.op("dve", lambda: nc.vector.tensor_tensor(out=xn[j][:P, :], in0=pp[j][:P, 0:256], in1=xc[j][:P, :],
                                                               op=ALU.add), reads=[bpp[j], b_xc[j]], writes=[b_xn[j]])
                    k.dma("sp", xo[r0:r0 + P, oc * 256:(oc + 1) * 256], xn[j][:P, :], reads=[b_xn[j]], writes=[b_out])
        k.finish()
    return nc


def _unshard_tok(res, key):
    xp = np.concatenate([r[key][:TP] for r in res], axis=0)
    xs = np.concatenate([r[key][TP:] for r in res], axis=0)
    return xp, xs


def _featmajor_all(xp, xs):
    return np.ascontiguousarray(np.concatenate([xp, xs.reshape(NSB * NST, -1)], axis=0).T)


def kernel(x_prompt, x_sample, state_ssm, state_conv, cache_k, cache_v, norm_w, final_norm_w,
           a_w_in, a_ln_g, a_ln_b, a_w_s, a_b_s, a_w_out,
           b_w_in, b_conv_w, b_conv_b, b_dt_bias, b_a_log, b_d, b_norm_w, b_w_out,
           c_w_in, c_w_out):
    f = lambda a: np.ascontiguousarray(np.asarray(a, dtype=np.float32))
    xp = f(x_prompt)[0]
    xs = f(x_sample)
    norm_w = f(norm_w)
    res = run_gmlp(xp, xs, norm_w[0], f(a_w_in)[0], f(a_ln_g)[0], f(a_ln_b)[0], f(a_w_s)[0], f(a_b_s)[0], f(a_w_out)[0])
    x1p, x1s = _unshard_tok(res, "xo")
    v0 = np.concatenate([r["vsT"].T for r in res], axis=0).reshape(NSB, NST, AW)
    xT1 = _featmajor_all(x1p, x1s)
    bw = f(b_w_in)[0]
    cw_ = f(b_conv_w)[0]
    cb_ = f(b_conv_b)[0]
    sst = f(state_ssm)[0]
    scv = f(state_conv)[0]
    nwT1 = np.ascontiguousarray(norm_w[1].reshape(16, 128).T)
    in_maps = []
    for g in range(NCORES):
        xcols = np.arange(4096 + g * 512, 4096 + (g + 1) * 512)
        bcols = np.arange(4096 + 4096 + g * 128, 4096 + 4096 + (g + 1) * 128)
        ccols = np.arange(4096 + 4096 + 1024 + g * 128, 4096 + 4096 + 1024 + (g + 1) * 128)
        cols = np.concatenate([xcols, bcols, ccols])
        cch = cols - 4096
        hs = slice(g * 8, (g + 1) * 8)
        m = {
            "xT": xT1, "normwT": nwT1,
            "w_z": np.ascontiguousarray(bw[:, g * 512:(g + 1) * 512]),
            "w_xbc": np.ascontiguousarray(bw[:, cols]),
            "w_dt": np.ascontiguousarray(bw[:, 4096 + 6144 + g * 8:4096 + 6144 + (g + 1) * 8]),
            "convw": np.ascontiguousarray(cw_[:, cch].T.reshape(6, 128, 4).transpose(1, 0, 2)),
            "convb": np.ascontiguousarray(cb_[cch].reshape(6, 128).T),
            "hp": np.ascontiguousarray(np.stack([f(b_dt_bias)[0][hs], f(b_a_log)[0][hs], f(b_d)[0][hs]])),
            "dexp": np.ascontiguousarray(np.repeat(f(b_d)[0][hs], 64)),
            "nw": np.ascontiguousarray(f(b_norm_w)[0][g * 512:(g + 1) * 512]),
            "sT_in": np.ascontiguousarray(sst[:, hs].reshape(NSB, 512, 128).transpose(0, 2, 1)),
            "cs_in": np.ascontiguousarray(scv[:, :, cch].transpose(2, 0, 1).reshape(6, 128, NSB, 3).transpose(1, 0, 2, 3)),
        }
        in_maps.append(m)
    resm = _run(build_mamba(), in_maps)
    ssm_p = np.zeros((1, 1, 64, 64, 128), np.float32)
    conv_p = np.zeros((1, 1, 3, 6144), np.float32)
    ssm_s = np.zeros((1, NSB, 64, 64, 128), np.float32)
    conv_s = np.zeros((1, NSB, 3, 6144), np.float32)
    yn_all = np.zeros((NTOK, 4096), np.float32)
    for g in range(NCORES):
        r = resm[g]
        xcols = np.arange(g * 512, (g + 1) * 512)
        bcols = np.arange(4096 + g * 128, 4096 + (g + 1) * 128)
        ccols = np.arange(4096 + 1024 + g * 128, 4096 + 1024 + (g + 1) * 128)
        cch = np.concatenate([xcols, bcols, ccols])
        ssm_p[0, 0, g * 8:(g + 1) * 8] = r["sT_p"].T.reshape(8, 64, 128)
        conv_p[0, 0][:, cch] = r["cs_p"].transpose(1, 0, 2).reshape(768, 3).T
        ssm_s[0, :, g * 8:(g + 1) * 8] = r["sT_s"].transpose(0, 2, 1).reshape(NSB, 8, 64, 128)
        conv_s[0][:, :, cch] = r["cs_s"].transpose(1, 0, 2, 3).reshape(768, NSB, 3).transpose(1, 2, 0)
        yn_all[:, g * 512:(g + 1) * 512] = r["yn"]
    ynp, yns = yn_all[:SEQ], yn_all[SEQ:].reshape(NSB, NST, 4096)
    in_maps = [{"aT": np.ascontiguousarray(tok_shard(ynp, yns, c).T), "w": f(b_w_out)[0], "x": tok_shard(x1p, x1s.reshape(NSB, NST, D), c)}
               for c in range(NCORES)]
    res = _run(build_oproj(4096), in_maps)
    x2p, x2s = _unshard_tok(res, "xo")
    xT2 = _featmajor_all(x2p, x2s)
    nwT2 = np.ascontiguousarray(norm_w[2].reshape(16, 128).T)
    cwi = f(c_w_in)[0]
    ck = f(cache_k)[0]
    cvv = f(cache_v)[0]
    in_maps = []
    for c in range(NCORES):
        wh = []
        for hh in range(2):
            h = 2 * c + hh
            wh.append(np.concatenate([cwi[:, part * 2048 + h * 128: part * 2048 + (h + 1) * 128] for part in range(4)], axis=1))
        in_maps.append({"xT": xT2, "normwT": nwT2, "w_att": np.ascontiguousarray(np.stack(wh)),
                        "ckT": np.ascontiguousarray(ck[:, 2 * c:2 * c + 2].transpose(0, 1, 3, 2)),
                        "cv": np.ascontiguousarray(cvv[:, 2 * c:2 * c + 2])})
    resa = _run(build_attn(), in_maps)
    k_all = np.zeros((16, NTOK, 128), np.float32)
    v_all = np.zeros((16, NTOK, 128), np.float32)
    og_all = np.zeros((NTOK, 2048), np.float32)
    for c in range(NCORES):
        r = resa[c]
        k_all[2 * c:2 * c + 2] = r["kTo"].transpose(0, 2, 1)
        v_all[2 * c:2 * c + 2] = r["vo"]
        og_all[:, c * 256:(c + 1) * 256] = r["ogT"].T
    k_p = k_all[:, :SEQ][None, None]
    v_p = v_all[:, :SEQ][None, None]
    k_s = np.ascontiguousarray(k_all[:, SEQ:].reshape(16, NSB, NST, 128).transpose(1, 0, 2, 3))[None]
    v_s = np.ascontiguousarray(v_all[:, SEQ:].reshape(16, NSB, NST, 128).transpose(1, 0, 2, 3))[None]
    ogp, ogs = og_all[:SEQ], og_all[SEQ:].reshape(NSB, NST, 2048)
    in_maps = [{"aT": np.ascontiguousarray(tok_shard(ogp, ogs, c).T), "w": f(c_w_out)[0], "x": tok_shard(x2p, x2s.reshape(NSB, NST, D), c)}
               for c in range(NCORES)]
    res = _run(build_oproj(2048), in_maps)
    x3p, x3s = _unshard_tok(res, "xo")
    res = run_gmlp(x3p, x3s.reshape(NSB, NST, D), norm_w[3], f(a_w_in)[1], f(a_ln_g)[1], f(a_ln_b)[1], f(a_w_s)[1], f(a_b_s)[1],
                   f(a_w_out)[1], f(final_norm_w))
    yp, ys = _unshard_tok(res, "yo")
    v1 = np.concatenate([r["vsT"].T for r in res], axis=0).reshape(NSB, NST, AW)
    return (np.ascontiguousarray(yp)[None], np.ascontiguousarray(ys).reshape(NSB, NST, D),
            np.ascontiguousarray(np.stack([v0, v1])), ssm_p, conv_p, ssm_s, conv_s,
            np.ascontiguousarray(k_p), np.ascontiguousarray(v_p), k_s, v_s)
```

```python
import numpy as np
from contextlib import ExitStack
import concourse.bass as bass
import concourse.mybir as mybir
from concourse.bass_utils import run_bass_kernel_spmd

F32 = mybir.dt.float32
BF16 = mybir.dt.bfloat16
AF = mybir.ActivationFunctionType
ALU = mybir.AluOpType
AX = mybir.AxisListType

NCORES = 8
D = 2048
SEQ = 16384
NSB = 32
NST = 16
TP = SEQ // NCORES
TS = NSB * NST // NCORES
TT = TP + TS
AW = 4096
EPS = 1e-6
DMA_RING = 12
EPOCH = 6000


class Buf:
    __slots__ = ("name", "w", "r", "excl")

    def __init__(self, name, excl=False):
        self.name = name
        self.w = None
        self.r = []
        self.excl = excl


class K:
    def __init__(self, nc, stack):
        self.nc = nc
        self.eng = {"pe": nc.tensor, "act": nc.scalar, "dve": nc.vector,
                    "pool": nc.gpsimd, "sp": nc.sync}
        self.sems = {}
        self.cnt = {}
        self.stack = stack
        self.epoch = {}
        for e in ("pe", "act", "dve", "pool"):
            self.sems[(e, 0)] = stack.enter_context(nc.semaphore("s_" + e + "_0"))
            self.cnt[e] = 0
            self.epoch[e] = 0
        self.dq = {}
        for q, e in (("sp", "sp"), ("poolq", "pool")):
            ring = [stack.enter_context(nc.semaphore(f"d_{q}_{i}")) for i in range(DMA_RING)]
            self.dq[q] = {"eng": e, "ring": ring, "n": 0}
            for i in range(DMA_RING):
                self.sems[(q, i)] = ring[i]
        self.waited = {}
        self.pending = {e: [] for e in ("pe", "act", "dve", "pool")}
        self.nb = 0
        self.ninstr = 0
        self.allbufs = []

    def sb(self, name, shape, dt):
        return self.nc.alloc_sbuf_tensor(name, list(shape), dt)

    def ps(self, name, shape, dt=F32):
        return self.nc.alloc_psum_tensor(name, list(shape), dt)

    def buf(self, name=None, excl=False):
        self.nb += 1
        b = Buf(name or f"b{self.nb}", excl)
        self.allbufs.append(b)
        return b

    def pbuf(self, name=None):
        return self.buf(name, excl=True)

    def _wait(self, engkey, ev):
        semkey, val, src = ev
        if src == engkey and engkey == "pe":
            return
        kk = (engkey, semkey)
        if self.waited.get(kk, 0) >= val:
            return
        self.waited[kk] = val
        self.eng[engkey].wait_ge(self.sems[semkey], val)
        self.ninstr += 1

    def _deps(self, engkey, reads, writes):
        for b in reads:
            if b.w is not None:
                self._wait(engkey, b.w)
            if b.excl:
                for ev in b.r:
                    if ev[2] != engkey:
                        self._wait(engkey, ev)
        for b in writes:
            if b.w is not None:
                self._wait(engkey, b.w)
            for ev in b.r:
                self._wait(engkey, ev)

    def _record(self, ev, reads, writes):
        for b in reads:
            b.r.append(ev)
            if len(b.r) > 16:
                d = {}
                for e in b.r:
                    if e[0] not in d or d[e[0]][1] < e[1]:
                        d[e[0]] = e
                b.r = list(d.values())
        for b in writes:
            b.w = ev
            b.r = []

    def op(self, engkey, fn, reads=(), writes=(), sig=True):
        reads = [b for b in reads if b is not None]
        writes = [b for b in writes if b is not None]
        self._deps(engkey, reads, writes)
        ins = fn()
        self.ninstr += 1
        if sig:
            if self.cnt[engkey] >= EPOCH:
                self.epoch[engkey] += 1
                self.cnt[engkey] = 0
                self.sems[(engkey, self.epoch[engkey])] = self.stack.enter_context(
                    self.nc.semaphore(f"s_{engkey}_{self.epoch[engkey]}"))
            self.cnt[engkey] += 1
            sk = (engkey, self.epoch[engkey])
            ins.then_inc(self.sems[sk], 1)
            ev = (sk, self.cnt[engkey], engkey)
            pr = self.pending[engkey]
            if pr:
                for b in pr:
                    b.r.append(ev)
                self.pending[engkey] = []
            self._record(ev, reads, writes)
        else:
            self.pending[engkey].extend(reads)
            self.pending[engkey].extend(writes)
        return ins

    def dma(self, q, out, in_, reads=(), writes=(), **kw):
        reads = [b for b in reads if b is not None]
        writes = [b for b in writes if b is not None]
        d = self.dq[q]
        engkey = d["eng"]
        i = d["n"]
        slot = i % DMA_RING
        rnd = i // DMA_RING
        if rnd > 0:
            self._wait(engkey, ((q, slot), 16 * rnd, q))
        self._deps(engkey, reads, writes)
        ins = self.eng[engkey].dma_start(out=out, in_=in_, **kw)
        ins.then_inc(d["ring"][slot], 16)
        self.ninstr += 1
        d["n"] = i + 1
        ev = ((q, slot), 16 * (rnd + 1), q)
        self._record(ev, reads, writes)
        return ev

    def finish(self):
        for b in self.allbufs:
            if b.w is not None:
                self._wait("sp", b.w)
            for ev in b.r:
                self._wait("sp", ev)
        for q, d in self.dq.items():
            n = d["n"]
            for i in range(max(0, n - DMA_RING), n):
                self._wait("sp", ((q, i % DMA_RING), 16 * (i // DMA_RING + 1), q))


def make_ident(k, nc):
    identf = k.sb("identf", [128, 128], F32)
    ident = k.sb("identb", [128, 128], BF16)
    bi = k.buf("ident")
    k.op("pool", lambda: nc.gpsimd.memset(identf[:], 0.0), writes=[bi])
    k.op("pool", lambda: nc.gpsimd.affine_select(out=identf[:], in_=identf[:], pattern=[[-1, 128]],
                                                 compare_op=ALU.not_equal, fill=1.0, base=0,
                                                 channel_multiplier=1), reads=[bi], writes=[bi])
    k.op("dve", lambda: nc.vector.tensor_copy(out=ident[:], in_=identf[:]), reads=[bi], writes=[bi])
    return identf, ident, bi


def rstd_from_sumsq(k, nc, out, ssq, n, rb, wb, np_=128):
    k.op("dve", lambda: nc.vector.tensor_scalar(out=out, in0=ssq, scalar1=1.0 / n, scalar2=EPS,
                                               op0=ALU.mult, op1=ALU.add), reads=rb, writes=wb)
    k.op("act", lambda: nc.scalar.activation(out=out, in_=out, func=AF.Sqrt), reads=wb, writes=wb)
    k.op("dve", lambda: nc.vector.reciprocal(out=out, in_=out), reads=wb, writes=wb)


def build_gmlp(final_norm):
    nc = bass.Bass("TRN2", target_bir_lowering=False)
    x = nc.dram_tensor("x", [TT, D], F32, kind="ExternalInput").ap()
    normw = nc.dram_tensor("normw", [D], F32, kind="ExternalInput").ap()
    w_in = nc.dram_tensor("w_in", [D, 3 * AW], F32, kind="ExternalInput").ap()
    ln_g = nc.dram_tensor("ln_g", [AW], F32, kind="ExternalInput").ap()
    ln_b = nc.dram_tensor("ln_b", [AW], F32, kind="ExternalInput").ap()
    w_s = nc.dram_tensor("w_s", [16, 128, 128], F32, kind="ExternalInput").ap()
    b_s = nc.dram_tensor("b_s", [16, 128], F32, kind="ExternalInput").ap()
    w_out = nc.dram_tensor("w_out", [AW, D], F32, kind="ExternalInput").ap()
    xo = nc.dram_tensor("xo", [TT, D], F32, kind="ExternalOutput").ap()
    vsT = nc.dram_tensor("vsT", [AW, TS], F32, kind="ExternalOutput").ap()
    if final_norm:
        fnw = nc.dram_tensor("fnw", [D], F32, kind="ExternalInput").ap()
        yo = nc.dram_tensor("yo", [TT, D], F32, kind="ExternalOutput").ap()

    with ExitStack() as st:
        k = K(nc, st)
        identf, ident, b_id = make_ident(k, nc)
        normw_bc = k.sb("normw_bc", [128, D], F32)
        b_const = k.buf("const")
        k.dma("sp", normw_bc[:], normw.partition_broadcast(128), writes=[b_const])
        if final_norm:
            fnw_bc = k.sb("fnw_bc", [128, D], F32)
            k.dma("sp", fnw_bc[:], fnw.partition_broadcast(128), writes=[b_const])
        ones_b = k.sb("ones_b", [128, 128], BF16)
        k.op("dve", lambda: nc.vector.memset(ones_b[:], 1.0), writes=[b_const])
        lnrow = k.sb("lnrow", [32, 2, 128], F32)
        k.dma("sp", lnrow[:, 0, :], ln_g.rearrange("(a p) -> a p", p=128), writes=[b_const])
        k.dma("sp", lnrow[:, 1, :], ln_b.rearrange("(a p) -> a p", p=128), writes=[b_const])
        lnT = k.sb("lnT", [128, 2, 32], F32)
        pA = k.ps("pA", [128, 512], F32)
        pB = k.ps("pB", [128, 512], F32)
        pC = k.ps("pC", [128, 512], F32)
        pD = k.ps("pD", [128, 512], F32)
        pT = k.ps("pT", [128, 1024], BF16)
        bpA, bpB, bpC, bpD, bpT = (k.pbuf(n) for n in ("pA", "pB", "pC", "pD", "pT"))
        for j in range(2):
            k.op("pe", lambda: nc.tensor.transpose(out=pA[:, j * 32:(j + 1) * 32], in_=lnrow[:, j, :],
                                                   identity=identf[0:32, 0:32]),
                 reads=[b_const, b_id], writes=[bpA])
        k.op("dve", lambda: nc.vector.tensor_copy(out=lnT[:].rearrange("p a b -> p (a b)"), in_=pA[:, 0:64]),
             reads=[bpA], writes=[b_const])
        wnat = [k.sb(f"wnat{i}", [128, 128], F32) for i in range(2)]
        b_wnat = [k.buf() for _ in range(2)]
        wposT = k.sb("wposT", [128, 16, 128], BF16)
        for g in range(16):
            pp = pA if g % 2 == 0 else pB
            bpp = bpA if g % 2 == 0 else bpB
            k.dma("sp", wnat[g % 2][:], w_s[g, :, :], writes=[b_wnat[g % 2]])
            k.op("pe", lambda: nc.tensor.transpose(out=pp[:, 0:128], in_=wnat[g % 2][:], identity=identf[:]),
                 reads=[b_wnat[g % 2], b_id], writes=[bpp])
            k.op("act", lambda: nc.scalar.copy(out=wposT[:, g, :], in_=pp[:, 0:128]), reads=[bpp], writes=[b_const])
        k.op("dve", lambda: nc.vector.memset(wposT[64:128, :, 0:64], 0.0), reads=[b_const], writes=[b_const])
        wposT_s = k.sb("wposT_s", [64, 16, 64], BF16)
        for g in range(16):
            pp = pA if g % 2 == 0 else pB
            bpp = bpA if g % 2 == 0 else bpB
            k.op("dve", lambda: nc.vector.memset(wnat[g % 2][:], 0.0), writes=[b_wnat[g % 2]])
            for b in range(4):
                k.dma("sp", wnat[g % 2][16 * b:16 * b + 16, 16 * b:16 * b + 16], w_s[g, 0:16, 0:16],
                      writes=[b_wnat[g % 2]])
            k.op("pe", lambda: nc.tensor.transpose(out=pp[0:64, 0:64], in_=wnat[g % 2][0:64, 0:64], identity=identf[0:64, 0:64]),
                 reads=[b_wnat[g % 2], b_id], writes=[bpp])
            k.op("act", lambda: nc.scalar.copy(out=wposT_s[:, g, :], in_=pp[0:64, 0:64]), reads=[bpp], writes=[b_const])
        bsbt = [k.sb(f"bsb{i}", [128, 128], F32) for i in range(2)]
        b_bsb = [k.buf() for _ in range(2)]
        biasT = k.sb("biasT", [128, 32, 128], F32)
        biasT_s = k.sb("biasT_s", [128, 32, 64], F32)
        for g in range(16):
            k.dma("sp", bsbt[g % 2][:], b_s[g, :].partition_broadcast(128), writes=[b_bsb[g % 2]])
            k.op("pe", lambda: nc.tensor.matmul(pC[:, 0:128], lhsT=ones_b[:], rhs=wposT[:, g, :], start=True, stop=True),
                 reads=[b_const], writes=[bpC])
            k.op("pe", lambda: nc.tensor.matmul(pC[:, 128:192], lhsT=ones_b[0:64, :], rhs=wposT_s[:, g, :], start=True, stop=True),
                 reads=[b_const], writes=[bpC])
            for j in range(2):
                ft = 2 * g + j
                k.op("dve", lambda: nc.vector.scalar_tensor_tensor(out=biasT[:, ft, :], in0=pC[:, 0:128],
                                                                   scalar=lnT[:, 1, ft:ft + 1], in1=bsbt[g % 2][:],
                                                                   op0=ALU.mult, op1=ALU.add),
                     reads=[bpC, b_const, b_bsb[g % 2]], writes=[b_const])
                for b in range(4):
                    k.op("dve", lambda: nc.vector.scalar_tensor_tensor(out=biasT_s[:, ft, 16 * b:16 * b + 16],
                                                                       in0=pC[:, 128 + 16 * b:128 + 16 * b + 16],
                                                                       scalar=lnT[:, 1, ft:ft + 1], in1=bsbt[g % 2][:, 0:16],
                                                                       op0=ALU.mult, op1=ALU.add),
                         reads=[bpC, b_const, b_bsb[g % 2]], writes=[b_const])

        TB = 512
        xt = k.sb("xt", [128, D], F32)
        hb = k.sb("hb", [128, D], BF16)
        hT = k.sb("hT", [128, 16, TB], BF16)
        gv = k.sb("gv", [128, 4, AW], BF16)
        yT = k.sb("yT", [128, 32, TB], BF16)
        wsl = [k.sb(f"wsl{i}", [128, 16, 512], BF16) for i in range(2)]
        wo = [k.sb(f"wo{i}", [128, 32, 256], BF16) for i in range(1)] * 2
        gvf = [k.sb(f"gvf{i}", [128, 512], F32) for i in range(2)]
        sqf = [k.sb(f"sqf{i}", [128, 512], F32) for i in range(2)]
        guf = gvf
        szf = sqf
        ssf = [k.sb(f"ssf{i}", [128, 512], F32) for i in range(2)]
        stats = k.sb("stats", [128, 4, 2, 8], F32)
        st2 = k.sb("st2", [128, 4, 8], F32)
        small = k.sb("small", [128, 8], F32)
        xc = [k.sb(f"xc{i}", [128, 256], F32) for i in range(2)]
        xn = [k.sb(f"xn{i}", [128, 256], F32) for i in range(2)]
        b_xt, b_hb, b_hT, b_gv, b_yT, b_stats, b_small = (k.buf(n) for n in
                                                          ("xt", "hb", "hT", "gv", "yT", "stats", "small"))
        b_wsl = [k.buf(f"wsl{i}") for i in range(2)]
        b_wo = [k.buf("wo0")] * 2
        b_gvf = [k.buf() for _ in range(2)]
        b_sqf = [k.buf() for _ in range(2)]
        b_guf = b_gvf
        b_szf = b_sqf
        b_ssf = [k.buf() for _ in range(2)]
        b_xc = [k.buf() for _ in range(2)]
        b_xn = [k.buf() for _ in range(2)]
        b_xo = k.buf("xo_dram")
        cnt = {"w": 0, "z": 0, "o": 0, "t": 0, "c": 0}

        def load_slab(dst, bdst, col0):
            for c4 in range(4):
                k.dma("poolq", dst[:, c4 * 4:(c4 + 1) * 4, :],
                      w_in[c4 * 512:(c4 + 1) * 512, col0:col0 + 512].rearrange("(kt p) c -> p kt c", p=128),
                      writes=[bdst])

        blocks = [(i * 512, 512) for i in range(TP // 512)] + [(TP, TS)]
        for (t0, nt) in blocks:
            samp = nt == TS
            ntile = max(1, nt // 128)
            P = min(nt, 128)
            for ti in range(ntile):
                r0 = t0 + ti * 128
                k.dma("sp", xt[:P, :], x[r0:r0 + P, :], writes=[b_xt])
                for q4 in range(4):
                    k.op("act", lambda: nc.scalar.activation(out=gvf[0][:P, :], in_=xt[:P, q4 * 512:(q4 + 1) * 512],
                                                             func=AF.Square), reads=[b_xt], writes=[b_gvf[0]])
                    k.op("dve", lambda: nc.vector.reduce_sum(out=small[:P, q4:q4 + 1], in_=gvf[0][:P, :], axis=AX.X),
                         reads=[b_gvf[0]], writes=[b_small])
                k.op("dve", lambda: nc.vector.reduce_sum(out=small[:P, 4:5], in_=small[:P, 0:4], axis=AX.X),
                     reads=[b_small], writes=[b_small])
                rstd_from_sumsq(k, nc, small[:P, 5:6], small[:P, 4:5], D, [b_small], [b_small])
                k.op("dve", lambda: nc.vector.scalar_tensor_tensor(out=hb[:P, :], in0=xt[:P, :], scalar=small[:P, 5:6],
                                                                   in1=normw_bc[:P, :], op0=ALU.mult, op1=ALU.mult),
                     reads=[b_xt, b_small, b_const], writes=[b_hb])
                for half in range(2):
                    for j in range(8):
                        kt = half * 8 + j
                        k.op("pe", lambda: nc.tensor.transpose(out=pT[:, j * 128:j * 128 + P], in_=hb[:P, kt * 128:(kt + 1) * 128],
                                                               identity=ident[:P, :P]),
                             reads=[b_hb, b_id], writes=[bpT])
                    k.op("act", lambda: nc.scalar.copy(
                        out=hT[:, half * 8:half * 8 + 8, ti * 128:ti * 128 + P],
                        in_=pT[:].rearrange("p (a b) -> p a b", b=128)[:, :, 0:P]),
                        reads=[bpT], writes=[b_hT])
            for vc in range(8):
                i = cnt["w"] % 2
                cnt["w"] += 1
                load_slab(wsl[i], b_wsl[i], AW + vc * 512)
                for ti in range(ntile):
                    pp, bpp = (pA, bpA) if (cnt["t"] % 2 == 0) else (pB, bpB)
                    j = cnt["t"] % 2
                    cnt["t"] += 1
                    for kt in range(16):
                        k.op("pe", lambda: nc.tensor.matmul(pp[:P, :], lhsT=hT[:, kt, ti * 128:ti * 128 + P],
                                                            rhs=wsl[i][:, kt, :], start=(kt == 0), stop=(kt == 15)),
                             reads=[b_hT, b_wsl[i]], writes=[bpp], sig=(kt == 15))
                    k.op("act", lambda: nc.scalar.activation(out=gvf[j][:P, :], in_=pp[:P, :], func=AF.Gelu),
                         reads=[bpp], writes=[b_gvf[j]])
                    k.op("act", lambda: nc.scalar.activation(out=sqf[j][:P, :], in_=gvf[j][:P, :], func=AF.Square),
                         reads=[b_gvf[j]], writes=[b_sqf[j]])
                    k.op("dve", lambda: nc.vector.reduce_sum(out=stats[:P, ti, 0, vc:vc + 1], in_=gvf[j][:P, :], axis=AX.X),
                         reads=[b_gvf[j]], writes=[b_stats])
                    k.op("dve", lambda: nc.vector.reduce_sum(out=stats[:P, ti, 1, vc:vc + 1], in_=sqf[j][:P, :], axis=AX.X),
                         reads=[b_sqf[j]], writes=[b_stats])
                    k.op("dve", lambda: nc.vector.tensor_copy(out=gv[:P, ti, vc * 512:(vc + 1) * 512], in_=gvf[j][:P, :]),
                         reads=[b_gvf[j]], writes=[b_gv])
            for ti in range(ntile):
                k.op("dve", lambda: nc.vector.reduce_sum(out=st2[:P, ti, 0:1], in_=stats[:P, ti, 0, :], axis=AX.X),
                     reads=[b_stats], writes=[b_stats])
                k.op("dve", lambda: nc.vector.reduce_sum(out=st2[:P, ti, 1:2], in_=stats[:P, ti, 1, :], axis=AX.X),
                     reads=[b_stats], writes=[b_stats])
                k.op("dve", lambda: nc.vector.tensor_scalar(out=st2[:P, ti, 2:3], in0=st2[:P, ti, 0:1], scalar1=1.0 / AW,
                                                           scalar2=None, op0=ALU.mult), reads=[b_stats], writes=[b_stats])
                k.op("dve", lambda: nc.vector.tensor_scalar(out=st2[:P, ti, 3:4], in0=st2[:P, ti, 1:2], scalar1=1.0 / AW,
                                                           scalar2=None, op0=ALU.mult), reads=[b_stats], writes=[b_stats])
                k.op("dve", lambda: nc.vector.tensor_tensor(out=st2[:P, ti, 4:5], in0=st2[:P, ti, 2:3], in1=st2[:P, ti, 2:3],
                                                           op=ALU.mult), reads=[b_stats], writes=[b_stats])
                k.op("dve", lambda: nc.vector.tensor_tensor(out=st2[:P, ti, 5:6], in0=st2[:P, ti, 3:4], in1=st2[:P, ti, 4:5],
                                                           op=ALU.subtract), reads=[b_stats], writes=[b_stats])
                k.op("dve", lambda: nc.vector.tensor_scalar(out=st2[:P, ti, 6:7], in0=st2[:P, ti, 5:6], scalar1=EPS,
                                                           scalar2=None, op0=ALU.add),
                     reads=[b_stats], writes=[b_stats])
                k.op("act", lambda: nc.scalar.activation(out=st2[:P, ti, 6:7], in_=st2[:P, ti, 6:7], func=AF.Sqrt),
                     reads=[b_stats], writes=[b_stats])
                k.op("dve", lambda: nc.vector.reciprocal(out=st2[:P, ti, 6:7], in_=st2[:P, ti, 6:7]),
                     reads=[b_stats], writes=[b_stats])
                k.op("dve", lambda: nc.vector.tensor_scalar(out=gv[:P, ti, :], in0=gv[:P, ti, :], scalar1=st2[:P, ti, 2:3],
                                                           scalar2=st2[:P, ti, 6:7], op0=ALU.subtract, op1=ALU.mult),
                     reads=[b_stats, b_gv], writes=[b_gv])
            if samp:
                for ft in range(32):
                    k.op("pe", lambda: nc.tensor.transpose(out=pT[:, 0:P], in_=gv[:P, 0, ft * 128:(ft + 1) * 128],
                                                           identity=ident[:P, :P]), reads=[b_gv, b_id], writes=[bpT])
                    k.op("act", lambda: nc.scalar.activation(out=ssf[ft % 2][:, 0:P], in_=pT[:, 0:P], func=AF.Identity,
                                                             bias=lnT[:, 1, ft:ft + 1], scale=lnT[:, 0, ft:ft + 1]),
                         reads=[bpT, b_const], writes=[b_ssf[ft % 2]])
                    k.dma("sp", vsT[ft * 128:(ft + 1) * 128, :], ssf[ft % 2][:, 0:P], reads=[b_ssf[ft % 2]], writes=[b_xo])
            for fc in range(8):
                load_slab(wsl[0], b_wsl[0], fc * 512)
                load_slab(wsl[1], b_wsl[1], 2 * AW + fc * 512)
                for fj in range(4):
                    ft = fc * 4 + fj
                    g = ft // 2
                    j = cnt["c"] % 2
                    cnt["c"] += 1
                    for kt in range(16):
                        k.op("pe", lambda: nc.tensor.matmul(pA[:, 0:nt], lhsT=wsl[0][:, kt, fj * 128:(fj + 1) * 128],
                                                            rhs=hT[:, kt, 0:nt], start=(kt == 0), stop=(kt == 15)),
                             reads=[b_hT, b_wsl[0]], writes=[bpA], sig=(kt == 15))
                    for kt in range(16):
                        k.op("pe", lambda: nc.tensor.matmul(pB[:, 0:nt], lhsT=wsl[1][:, kt, fj * 128:(fj + 1) * 128],
                                                            rhs=hT[:, kt, 0:nt], start=(kt == 0), stop=(kt == 15)),
                             reads=[b_hT, b_wsl[1]], writes=[bpB], sig=(kt == 15))
                    for ti in range(ntile):
                        if samp:
                            k.op("pe", lambda: nc.tensor.matmul(pC[:, 0:P], lhsT=gv[:P, 0, ft * 128:(ft + 1) * 128],
                                                                rhs=wposT_s[:, g, :], start=True, stop=True),
                                 reads=[b_gv, b_const], writes=[bpC])
                        else:
                            k.op("pe", lambda: nc.tensor.matmul(pC[:, ti * 128:(ti + 1) * 128],
                                                                lhsT=gv[:, ti, ft * 128:(ft + 1) * 128],
                                                                rhs=wposT[:, g, :], start=True, stop=True),
                                 reads=[b_gv, b_const], writes=[bpC])
                    k.op("act", lambda: nc.scalar.activation(out=guf[j][:, 0:nt], in_=pA[:, 0:nt], func=AF.Gelu),
                         reads=[bpA], writes=[b_guf[j]])
                    k.op("act", lambda: nc.scalar.activation(out=szf[j][:, 0:nt], in_=pB[:, 0:nt], func=AF.Silu),
                         reads=[bpB], writes=[b_szf[j]])
                    for ti in range(ntile):
                        bt = biasT_s[:, ft, :] if samp else biasT[:, ft, :]
                        k.op("dve", lambda: nc.vector.scalar_tensor_tensor(
                            out=ssf[j][:, ti * 128:ti * 128 + P], in0=pC[:, ti * 128:ti * 128 + P],
                            scalar=lnT[:, 0, ft:ft + 1], in1=bt, op0=ALU.mult, op1=ALU.add),
                            reads=[bpC, b_const], writes=[b_ssf[j]])
                    k.op("dve", lambda: nc.vector.tensor_tensor(out=ssf[j][:, 0:nt], in0=ssf[j][:, 0:nt], in1=guf[j][:, 0:nt],
                                                               op=ALU.mult), reads=[b_ssf[j], b_guf[j]], writes=[b_ssf[j]])
                    k.op("dve", lambda: nc.vector.tensor_tensor(out=yT[:, ft, 0:nt], in0=ssf[j][:, 0:nt], in1=szf[j][:, 0:nt],
                                                               op=ALU.mult), reads=[b_ssf[j], b_szf[j]], writes=[b_yT])
            for oc in range(8):
                i = cnt["o"] % 2
                cnt["o"] += 1
                for c4 in range(4):
                    k.dma("poolq", wo[i][:, c4 * 8:(c4 + 1) * 8, :],
                          w_out[c4 * 1024:(c4 + 1) * 1024, oc * 256:(oc + 1) * 256].rearrange("(kt p) c -> p kt c", p=128),
                          writes=[b_wo[i]])
                for ti in range(ntile):
                    r0 = t0 + ti * 128
                    j = cnt["t"] % 2
                    cnt["t"] += 1
                    pp, bpp = (pD, bpD) if j == 0 else (pB, bpB)
                    k.dma("sp", xc[j][:P, :], x[r0:r0 + P, oc * 256:(oc + 1) * 256], writes=[b_xc[j]])
                    for ft in range(32):
                        k.op("pe", lambda: nc.tensor.matmul(pp[:P, 0:256], lhsT=yT[:, ft, ti * 128:ti * 128 + P],
                                                            rhs=wo[i][:, ft, :], start=(ft == 0), stop=(ft == 31)),
                             reads=[b_yT, b_wo[i]], writes=[bpp], sig=(ft == 31))
                    k.op("dve", lambda: nc.vector.tensor_tensor(out=xn[j][:P, :], in0=pp[:P, 0:256], in1=xc[j][:P, :],
                                                               op=ALU.add), reads=[bpp, b_xc[j]], writes=[b_xn[j]])
                    k.dma("sp", xo[r0:r0 + P, oc * 256:(oc + 1) * 256], xn[j][:P, :], reads=[b_xn[j]], writes=[b_xo])
            if final_norm:
                for ti in range(ntile):
                    r0 = t0 + ti * 128
                    k.dma("sp", hT[:P, :, :].rearrange("p a b -> p (a b)").bitcast(F32)[:, 0:D], xo[r0:r0 + P, :],
                          reads=[b_xo], writes=[b_hT])
                    xfl = hT[:P, :, :].rearrange("p a b -> p (a b)").bitcast(F32)
                    for q4 in range(4):
                        k.op("act", lambda: nc.scalar.activation(out=gvf[0][:P, :], in_=xfl[:, q4 * 512:(q4 + 1) * 512],
                                                                 func=AF.Square), reads=[b_hT], writes=[b_gvf[0]])
                        k.op("dve", lambda: nc.vector.reduce_sum(out=small[:P, q4:q4 + 1], in_=gvf[0][:P, :], axis=AX.X),
                             reads=[b_gvf[0]], writes=[b_small])
                    k.op("dve", lambda: nc.vector.reduce_sum(out=small[:P, 4:5], in_=small[:P, 0:4], axis=AX.X),
                         reads=[b_small], writes=[b_small])
                    rstd_from_sumsq(k, nc, small[:P, 5:6], small[:P, 4:5], D, [b_small], [b_small])
                    k.op("dve", lambda: nc.vector.scalar_tensor_tensor(out=xt[:P, :], in0=xfl[:, 0:D], scalar=small[:P, 5:6],
                                                                       in1=fnw_bc[:P, :], op0=ALU.mult, op1=ALU.mult),
                         reads=[b_hT, b_small, b_const], writes=[b_xt])
                    k.dma("sp", yo[r0:r0 + P, :], xt[:P, :], reads=[b_xt], writes=[b_xo])
        k.finish()
    return nc


def _run(nc, in_maps):
    res = run_bass_kernel_spmd(nc, in_maps, core_ids=list(range(NCORES)))
    return res.results


def tok_shard(xp, xs, c):
    return np.ascontiguousarray(np.concatenate(
        [xp[c * TP:(c + 1) * TP], xs[c * 4:(c + 1) * 4].reshape(TS, -1)], axis=0))


def run_gmlp(xp, xs, normw, w_in, ln_g, ln_b, w_s, b_s, w_out, fnw=None):
    nc = build_gmlp(fnw is not None)
    in_maps = []
    for c in range(NCORES):
        m = {"x": tok_shard(xp, xs, c), "normw": normw, "w_in": w_in, "ln_g": ln_g, "ln_b": ln_b,
             "w_s": w_s, "b_s": b_s, "w_out": w_out}
        if fnw is not None:
            m["fnw"] = fnw
        in_maps.append(m)
    return _run(nc, in_maps)


NTOK = SEQ + NSB * NST
GC = 768


def build_mamba():
    nc = bass.Bass("TRN2", target_bir_lowering=False)
    xT = nc.dram_tensor("xT", [D, NTOK], F32, kind="ExternalInput").ap()
    normwT = nc.dram_tensor("normwT", [128, 16], F32, kind="ExternalInput").ap()
    w_z = nc.dram_tensor("w_z", [D, 512], F32, kind="ExternalInput").ap()
    w_xbc = nc.dram_tensor("w_xbc", [D, GC], F32, kind="ExternalInput").ap()
    w_dt = nc.dram_tensor("w_dt", [D, 8], F32, kind="ExternalInput").ap()
    convw = nc.dram_tensor("convw", [128, 6, 4], F32, kind="ExternalInput").ap()
    convb = nc.dram_tensor("convb", [128, 6], F32, kind="ExternalInput").ap()
    hp = nc.dram_tensor("hp", [3, 8], F32, kind="ExternalInput").ap()
    dexp = nc.dram_tensor("dexp", [512], F32, kind="ExternalInput").ap()
    nw = nc.dram_tensor("nw", [512], F32, kind="ExternalInput").ap()
    sT_in = nc.dram_tensor("sT_in", [NSB, 128, 512], F32, kind="ExternalInput").ap()
    cs_in = nc.dram_tensor("cs_in", [128, 6, NSB, 3], F32, kind="ExternalInput").ap()
    yn = nc.dram_tensor("yn", [NTOK, 512], F32, kind="ExternalOutput").ap()
    sT_p = nc.dram_tensor("sT_p", [128, 512], F32, kind="ExternalOutput").ap()
    cs_p = nc.dram_tensor("cs_p", [128, 6, 3], F32, kind="ExternalOutput").ap()
    sT_s = nc.dram_tensor("sT_s", [NSB, 128, 512], F32, kind="ExternalOutput").ap()
    cs_s = nc.dram_tensor("cs_s", [128, 6, NSB, 3], F32, kind="ExternalOutput").ap()

    with ExitStack() as st:
        k = K(nc, st)
        identf, ident, b_id = make_ident(k, nc)
        bc = k.buf("const")
        nwT = k.sb("nwT", [128, 16], F32)
        k.dma("sp", nwT[:], normwT[:, :], writes=[bc])
        cw = k.sb("cw", [128, 6, 4], F32)
        k.dma("sp", cw[:], convw[:, :, :], writes=[bc])
        cbias = k.sb("cbias", [128, 6], F32)
        k.dma("sp", cbias[:], convb[:, :], writes=[bc])
        hpb = k.sb("hpb", [128, 3, 8], F32)
        k.dma("sp", hpb[:].rearrange("p a b -> p (a b)"), hp.rearrange("a b -> (a b)").partition_broadcast(128), writes=[bc])
        a_bc = k.sb("a_bc", [128, 8], F32)
        k.op("act", lambda: nc.scalar.activation(out=a_bc[:], in_=hpb[:, 1, :], func=AF.Exp), reads=[bc], writes=[bc])
        k.op("dve", lambda: nc.vector.tensor_scalar(out=a_bc[:], in0=a_bc[:], scalar1=-1.0, scalar2=None, op0=ALU.mult),
             reads=[bc], writes=[bc])
        d_bc = k.sb("d_bc", [128, 512], F32)
        k.dma("sp", d_bc[:], dexp.partition_broadcast(128), writes=[bc])
        nw_bc = k.sb("nw_bc", [128, 512], F32)
        k.dma("sp", nw_bc[:], nw.partition_broadcast(128), writes=[bc])
        ones_b = k.sb("ones_b", [128, 128], BF16)
        k.op("dve", lambda: nc.vector.memset(ones_b[:], 1.0), writes=[bc])
        ones_f = k.sb("ones_f", [128, 128], F32)
        k.op("dve", lambda: nc.vector.memset(ones_f[:], 1.0), writes=[bc])
        tri_le = k.sb("tri_le", [128, 128], F32)
        k.op("pool", lambda: nc.gpsimd.affine_select(out=tri_le[:], in_=ones_f[:], pattern=[[1, 128]],
                                                     compare_op=ALU.is_ge, fill=0.0, base=0, channel_multiplier=-1),
             reads=[bc], writes=[bc])
        mgt = k.sb("mgt", [128, 128], F32)
        k.op("pool", lambda: nc.gpsimd.affine_select(out=mgt[:], in_=ones_f[:], pattern=[[-1, 128]],
                                                     compare_op=ALU.is_gt, fill=0.0, base=0, channel_multiplier=1),
             reads=[bc], writes=[bc])
        Wx = k.sb("Wx", [128, 16, GC], BF16)
        Wz = k.sb("Wz", [128, 16, 512], BF16)
        Wd = k.sb("Wd", [128, 16, 8], BF16)
        for c4 in range(4):
            k.dma("poolq", Wx[:, c4 * 4:(c4 + 1) * 4, :], w_xbc[c4 * 512:(c4 + 1) * 512, :].rearrange("(kt p) c -> p kt c", p=128), writes=[bc])
            k.dma("poolq", Wz[:, c4 * 4:(c4 + 1) * 4, :], w_z[c4 * 512:(c4 + 1) * 512, :].rearrange("(kt p) c -> p kt c", p=128), writes=[bc])
        k.dma("poolq", Wd[:], w_dt.rearrange("(kt p) c -> p kt c", p=128), writes=[bc])

        pX = k.ps("pX", [128, 1024], F32)
        pZ = k.ps("pZ", [128, 512], F32)
        pSeg = k.ps("pSeg", [128, 1024], F32)
        pY = k.ps("pY", [128, 512], F32)
        pYi = k.ps("pYi", [128, 512], F32)
        pM = k.ps("pM", [128, 512], F32)
        bpX, bpZ, bpSeg, bpY, bpYi, bpM = (k.pbuf(n) for n in ("pX", "pZ", "pSeg", "pY", "pYi", "pM"))
        pYi_b = pYi[:].bitcast(BF16)

        xTb = k.sb("xTb", [128, 16, 512], F32)
        sqb = [k.sb(f"sqb{i}", [128, 512], BF16) for i in range(2)]
        hT = k.sb("hT", [128, 16, 512], BF16)
        rstd = k.sb("rstd", [128, 512], F32)
        cbuf = [k.sb(f"cbuf{i}", [128, 6, 131], F32) for i in range(2)]
        cv = k.sb("cv", [128, 6, 128], F32)
        xbc = k.sb("xbc", [128, 6, 128], BF16)
        Et = k.sb("Et", [128, 8, 128], F32)
        MT = k.sb("MT", [128, 8, 128], BF16)
        lh = k.sb("lh", [128, 8, 128], F32)
        x_tok = k.sb("x_tok", [128, 512], BF16)
        B_tok = k.sb("B_tok", [128, 128], BF16)
        xdt = k.sb("xdt", [128, 512], BF16)
        xw = k.sb("xw", [128, 512], BF16)
        y_sb = k.sb("y_sb", [128, 512], F32)
        zs = k.sb("zs", [128, 512], F32)
        yg = k.sb("yg", [128, 512], F32)
        ysq = k.sb("ysq", [128, 512], F32)
        yo = [k.sb(f"yo{i}", [128, 512], F32) for i in range(2)]
        ST = k.sb("ST", [128, 512], F32)
        STb = k.sb("STb", [128, 512], BF16)
        dtt = k.sb("dtt", [128, 8], F32)
        da = k.sb("da", [128, 8], F32)
        ecum = k.sb("ecum", [128, 8], F32)
        dec = k.sb("dec", [128, 8], F32)
        cbm = k.sb("cbm", [128, 128], F32)
        sm = k.sb("sm", [128, 4], F32)
        cso = k.sb("cso", [128, 6, NSB, 3], F32)
        (b_xTb, b_hT, b_rstd, b_cv, b_xbc, b_Et, b_MT, b_lh, b_xtok, b_Btok, b_xdt, b_xw, b_ysb, b_zs, b_yg, b_ysq,
         b_ST, b_STb, b_dt, b_da, b_ecum, b_dec, b_cbm, b_sm, b_cso, b_out) = (k.buf() for _ in range(26))
        b_sqb = [k.buf() for _ in range(2)]
        b_cbuf = [k.buf() for _ in range(2)]
        b_yo = [k.buf() for _ in range(2)]
        k.dma("sp", cso[:], cs_in[:, :, :, :], writes=[b_cso])
        k.op("dve", lambda: nc.vector.memset(ST[:], 0.0), writes=[b_ST])
        k.op("dve", lambda: nc.vector.memset(STb[:], 0.0), writes=[b_STb])
        k.op("dve", lambda: nc.vector.memset(cbuf[0][:], 0.0), writes=[b_cbuf[0]])
        k.op("dve", lambda: nc.vector.memset(cbuf[1][:], 0.0), writes=[b_cbuf[1]])
        ucount = [0]

        def unit(u0, L, tok0, sample_b):
            ci = ucount[0] % 2
            ucount[0] += 1
            cb_, bcb = cbuf[ci], b_cbuf[ci]
            cbn, bcbn = cbuf[1 - ci], b_cbuf[1 - ci]
            if sample_b is not None:
                k.op("dve", lambda: nc.vector.tensor_copy(out=cb_[:, :, 0:3], in_=cso[:, :, sample_b, :]),
                     reads=[b_cso], writes=[bcb])
                k.dma("sp", ST[:], sT_in[sample_b, :, :], writes=[b_ST])
                k.op("act", lambda: nc.scalar.copy(out=STb[:], in_=ST[:]), reads=[b_ST], writes=[b_STb])
            for ct in range(6):
                for kt in range(16):
                    k.op("pe", lambda: nc.tensor.matmul(pX[:, ct * 128:ct * 128 + L], lhsT=Wx[:, kt, ct * 128:(ct + 1) * 128],
                                                        rhs=hT[:, kt, u0:u0 + L], start=(kt == 0), stop=(kt == 15)),
                         reads=[b_hT, bc], writes=[bpX], sig=(kt == 15))
            k.op("act", lambda: nc.scalar.copy(out=cb_[:, 0:4, 3:3 + L],
                                               in_=pX[:, 0:512].rearrange("p (a b) -> p a b", b=128)[:, :, 0:L]),
                 reads=[bpX], writes=[bcb])
            k.op("act", lambda: nc.scalar.copy(out=cb_[:, 4:6, 3:3 + L],
                                               in_=pX[:, 512:768].rearrange("p (a b) -> p a b", b=128)[:, :, 0:L]),
                 reads=[bpX], writes=[bcb])
            if sample_b is not None:
                k.op("dve", lambda: nc.vector.tensor_copy(out=cso[:, :, sample_b, :], in_=cb_[:, :, L:L + 3]),
                     reads=[bcb], writes=[b_cso])
            else:
                k.op("dve", lambda: nc.vector.tensor_copy(out=cbn[:, :, 0:3], in_=cb_[:, :, L:L + 3]),
                     reads=[bcb], writes=[bcbn])
            for ct in range(6):
                k.op("act", lambda: nc.scalar.activation(out=cv[:, ct, 0:L], in_=cb_[:, ct, 0:L], func=AF.Identity,
                                                         bias=cbias[:, ct:ct + 1], scale=cw[:, ct, 0:1]),
                     reads=[bcb, bc], writes=[b_cv])
                for kk in range(1, 4):
                    k.op("dve", lambda: nc.vector.scalar_tensor_tensor(out=cv[:, ct, 0:L], in0=cb_[:, ct, kk:kk + L],
                                                                       scalar=cw[:, ct, kk:kk + 1], in1=cv[:, ct, 0:L],
                                                                       op0=ALU.mult, op1=ALU.add),
                         reads=[bcb, bc, b_cv], writes=[b_cv])
            k.op("act", lambda: nc.scalar.activation(out=xbc[:, :, 0:L], in_=cv[:, :, 0:L], func=AF.Silu),
                 reads=[b_cv], writes=[b_xbc])
            for kt in range(16):
                k.op("pe", lambda: nc.tensor.matmul(pZ[:L, :], lhsT=hT[:, kt, u0:u0 + L], rhs=Wz[:, kt, :],
                                                    start=(kt == 0), stop=(kt == 15)),
                     reads=[b_hT, bc], writes=[bpZ], sig=(kt == 15))
            for kt in range(16):
                k.op("pe", lambda: nc.tensor.matmul(pM[:L, 0:8], lhsT=hT[:, kt, u0:u0 + L], rhs=Wd[:, kt, :],
                                                    start=(kt == 0), stop=(kt == 15)),
                     reads=[b_hT, bc], writes=[bpM], sig=(kt == 15))
            k.op("dve", lambda: nc.vector.tensor_tensor(out=dtt[:L, :], in0=pM[:L, 0:8], in1=hpb[:L, 0, :], op=ALU.add),
                 reads=[bpM, bc], writes=[b_dt])
            k.op("act", lambda: nc.scalar.activation(out=dtt[:L, :], in_=dtt[:L, :], func=AF.Exp), reads=[b_dt], writes=[b_dt])
            k.op("act", lambda: nc.scalar.activation(out=dtt[:L, :], in_=dtt[:L, :], func=AF.Ln, bias=1.0), reads=[b_dt], writes=[b_dt])
            k.op("dve", lambda: nc.vector.tensor_tensor(out=da[:L, :], in0=dtt[:L, :], in1=a_bc[:L, :], op=ALU.mult),
                 reads=[b_dt, bc], writes=[b_da])
            for ct in range(5):
                k.op("pe", lambda: nc.tensor.transpose(out=pYi_b[:L, ct * 128:(ct + 1) * 128], in_=xbc[:, ct, 0:L], identity=ident[:]),
                     reads=[b_xbc, b_id], writes=[bpYi])
            k.op("act", lambda: nc.scalar.copy(out=x_tok[:L, :], in_=pYi_b[:L, 0:512]), reads=[bpYi], writes=[b_xtok])
            k.op("act", lambda: nc.scalar.copy(out=B_tok[:L, :], in_=pYi_b[:L, 512:640]), reads=[bpYi], writes=[b_Btok])
            for h in range(8):
                k.op("dve", lambda: nc.vector.tensor_scalar(out=xdt[:L, h * 64:(h + 1) * 64], in0=x_tok[:L, h * 64:(h + 1) * 64],
                                                           scalar1=dtt[:L, h:h + 1], scalar2=None, op0=ALU.mult),
                     reads=[b_xtok, b_dt], writes=[b_xdt])
            for h in range(8):
                k.op("dve", lambda: nc.vector.tensor_scalar(out=lh[:L, h, 0:L], in0=mgt[:L, 0:L], scalar1=da[:L, h:h + 1],
                                                           scalar2=None, op0=ALU.mult), reads=[b_da, bc], writes=[b_lh])
            for h in range(8):
                k.op("pe", lambda: nc.tensor.matmul(pSeg[:L, h * 128:h * 128 + L], lhsT=lh[:L, h, 0:L], rhs=tri_le[:L, 0:L],
                                                    start=True, stop=True), reads=[b_lh, bc], writes=[bpSeg])
            for hb_ in range(2):
                k.op("act", lambda: nc.scalar.activation(
                    out=Et[:L, hb_ * 4:hb_ * 4 + 4, 0:L],
                    in_=pSeg[:L, hb_ * 512:(hb_ + 1) * 512].rearrange("p (a b) -> p a b", b=128)[:, :, 0:L], func=AF.Exp),
                    reads=[bpSeg], writes=[b_Et])
            k.op("pe", lambda: nc.tensor.matmul(pM[:L, 8:16], lhsT=tri_le[:L, 0:L], rhs=da[:L, :], start=True, stop=True),
                 reads=[b_da, bc], writes=[bpM])
            k.op("pe", lambda: nc.tensor.matmul(pM[:, 16:24], lhsT=ones_f[:L, :], rhs=da[:L, :], start=True, stop=True),
                 reads=[b_da, bc], writes=[bpM])
            k.op("act", lambda: nc.scalar.activation(out=ecum[:L, :], in_=pM[:L, 8:16], func=AF.Exp), reads=[bpM], writes=[b_ecum])
            k.op("act", lambda: nc.scalar.activation(out=dec[:, :], in_=pM[:, 16:24], func=AF.Exp), reads=[bpM], writes=[b_dec])
            k.op("pe", lambda: nc.tensor.matmul(pM[:L, 128:128 + L], lhsT=xbc[:, 4, 0:L], rhs=xbc[:, 5, 0:L], start=True, stop=True),
                 reads=[b_xbc], writes=[bpM])
            k.op("dve", lambda: nc.vector.tensor_tensor(out=cbm[:L, 0:L], in0=pM[:L, 128:128 + L], in1=tri_le[:L, 0:L], op=ALU.mult),
                 reads=[bpM, bc], writes=[b_cbm])
            for h in range(8):
                k.op("dve", lambda: nc.vector.tensor_tensor(out=MT[:L, h, 0:L], in0=Et[:L, h, 0:L], in1=cbm[:L, 0:L], op=ALU.mult),
                     reads=[b_Et, b_cbm], writes=[b_MT])
            for h in range(8):
                k.op("pe", lambda: nc.tensor.matmul(pY[:L, h * 64:(h + 1) * 64], lhsT=MT[:L, h, 0:L], rhs=xdt[:L, h * 64:(h + 1) * 64],
                                                    start=True, stop=True), reads=[b_MT, b_xdt], writes=[bpY])
            k.op("pe", lambda: nc.tensor.matmul(pYi[:L, :], lhsT=xbc[:, 5, 0:L], rhs=STb[:], start=True, stop=True),
                 reads=[b_xbc, b_STb], writes=[bpYi])
            k.op("act", lambda: nc.scalar.copy(out=y_sb[:L, :], in_=pY[:L, :]), reads=[bpY], writes=[b_ysb])
            for h in range(8):
                k.op("dve", lambda: nc.vector.scalar_tensor_tensor(out=y_sb[:L, h * 64:(h + 1) * 64], in0=pYi[:L, h * 64:(h + 1) * 64],
                                                                   scalar=ecum[:L, h:h + 1], in1=y_sb[:L, h * 64:(h + 1) * 64],
                                                                   op0=ALU.mult, op1=ALU.add),
                     reads=[bpYi, b_ecum, b_ysb], writes=[b_ysb])
            k.op("dve", lambda: nc.vector.tensor_tensor(out=yg[:L, :], in0=x_tok[:L, :], in1=d_bc[:L, :], op=ALU.mult),
                 reads=[b_xtok, bc], writes=[b_yg])
            k.op("dve", lambda: nc.vector.tensor_tensor(out=y_sb[:L, :], in0=y_sb[:L, :], in1=yg[:L, :], op=ALU.add),
                 reads=[b_ysb, b_yg], writes=[b_ysb])
            for h in range(8):
                k.op("dve", lambda: nc.vector.tensor_scalar(out=xw[:L, h * 64:(h + 1) * 64], in0=xdt[:L, h * 64:(h + 1) * 64],
                                                           scalar1=Et[:L, h, L - 1:L], scalar2=None, op0=ALU.mult),
                     reads=[b_xdt, b_Et], writes=[b_xw])
            k.op("pe", lambda: nc.tensor.matmul(pX[:, 0:512], lhsT=B_tok[:L, :], rhs=xw[:L, :], start=True, stop=True),
                 reads=[b_Btok, b_xw], writes=[bpX])
            for h in range(8):
                k.op("dve", lambda: nc.vector.scalar_tensor_tensor(out=ST[:, h * 64:(h + 1) * 64], in0=ST[:, h * 64:(h + 1) * 64],
                                                                   scalar=dec[:, h:h + 1], in1=pX[:, h * 64:(h + 1) * 64],
                                                                   op0=ALU.mult, op1=ALU.add),
                     reads=[b_ST, b_dec, bpX], writes=[b_ST])
            if sample_b is not None:
                k.dma("sp", sT_s[sample_b, :, :], ST[:], reads=[b_ST], writes=[b_out])
            else:
                k.op("act", lambda: nc.scalar.copy(out=STb[:], in_=ST[:]), reads=[b_ST], writes=[b_STb])
            k.op("act", lambda: nc.scalar.activation(out=zs[:L, :], in_=pZ[:L, :], func=AF.Silu), reads=[bpZ], writes=[b_zs])
            k.op("dve", lambda: nc.vector.tensor_tensor(out=yg[:L, :], in0=y_sb[:L, :], in1=zs[:L, :], op=ALU.mult),
                 reads=[b_ysb, b_zs], writes=[b_yg])
            k.op("act", lambda: nc.scalar.activation(out=ysq[:L, :], in_=yg[:L, :], func=AF.Square), reads=[b_yg], writes=[b_ysq])
            k.op("dve", lambda: nc.vector.reduce_sum(out=sm[:L, 0:1], in_=ysq[:L, :], axis=AX.X), reads=[b_ysq], writes=[b_sm])
            rstd_from_sumsq(k, nc, sm[:L, 1:2], sm[:L, 0:1], 512, [b_sm], [b_sm])
            oi = ucount[0] % 2
            k.op("dve", lambda: nc.vector.scalar_tensor_tensor(out=yo[oi][:L, :], in0=yg[:L, :], scalar=sm[:L, 1:2], in1=nw_bc[:L, :],
                                                               op0=ALU.mult, op1=ALU.mult),
                 reads=[b_yg, b_sm, bc], writes=[b_yo[oi]])
            k.dma("sp", yn[tok0:tok0 + L, :], yo[oi][:L, :], reads=[b_yo[oi]], writes=[b_out])
            return cbn, bcbn

        nblk = NTOK // 512
        last = None
        for bi in range(nblk):
            c0 = bi * 512
            for c4 in range(4):
                k.dma("sp", xTb[:, c4 * 4:(c4 + 1) * 4, :],
                      xT[c4 * 512:(c4 + 1) * 512, c0:c0 + 512].rearrange("(kt p) t -> p kt t", p=128), writes=[b_xTb])
            for kt in range(16):
                j = kt % 2
                k.op("act", lambda: nc.scalar.activation(out=sqb[j][:], in_=xTb[:, kt, :], func=AF.Square),
                     reads=[b_xTb], writes=[b_sqb[j]])
                k.op("pe", lambda: nc.tensor.matmul(pZ[:, :], lhsT=ones_b[:], rhs=sqb[j][:], start=(kt == 0), stop=(kt == 15)),
                     reads=[b_sqb[j], bc], writes=[bpZ])
            k.op("dve", lambda: nc.vector.tensor_scalar(out=rstd[:], in0=pZ[:, :], scalar1=1.0 / D, scalar2=EPS,
                                                       op0=ALU.mult, op1=ALU.add), reads=[bpZ], writes=[b_rstd])
            k.op("act", lambda: nc.scalar.activation(out=rstd[:], in_=rstd[:], func=AF.Sqrt), reads=[b_rstd], writes=[b_rstd])
            k.op("dve", lambda: nc.vector.reciprocal(out=rstd[:], in_=rstd[:]), reads=[b_rstd], writes=[b_rstd])
            for kt in range(16):
                k.op("dve", lambda: nc.vector.scalar_tensor_tensor(out=hT[:, kt, :], in0=xTb[:, kt, :], scalar=nwT[:, kt:kt + 1],
                                                                   in1=rstd[:], op0=ALU.mult, op1=ALU.mult),
                     reads=[b_xTb, b_rstd, bc], writes=[b_hT])
            if bi < SEQ // 512:
                for u in range(4):
                    last = unit(u * 128, 128, c0 + u * 128, None)
                if bi == SEQ // 512 - 1:
                    k.dma("sp", sT_p[:, :], ST[:], reads=[b_ST], writes=[b_out])
                    k.dma("sp", cs_p[:, :, :], last[0][:, :, 0:3], reads=[last[1]], writes=[b_out])
            else:
                for b in range(NSB):
                    unit(b * 16, 16, c0 + b * 16, b)
        k.dma("sp", cs_s[:, :, :, :], cso[:], reads=[b_cso], writes=[b_out])
        k.finish()
    return nc


PAST = 1024


def build_attn(dbg_heads=2, dbg_blocks=None, dbg_attend=True):
    nc = bass.Bass("TRN2", target_bir_lowering=False)
    xT = nc.dram_tensor("xT", [D, NTOK], F32, kind="ExternalInput").ap()
    normwT = nc.dram_tensor("normwT", [128, 16], F32, kind="ExternalInput").ap()
    w_att = nc.dram_tensor("w_att", [2, D, 512], F32, kind="ExternalInput").ap()
    ckT = nc.dram_tensor("ckT", [NSB, 2, 128, PAST], F32, kind="ExternalInput").ap()
    cv_ = nc.dram_tensor("cv", [NSB, 2, PAST, 128], F32, kind="ExternalInput").ap()
    ogT = nc.dram_tensor("ogT", [256, NTOK], F32, kind="ExternalOutput").ap()
    kTo = nc.dram_tensor("kTo", [2, 128, NTOK], F32, kind="ExternalOutput").ap()
    vo = nc.dram_tensor("vo", [2, NTOK, 128], F32, kind="ExternalOutput").ap()
    SCALE = 128 ** -0.5

    with ExitStack() as st:
        k = K(nc, st)
        identf, ident, b_id = make_ident(k, nc)
        bc = k.buf("const")
        nwT = k.sb("nwT", [128, 16], F32)
        k.dma("sp", nwT[:], normwT[:, :], writes=[bc])
        ones_b = k.sb("ones_b", [128, 128], BF16)
        k.op("dve", lambda: nc.vector.memset(ones_b[:], 1.0), writes=[bc])
        ones_f = k.sb("ones_f", [128, 128], F32)
        k.op("dve", lambda: nc.vector.memset(ones_f[:], 1.0), writes=[bc])
        ones_w = k.sb("ones_w", [128, 512], F32)
        k.op("dve", lambda: nc.vector.memset(ones_w[:], 1.0), writes=[bc])
        mgt = k.sb("mgt", [128, 128], F32)
        k.op("pool", lambda: nc.gpsimd.affine_select(out=mgt[:], in_=ones_f[:], pattern=[[-1, 128]],
                                                     compare_op=ALU.is_gt, fill=0.0, base=0, channel_multiplier=1),
             reads=[bc], writes=[bc])
        m01 = k.sb("m01", [128, 4, 512], F32)
        negm = k.sb("negm", [128, 4, 512], F32)
        for jl in range(4):
            k.op("pool", lambda: nc.gpsimd.affine_select(out=m01[:, jl, :], in_=ones_w[:], pattern=[[-1, 512]],
                                                         compare_op=ALU.is_gt, fill=0.0, base=jl * 128, channel_multiplier=1),
                 reads=[bc], writes=[bc])
        k.op("dve", lambda: nc.vector.tensor_scalar(out=negm[:].rearrange("p a b -> p (a b)"), in0=m01[:].rearrange("p a b -> p (a b)"),
                                                   scalar1=-1.0, scalar2=1e30, op0=ALU.add, op1=ALU.mult), reads=[bc], writes=[bc])

        pQ = k.ps("pQ", [128, 512], F32)
        pV = k.ps("pV", [128, 512], F32)
        pS = [k.ps(f"pS{i}", [128, 512], F32) for i in range(2)]
        pT = [k.ps(f"pT{i}", [128, 1024], BF16) for i in range(2)]
        pO = k.ps("pO", [128, 512], F32)
        bpQ, bpV, bpO = (k.pbuf(n) for n in ("pQ", "pV", "pO"))
        bpS = [k.pbuf() for _ in range(2)]
        bpT = [k.pbuf() for _ in range(2)]

        xTb = k.sb("xTb", [128, 16, 512], F32)
        sqb = [k.sb(f"sqb{i}", [128, 512], BF16) for i in range(2)]
        hT = k.sb("hT", [128, 16, 512], BF16)
        rstd = k.sb("rstd", [128, 512], F32)
        W = k.sb("W", [128, 16, 512], BF16)
        kT_all = k.sb("kT_all", [128, SEQ], BF16)
        v_all = k.sb("v_all", [128, SEQ // 128, 128], BF16)
        qT = k.sb("qT", [128, 512], BF16)
        kT_s = k.sb("kT_s", [128, 512], BF16)
        v_s = k.sb("v_s", [16, 128], BF16)
        szT = k.sb("szT", [128, 512], F32)
        kf = k.sb("kf", [128, 512], F32)
        vf = [k.sb(f"vf{i}", [128, 128], F32) for i in range(2)]
        kc = k.sb("kc", [128, PAST], BF16)
        vc = k.sb("vc", [128, PAST // 128, 128], BF16)
        ef = [k.sb(f"ef{i}", [128, 512], F32) for i in range(2)]
        spf = [k.sb(f"spf{i}", [128, 512], F32) for i in range(2)]
        t1 = [k.sb(f"t1{i}", [128, 512], F32) for i in range(2)]
        t2 = [k.sb(f"t2{i}", [128, 512], F32) for i in range(2)]
        wT = [k.sb(f"wT{i}", [128, 512], BF16) for i in range(2)]
        wTt = [k.sb(f"wTt{i}", [128, 512], BF16) for i in range(2)]
        b_wTt = [k.buf() for _ in range(2)]
        carry = [k.sb(f"carry{i}", [128, 2], F32) for i in range(2)]
        ogf = [k.sb(f"ogf{i}", [128, 512], F32) for i in range(2)]
        (b_xTb, b_hT, b_rstd, b_W, b_kT, b_v, b_qT, b_kTs, b_vs, b_szT, b_kf, b_kc, b_vc, b_out) = (k.buf() for _ in range(14))
        b_sqb = [k.buf() for _ in range(2)]
        b_vf = [k.buf() for _ in range(2)]
        b_ef = [k.buf() for _ in range(2)]
        b_spf = [k.buf() for _ in range(2)]
        b_t1 = [k.buf() for _ in range(2)]
        b_t2 = [k.buf() for _ in range(2)]
        b_wT = [k.buf() for _ in range(2)]
        b_carry = [k.buf() for _ in range(2)]
        b_ogf = [k.buf() for _ in range(2)]
        tc = [0]
        cc = [0]
        oc = [0]

        def attend(q_ap, T, tiles):
            nt_ = len(tiles)
            base = tc[0]
            tc[0] += nt_
            c_first = cc[0] % 2
            cc[0] += nt_
            k.op("pool", lambda: nc.gpsimd.memset(carry[c_first][:, :], 0.0), writes=[b_carry[c_first]])

            def stage_a(idx):
                k_ap, vs, J, m_ap, n_ap, rds = tiles[idx]
                i = (base + idx) % 2
                k.op("pe", lambda: nc.tensor.matmul(pS[i][:T, 0:J], lhsT=q_ap, rhs=k_ap, start=True, stop=True),
                     reads=rds + [b_qT], writes=[bpS[i]])
                k.op("act", lambda: nc.scalar.activation(out=ef[i][:T, 0:J], in_=pS[i][:T, 0:J], func=AF.Exp),
                     reads=[bpS[i]], writes=[b_ef[i]])
                k.op("act", lambda: nc.scalar.activation(out=spf[i][:T, 0:J], in_=ef[i][:T, 0:J], func=AF.Ln, bias=1.0),
                     reads=[b_ef[i]], writes=[b_spf[i]])
                if m_ap is not None:
                    k.op("pool", lambda: nc.gpsimd.tensor_tensor(out=spf[i][:T, 0:J], in0=spf[i][:T, 0:J], in1=m_ap, op=ALU.mult),
                         reads=[b_spf[i], bc], writes=[b_spf[i]])
                k.op("dve", lambda: nc.vector.tensor_tensor(out=t1[i][:T, 0:J], in0=pS[i][:T, 0:J], in1=spf[i][:T, 0:J], op=ALU.subtract),
                     reads=[bpS[i], b_spf[i]], writes=[b_t1[i]])

            def stage_b(idx):
                k_ap, vs, J, m_ap, n_ap, rds = tiles[idx]
                i = (base + idx) % 2
                ci = (c_first + idx) % 2
                cn = 1 - ci
                k.op("dve", lambda: nc.vector.tensor_tensor_scan(out=t2[i][:T, 0:J], data0=ones_w[:T, 0:J], data1=spf[i][:T, 0:J],
                                                                 initial=0.0, op0=ALU.mult, op1=ALU.add),
                     reads=[b_spf[i], bc], writes=[b_t2[i]])
                k.op("dve", lambda: nc.vector.tensor_tensor(out=carry[cn][:T, 0:1], in0=carry[ci][:T, 0:1], in1=t2[i][:T, J - 1:J],
                                                           op=ALU.subtract), reads=[b_carry[ci], b_t2[i]], writes=[b_carry[cn]])
                k.op("dve", lambda: nc.vector.tensor_tensor(out=t2[i][:T, 0:J], in0=t2[i][:T, 0:J], in1=t1[i][:T, 0:J], op=ALU.add),
                     reads=[b_t2[i], b_t1[i]], writes=[b_t2[i]])
                if n_ap is not None:
                    k.op("pool", lambda: nc.gpsimd.tensor_tensor(out=t2[i][:T, 0:J], in0=t2[i][:T, 0:J], in1=n_ap, op=ALU.add),
                         reads=[b_t2[i], bc], writes=[b_t2[i]])
                k.op("act", lambda: nc.scalar.activation(out=wT[i][:T, 0:J], in_=t2[i][:T, 0:J], func=AF.Exp,
                                                         bias=carry[cn][:T, 0:1]),
                     reads=[b_t2[i], b_carry[cn]], writes=[b_wT[i]])
                for jb, (v_ap, jn) in enumerate(vs):
                    k.op("pe", lambda: nc.tensor.transpose(out=pT[i][:jn, jb * 128:jb * 128 + T], in_=wT[i][:T, jb * 128:jb * 128 + jn],
                                                           identity=ident[:T, :T]),
                         reads=[b_wT[i], b_id], writes=[bpT[i]])
                jn0 = vs[0][1]
                nb_ = len(vs)
                if i == 0:
                    k.op("act", lambda: nc.scalar.copy(out=wTt[i][:jn0, 0:nb_ * 128], in_=pT[i][:jn0, 0:nb_ * 128]),
                         reads=[bpT[i]], writes=[b_wTt[i]])
                else:
                    k.op("dve", lambda: nc.vector.tensor_copy(out=wTt[i][:jn0, 0:nb_ * 128], in_=pT[i][:jn0, 0:nb_ * 128]),
                         reads=[bpT[i]], writes=[b_wTt[i]])
                for jb, (v_ap, jn) in enumerate(vs):
                    first = (idx == 0 and jb == 0)
                    last = (idx == nt_ - 1 and jb == nb_ - 1)
                    k.op("pe", lambda: nc.tensor.matmul(pO[:, 0:T], lhsT=v_ap, rhs=wTt[i][:jn, jb * 128:jb * 128 + T],
                                                        start=first, stop=last),
                         reads=rds + [b_wTt[i]], writes=[bpO])

            stage_a(0)
            for idx in range(nt_):
                if idx + 1 < nt_:
                    stage_a(idx + 1)
                stage_b(idx)

        def emit_og(hh, N, col0, sz_ap):
            j = oc[0] % 2
            oc[0] += 1
            k.op("dve", lambda: nc.vector.tensor_tensor(out=ogf[j][:, 0:N], in0=pO[:, 0:N], in1=sz_ap, op=ALU.mult),
                 reads=[bpO, b_szT], writes=[b_ogf[j]])
            k.dma("sp", ogT[hh * 128:(hh + 1) * 128, col0:col0 + N], ogf[j][:, 0:N], reads=[b_ogf[j]], writes=[b_out])

        nblk = NTOK // 512
        for hh in range(dbg_heads):
            for c4 in range(4):
                k.dma("poolq", W[:, c4 * 4:(c4 + 1) * 4, :], w_att[hh, c4 * 512:(c4 + 1) * 512, :].rearrange("(kt p) c -> p kt c", p=128),
                      writes=[b_W])
            for bi in (range(nblk) if dbg_blocks is None else dbg_blocks):
                c0 = bi * 512
                samp = bi >= SEQ // 512
                for c4 in range(4):
                    k.dma("sp", xTb[:, c4 * 4:(c4 + 1) * 4, :],
                          xT[c4 * 512:(c4 + 1) * 512, c0:c0 + 512].rearrange("(kt p) t -> p kt t", p=128), writes=[b_xTb])
                for kt in range(16):
                    j = kt % 2
                    k.op("act", lambda: nc.scalar.activation(out=sqb[j][:], in_=xTb[:, kt, :], func=AF.Square),
                         reads=[b_xTb], writes=[b_sqb[j]])
                    k.op("pe", lambda: nc.tensor.matmul(pQ[:, :], lhsT=ones_b[:], rhs=sqb[j][:], start=(kt == 0), stop=(kt == 15)),
                         reads=[b_sqb[j], bc], writes=[bpQ])
                k.op("dve", lambda: nc.vector.tensor_scalar(out=rstd[:], in0=pQ[:, :], scalar1=1.0 / D, scalar2=EPS,
                                                           op0=ALU.mult, op1=ALU.add), reads=[bpQ], writes=[b_rstd])
                k.op("act", lambda: nc.scalar.activation(out=rstd[:], in_=rstd[:], func=AF.Sqrt), reads=[b_rstd], writes=[b_rstd])
                k.op("dve", lambda: nc.vector.reciprocal(out=rstd[:], in_=rstd[:]), reads=[b_rstd], writes=[b_rstd])
                for kt in range(16):
                    k.op("dve", lambda: nc.vector.scalar_tensor_tensor(out=hT[:, kt, :], in0=xTb[:, kt, :], scalar=nwT[:, kt:kt + 1],
                                                                       in1=rstd[:], op0=ALU.mult, op1=ALU.mult),
                         reads=[b_xTb, b_rstd, bc], writes=[b_hT])
                import os
                LVL = int(os.environ.get("ATT_LVL", "9"))
                if LVL < 2:
                    continue
                for kt in range(16):
                    k.op("pe", lambda: nc.tensor.matmul(pQ[:, :], lhsT=W[:, kt, 0:128], rhs=hT[:, kt, :], start=(kt == 0), stop=(kt == 15)),
                         reads=[b_W, b_hT], writes=[bpQ], sig=(kt == 15))
                k.op("act", lambda: nc.scalar.activation(out=qT[:], in_=pQ[:, :], func=AF.Identity, scale=SCALE), reads=[bpQ], writes=[b_qT])
                if LVL < 3:
                    continue
                for kt in range(16):
                    k.op("pe", lambda: nc.tensor.matmul(pV[:, :], lhsT=W[:, kt, 128:256], rhs=hT[:, kt, :], start=(kt == 0), stop=(kt == 15)),
                         reads=[b_W, b_hT], writes=[bpV], sig=(kt == 15))
                if samp:
                    k.op("act", lambda: nc.scalar.copy(out=kT_s[:], in_=pV[:, :]), reads=[bpV], writes=[b_kTs])
                else:
                    k.op("act", lambda: nc.scalar.copy(out=kT_all[:, c0:c0 + 512], in_=pV[:, :]), reads=[bpV], writes=[b_kT])
                k.op("dve", lambda: nc.vector.tensor_copy(out=kf[:], in_=pV[:, :]), reads=[bpV], writes=[b_kf])
                if os.environ.get("ATT_NOKDMA") is None:
                    k.dma("sp", kTo[hh, :, c0:c0 + 512], kf[:], reads=[b_kf], writes=[b_out])
                if LVL < 4:
                    continue
                for kt in range(16):
                    k.op("pe", lambda: nc.tensor.matmul(pQ[:, :], lhsT=W[:, kt, 384:512], rhs=hT[:, kt, :], start=(kt == 0), stop=(kt == 15)),
                         reads=[b_W, b_hT], writes=[bpQ], sig=(kt == 15))
                k.op("act", lambda: nc.scalar.activation(out=szT[:], in_=pQ[:, :], func=AF.Silu), reads=[bpQ], writes=[b_szT])
                if LVL < 5:
                    continue
                if not samp:
                    for ti in range(4):
                        j = ti % 2
                        for kt in range(16):
                            k.op("pe", lambda: nc.tensor.matmul(pV[:, 0:128], lhsT=hT[:, kt, ti * 128:(ti + 1) * 128], rhs=W[:, kt, 256:384],
                                                                start=(kt == 0), stop=(kt == 15)),
                                 reads=[b_W, b_hT], writes=[bpV], sig=(kt == 15))
                        k.op("act", lambda: nc.scalar.copy(out=v_all[:, bi * 4 + ti, :], in_=pV[:, 0:128]), reads=[bpV], writes=[b_v])
                        k.op("dve", lambda: nc.vector.tensor_copy(out=vf[j][:], in_=pV[:, 0:128]), reads=[bpV], writes=[b_vf[j]])
                        k.dma("sp", vo[hh, c0 + ti * 128:c0 + (ti + 1) * 128, :], vf[j][:], reads=[b_vf[j]], writes=[b_out])
                    for qi in range(4):
                        tiles = []
                        for g in range(bi, -1, -1):
                            vs = [(v_all[:, g * 4 + jb, :], 128) for jb in range(4)]
                            if g == bi:
                                tiles.append((kT_all[:, g * 512:(g + 1) * 512], vs, 512, m01[:, qi, :], negm[:, qi, :], [b_kT, b_v]))
                            else:
                                tiles.append((kT_all[:, g * 512:(g + 1) * 512], vs, 512, None, None, [b_kT, b_v]))
                        if dbg_attend:
                            attend(qT[:, qi * 128:(qi + 1) * 128], 128, tiles)
                            emit_og(hh, 128, c0 + qi * 128, szT[:, qi * 128:(qi + 1) * 128])
                else:
                    for b in range(NSB):
                        j = b % 2
                        for kt in range(16):
                            k.op("pe", lambda: nc.tensor.matmul(pV[:16, 0:128], lhsT=hT[:, kt, b * 16:(b + 1) * 16], rhs=W[:, kt, 256:384],
                                                                start=(kt == 0), stop=(kt == 15)),
                                 reads=[b_W, b_hT], writes=[bpV], sig=(kt == 15))
                        k.op("act", lambda: nc.scalar.copy(out=v_s[:, :], in_=pV[:16, 0:128]), reads=[bpV], writes=[b_vs])
                        k.op("dve", lambda: nc.vector.tensor_copy(out=vf[j][:16, :], in_=pV[:16, 0:128]), reads=[bpV], writes=[b_vf[j]])
                        k.dma("sp", vo[hh, c0 + b * 16:c0 + (b + 1) * 16, :], vf[j][:16, :], reads=[b_vf[j]], writes=[b_out])
                        k.dma("poolq", kc[:], ckT[b, hh, :, :], writes=[b_kc])
                        k.dma("poolq", vc[:], cv_[b, hh, :, :].rearrange("(a p) d -> p a d", p=128), writes=[b_vc])
                        tiles = [(kT_s[:, b * 16:(b + 1) * 16], [(v_s[:, :], 16)], 16, m01[:16, 0, 0:16], negm[:16, 0, 0:16], [b_kTs, b_vs])]
                        for g in range(PAST // 512 - 1, -1, -1):
                            tiles.append((kc[:, g * 512:(g + 1) * 512], [(vc[:, g * 4 + jb, :], 128) for jb in range(4)], 512,
                                          None, None, [b_kc, b_vc]))
                        if dbg_attend:
                            attend(qT[:, b * 16:(b + 1) * 16], 16, tiles)
                            emit_og(hh, 16, c0 + b * 16, szT[:, b * 16:(b + 1) * 16])
        k.finish()
    return nc


def build_oproj(KD):
    nc = bass.Bass("TRN2", target_bir_lowering=False)
    nk = KD // 128
    aT = nc.dram_tensor("aT", [KD, TT], F32, kind="ExternalInput").ap()
    w = nc.dram_tensor("w", [KD, D], F32, kind="ExternalInput").ap()
    x = nc.dram_tensor("x", [TT, D], F32, kind="ExternalInput").ap()
    xo = nc.dram_tensor("xo", [TT, D], F32, kind="ExternalOutput").ap()
    with ExitStack() as st:
        k = K(nc, st)
        pp = [k.ps(f"pp{i}", [128, 512], F32) for i in range(2)]
        bpp = [k.pbuf() for _ in range(2)]
        yT = k.sb("yT", [128, nk, 512], BF16)
        wo = [k.sb(f"wo{i}", [128, nk, 256], BF16) for i in range(2)]
        xc = [k.sb(f"xc{i}", [128, 256], F32) for i in range(2)]
        xn = [k.sb(f"xn{i}", [128, 256], F32) for i in range(2)]
        b_yT = k.buf()
        b_wo = [k.buf() for _ in range(2)]
        b_xc = [k.buf() for _ in range(2)]
        b_xn = [k.buf() for _ in range(2)]
        b_out = k.buf()
        cnt = [0, 0]
        blocks = [(i * 512, 512) for i in range(TP // 512)] + [(TP, TS)]
        for (t0, nt) in blocks:
            ntile = max(1, nt // 128)
            P = min(nt, 128)
            for c8 in range(nk // 8):
                k.dma("poolq", yT[:, c8 * 8:(c8 + 1) * 8, 0:nt],
                      aT[c8 * 1024:(c8 + 1) * 1024, t0:t0 + nt].rearrange("(kt p) t -> p kt t", p=128), writes=[b_yT])
            for oc in range(8):
                i = cnt[0] % 2
                cnt[0] += 1
                for c8 in range(nk // 8):
                    k.dma("poolq", wo[i][:, c8 * 8:(c8 + 1) * 8, :],
                          w[c8 * 1024:(c8 + 1) * 1024, oc * 256:(oc + 1) * 256].rearrange("(kt p) c -> p kt c", p=128),
                          writes=[b_wo[i]])
                for ti in range(ntile):
                    r0 = t0 + ti * 128
                    j = cnt[1] % 2
                    cnt[1] += 1
                    k.dma("sp", xc[j][:P, :], x[r0:r0 + P, oc * 256:(oc + 1) * 256], writes=[b_xc[j]])
                    for ft in range(nk):
                        k.op("pe", lambda: nc.tensor.matmul(pp[j][:P, 0:256], lhsT=yT[:, ft, ti * 128:ti * 128 + P],
                                                            rhs=wo[i][:, ft, :], start=(ft == 0), stop=(ft == nk - 1)),
                             reads=[b_yT, b_wo[i]], writes=[bpp[j]], sig=(ft == nk - 1))
                    k.op("dve", lambda: nc.vector.tensor_tensor(out=xn[j][:P, :], in0=pp[j][:P, 0:256], in1=xc[j][:P, :],
                                                               op=ALU.add), reads=[bpp[j], b_xc[j]], writes=[b_xn[j]])
                    k.dma("sp", xo[r0:r0 + P, oc * 256:(oc + 1) * 256], xn[j][:P, :], reads=[b_xn[j]], writes=[b_out])
        k.finish()
    return nc


def _unshard_tok(res, key):
    xp = np.concatenate([r[key][:TP] for r in res], axis=0)
    xs = np.concatenate([r[key][TP:] for r in res], axis=0)
    return xp, xs


def _featmajor_all(xp, xs):
    return np.ascontiguousarray(np.concatenate([xp, xs.reshape(NSB * NST, -1)], axis=0).T)


def kernel(x_prompt, x_sample, state_ssm, state_conv, cache_k, cache_v, norm_w, final_norm_w,
           a_w_in, a_ln_g, a_ln_b, a_w_s, a_b_s, a_w_out,
           b_w_in, b_conv_w, b_conv_b, b_dt_bias, b_a_log, b_d, b_norm_w, b_w_out,
           c_w_in, c_w_out):
    f = lambda a: np.ascontiguousarray(np.asarray(a, dtype=np.float32))
    xp = f(x_prompt)[0]
    xs = f(x_sample)
    norm_w = f(norm_w)
    res = run_gmlp(xp, xs, norm_w[0], f(a_w_in)[0], f(a_ln_g)[0], f(a_ln_b)[0], f(a_w_s)[0], f(a_b_s)[0], f(a_w_out)[0])
    x1p, x1s = _unshard_tok(res, "xo")
    v0 = np.concatenate([r["vsT"].T for r in res], axis=0).reshape(NSB, NST, AW)
    xT1 = _featmajor_all(x1p, x1s)
    bw = f(b_w_in)[0]
    cw_ = f(b_conv_w)[0]
    cb_ = f(b_conv_b)[0]
    sst = f(state_ssm)[0]
    scv = f(state_conv)[0]
    nwT1 = np.ascontiguousarray(norm_w[1].reshape(16, 128).T)
    in_maps = []
    for g in range(NCORES):
        xcols = np.arange(4096 + g * 512, 4096 + (g + 1) * 512)
        bcols = np.arange(4096 + 4096 + g * 128, 4096 + 4096 + (g + 1) * 128)
        ccols = np.arange(4096 + 4096 + 1024 + g * 128, 4096 + 4096 + 1024 + (g + 1) * 128)
        cols = np.concatenate([xcols, bcols, ccols])
        cch = cols - 4096
        hs = slice(g * 8, (g + 1) * 8)
        m = {
            "xT": xT1, "normwT": nwT1,
            "w_z": np.ascontiguousarray(bw[:, g * 512:(g + 1) * 512]),
            "w_xbc": np.ascontiguousarray(bw[:, cols]),
            "w_dt": np.ascontiguousarray(bw[:, 4096 + 6144 + g * 8:4096 + 6144 + (g + 1) * 8]),
            "convw": np.ascontiguousarray(cw_[:, cch].T.reshape(6, 128, 4).transpose(1, 0, 2)),
            "convb": np.ascontiguousarray(cb_[cch].reshape(6, 128).T),
            "hp": np.ascontiguousarray(np.stack([f(b_dt_bias)[0][hs], f(b_a_log)[0][hs], f(b_d)[0][hs]])),
            "dexp": np.ascontiguousarray(np.repeat(f(b_d)[0][hs], 64)),
            "nw": np.ascontiguousarray(f(b_norm_w)[0][g * 512:(g + 1) * 512]),
            "sT_in": np.ascontiguousarray(sst[:, hs].reshape(NSB, 512, 128).transpose(0, 2, 1)),
            "cs_in": np.ascontiguousarray(scv[:, :, cch].transpose(2, 0, 1).reshape(6, 128, NSB, 3).transpose(1, 0, 2, 3)),
        }
        in_maps.append(m)
    resm = _run(build_mamba(), in_maps)
    ssm_p = np.zeros((1, 1, 64, 64, 128), np.float32)
    conv_p = np.zeros((1, 1, 3, 6144), np.float32)
    ssm_s = np.zeros((1, NSB, 64, 64, 128), np.float32)
    conv_s = np.zeros((1, NSB, 3, 6144), np.float32)
    yn_all = np.zeros((NTOK, 4096), np.float32)
    for g in range(NCORES):
        r = resm[g]
        xcols = np.arange(g * 512, (g + 1) * 512)
        bcols = np.arange(4096 + g * 128, 4096 + (g + 1) * 128)
        ccols = np.arange(4096 + 1024 + g * 128, 4096 + 1024 + (g + 1) * 128)
        cch = np.concatenate([xcols, bcols, ccols])
        ssm_p[0, 0, g * 8:(g + 1) * 8] = r["sT_p"].T.reshape(8, 64, 128)
        conv_p[0, 0][:, cch] = r["cs_p"].transpose(1, 0, 2).reshape(768, 3).T
        ssm_s[0, :, g * 8:(g + 1) * 8] = r["sT_s"].transpose(0, 2, 1).reshape(NSB, 8, 64, 128)
        conv_s[0][:, :, cch] = r["cs_s"].transpose(1, 0, 2, 3).reshape(768, NSB, 3).transpose(1, 2, 0)
        yn_all[:, g * 512:(g + 1) * 512] = r["yn"]
    ynp, yns = yn_all[:SEQ], yn_all[SEQ:].reshape(NSB, NST, 4096)
    in_maps = [{"aT": np.ascontiguousarray(tok_shard(ynp, yns, c).T), "w": f(b_w_out)[0], "x": tok_shard(x1p, x1s.reshape(NSB, NST, D), c)}
               for c in range(NCORES)]
    res = _run(build_oproj(4096), in_maps)
    x2p, x2s = _unshard_tok(res, "xo")
    xT2 = _featmajor_all(x2p, x2s)
    nwT2 = np.ascontiguousarray(norm_w[2].reshape(16, 128).T)
    cwi = f(c_w_in)[0]
    ck = f(cache_k)[0]
    cvv = f(cache_v)[0]
    in_maps = []
    for c in range(NCORES):
        wh = []
        for hh in range(2):
            h = 2 * c + hh
            wh.append(np.concatenate([cwi[:, part * 2048 + h * 128: part * 2048 + (h + 1) * 128] for part in range(4)], axis=1))
        in_maps.append({"xT": xT2, "normwT": nwT2, "w_att": np.ascontiguousarray(np.stack(wh)),
                        "ckT": np.ascontiguousarray(ck[:, 2 * c:2 * c + 2].transpose(0, 1, 3, 2)),
                        "cv": np.ascontiguousarray(cvv[:, 2 * c:2 * c + 2])})
    resa = _run(build_attn(), in_maps)
    k_all = np.zeros((16, NTOK, 128), np.float32)
    v_all = np.zeros((16, NTOK, 128), np.float32)
    og_all = np.zeros((NTOK, 2048), np.float32)
    for c in range(NCORES):
        r = resa[c]
        k_all[2 * c:2 * c + 2] = r["kTo"].transpose(0, 2, 1)
        v_all[2 * c:2 * c + 2] = r["vo"]
        og_all[:, c * 256:(c + 1) * 256] = r["ogT"].T
    k_p = k_all[:, :SEQ][None, None]
    v_p = v_all[:, :SEQ][None, None]
    k_s = np.ascontiguousarray(k_all[:, SEQ:].reshape(16, NSB, NST, 128).transpose(1, 0, 2, 3))[None]
    v_s = np.ascontiguousarray(v_all[:, SEQ:].reshape(16, NSB, NST, 128).transpose(1, 0, 2, 3))[None]
    ogp, ogs = og_all[:SEQ], og_all[SEQ:].reshape(NSB, NST, 2048)
    in_maps = [{"aT": np.ascontiguousarray(tok_shard(ogp, ogs, c).T), "w": f(c_w_out)[0], "x": tok_shard(x2p, x2s.reshape(NSB, NST, D), c)}
               for c in range(NCORES)]
    res = _run(build_oproj(2048), in_maps)
    x3p, x3s = _unshard_tok(res, "xo")
    res = run_gmlp(x3p, x3s.reshape(NSB, NST, D), norm_w[3], f(a_w_in)[1], f(a_ln_g)[1], f(a_ln_b)[1], f(a_w_s)[1], f(a_b_s)[1],
                   f(a_w_out)[1], f(final_norm_w))
    yp, ys = _unshard_tok(res, "yo")
    v1 = np.concatenate([r["vsT"].T for r in res], axis=0).reshape(NSB, NST, AW)
    return (np.ascontiguousarray(yp)[None], np.ascontiguousarray(ys).reshape(NSB, NST, D),
            np.ascontiguousarray(np.stack([v0, v1])), ssm_p, conv_p, ssm_s, conv_s,
            np.ascontiguousarray(k_p), np.ascontiguousarray(v_p), k_s, v_s)
```

```python
import numpy as np
from contextlib import ExitStack
import concourse.bass as bass
import concourse.mybir as mybir
from concourse.bass_utils import run_bass_kernel_spmd

F32 = mybir.dt.float32
BF16 = mybir.dt.bfloat16
AF = mybir.ActivationFunctionType
ALU = mybir.AluOpType
AX = mybir.AxisListType

NCORES = 8
D = 2048
SEQ = 16384
NSB = 32
NST = 16
TP = SEQ // NCORES
TS = NSB * NST // NCORES
TT = TP + TS
AW = 4096
EPS = 1e-6
DMA_RING = 12
EPOCH = 6000


class Buf:
    __slots__ = ("name", "w", "r", "excl")

    def __init__(self, name, excl=False):
        self.name = name
        self.w = None
        self.r = []
        self.excl = excl


class K:
    def __init__(self, nc, stack):
        self.nc = nc
        self.eng = {"pe": nc.tensor, "act": nc.scalar, "dve": nc.vector,
                    "pool": nc.gpsimd, "sp": nc.sync}
        self.sems = {}
        self.cnt = {}
        self.stack = stack
        self.epoch = {}
        for e in ("pe", "act", "dve", "pool"):
            self.sems[(e, 0)] = stack.enter_context(nc.semaphore("s_" + e + "_0"))
            self.cnt[e] = 0
            self.epoch[e] = 0
        self.dq = {}
        for q, e in (("sp", "sp"), ("poolq", "pool")):
            ring = [stack.enter_context(nc.semaphore(f"d_{q}_{i}")) for i in range(DMA_RING)]
            self.dq[q] = {"eng": e, "ring": ring, "n": 0}
            for i in range(DMA_RING):
                self.sems[(q, i)] = ring[i]
        self.waited = {}
        self.pending = {e: [] for e in ("pe", "act", "dve", "pool")}
        self.nb = 0
        self.ninstr = 0
        self.allbufs = []

    def sb(self, name, shape, dt):
        return self.nc.alloc_sbuf_tensor(name, list(shape), dt)

    def ps(self, name, shape, dt=F32):
        return self.nc.alloc_psum_tensor(name, list(shape), dt)

    def buf(self, name=None, excl=False):
        self.nb += 1
        b = Buf(name or f"b{self.nb}", excl)
        self.allbufs.append(b)
        return b

    def pbuf(self, name=None):
        return self.buf(name, excl=True)

    def _wait(self, engkey, ev):
        semkey, val, src = ev
        if src == engkey and engkey == "pe":
            return
        kk = (engkey, semkey)
        if self.waited.get(kk, 0) >= val:
            return
        self.waited[kk] = val
        self.eng[engkey].wait_ge(self.sems[semkey], val)
        self.ninstr += 1

    def _deps(self, engkey, reads, writes):
        for b in reads:
            if b.w is not None:
                self._wait(engkey, b.w)
            if b.excl:
                for ev in b.r:
                    if ev[2] != engkey:
                        self._wait(engkey, ev)
        for b in writes:
            if b.w is not None:
                self._wait(engkey, b.w)
            for ev in b.r:
                self._wait(engkey, ev)

    def _record(self, ev, reads, writes):
        for b in reads:
            b.r.append(ev)
            if len(b.r) > 16:
                d = {}
                for e in b.r:
                    if e[0] not in d or d[e[0]][1] < e[1]:
                        d[e[0]] = e
                b.r = list(d.values())
        for b in writes:
            b.w = ev
            b.r = []

    def op(self, engkey, fn, reads=(), writes=(), sig=True):
        reads = [b for b in reads if b is not None]
        writes = [b for b in writes if b is not None]
        self._deps(engkey, reads, writes)
        ins = fn()
        self.ninstr += 1
        if sig:
            if self.cnt[engkey] >= EPOCH:
                self.epoch[engkey] += 1
                self.cnt[engkey] = 0
                self.sems[(engkey, self.epoch[engkey])] = self.stack.enter_context(
                    self.nc.semaphore(f"s_{engkey}_{self.epoch[engkey]}"))
            self.cnt[engkey] += 1
            sk = (engkey, self.epoch[engkey])
            ins.then_inc(self.sems[sk], 1)
            ev = (sk, self.cnt[engkey], engkey)
            pr = self.pending[engkey]
            if pr:
                for b in pr:
                    b.r.append(ev)
                self.pending[engkey] = []
            self._record(ev, reads, writes)
        else:
            self.pending[engkey].extend(reads)
            self.pending[engkey].extend(writes)
        return ins

    def dma(self, q, out, in_, reads=(), writes=(), **kw):
        reads = [b for b in reads if b is not None]
        writes = [b for b in writes if b is not None]
        d = self.dq[q]
        engkey = d["eng"]
        i = d["n"]
        slot = i % DMA_RING
        rnd = i // DMA_RING
        if rnd > 0:
            self._wait(engkey, ((q, slot), 16 * rnd, q))
        self._deps(engkey, reads, writes)
        ins = self.eng[engkey].dma_start(out=out, in_=in_, **kw)
        ins.then_inc(d["ring"][slot], 16)
        self.ninstr += 1
        d["n"] = i + 1
        ev = ((q, slot), 16 * (rnd + 1), q)
        self._record(ev, reads, writes)
        return ev

    def finish(self):
        for b in self.allbufs:
            if b.w is not None:
                self._wait("sp", b.w)
            for ev in b.r:
                self._wait("sp", ev)
        for q, d in self.dq.items():
            n = d["n"]
            for i in range(max(0, n - DMA_RING), n):
                self._wait("sp", ((q, i % DMA_RING), 16 * (i // DMA_RING + 1), q))


def make_ident(k, nc):
    identf = k.sb("identf", [128, 128], F32)
    ident = k.sb("identb", [128, 128], BF16)
    bi = k.buf("ident")
    k.op("pool", lambda: nc.gpsimd.memset(identf[:], 0.0), writes=[bi])
    k.op("pool", lambda: nc.gpsimd.affine_select(out=identf[:], in_=identf[:], pattern=[[-1, 128]],
                                                 compare_op=ALU.not_equal, fill=1.0, base=0,
                                                 channel_multiplier=1), reads=[bi], writes=[bi])
    k.op("dve", lambda: nc.vector.tensor_copy(out=ident[:], in_=identf[:]), reads=[bi], writes=[bi])
    return identf, ident, bi


def rstd_from_sumsq(k, nc, out, ssq, n, rb, wb, np_=128):
    k.op("dve", lambda: nc.vector.tensor_scalar(out=out, in0=ssq, scalar1=1.0 / n, scalar2=EPS,
                                               op0=ALU.mult, op1=ALU.add), reads=rb, writes=wb)
    k.op("act", lambda: nc.scalar.activation(out=out, in_=out, func=AF.Sqrt), reads=wb, writes=wb)
    k.op("dve", lambda: nc.vector.reciprocal(out=out, in_=out), reads=wb, writes=wb)


def build_gmlp(final_norm, pre_oproj=False):
    nc = bass.Bass("TRN2", target_bir_lowering=False)
    x = nc.dram_tensor("x", [TT, D], F32, kind="ExternalInput").ap()
    if pre_oproj:
        aT_in = nc.dram_tensor("aT", [D, TT], F32, kind="ExternalInput").ap()
        w_o = nc.dram_tensor("w_o", [D, D], F32, kind="ExternalInput").ap()
        x3 = nc.dram_tensor("x3_scratch", [TT, D], F32).ap()
    normw = nc.dram_tensor("normw", [D], F32, kind="ExternalInput").ap()
    w_in = nc.dram_tensor("w_in", [D, 3 * AW], F32, kind="ExternalInput").ap()
    ln_g = nc.dram_tensor("ln_g", [AW], F32, kind="ExternalInput").ap()
    ln_b = nc.dram_tensor("ln_b", [AW], F32, kind="ExternalInput").ap()
    w_s = nc.dram_tensor("w_s", [16, 128, 128], F32, kind="ExternalInput").ap()
    b_s = nc.dram_tensor("b_s", [16, 128], F32, kind="ExternalInput").ap()
    w_out = nc.dram_tensor("w_out", [AW, D], F32, kind="ExternalInput").ap()
    xo = nc.dram_tensor("xo", [TT, D], F32, kind="ExternalOutput").ap()
    vsT = nc.dram_tensor("vsT", [AW, TS], F32, kind="ExternalOutput").ap()
    if final_norm:
        fnw = nc.dram_tensor("fnw", [D], F32, kind="ExternalInput").ap()
        yo = nc.dram_tensor("yo", [TT, D], F32, kind="ExternalOutput").ap()

    with ExitStack() as st:
        k = K(nc, st)
        identf, ident, b_id = make_ident(k, nc)
        normw_bc = k.sb("normw_bc", [128, D], F32)
        b_const = k.buf("const")
        k.dma("sp", normw_bc[:], normw.partition_broadcast(128), writes=[b_const])
        if final_norm:
            fnw_bc = k.sb("fnw_bc", [128, D], F32)
            k.dma("sp", fnw_bc[:], fnw.partition_broadcast(128), writes=[b_const])
        ones_b = k.sb("ones_b", [128, 128], BF16)
        k.op("dve", lambda: nc.vector.memset(ones_b[:], 1.0), writes=[b_const])
        lnrow = k.sb("lnrow", [32, 2, 128], F32)
        k.dma("sp", lnrow[:, 0, :], ln_g.rearrange("(a p) -> a p", p=128), writes=[b_const])
        k.dma("sp", lnrow[:, 1, :], ln_b.rearrange("(a p) -> a p", p=128), writes=[b_const])
        lnT = k.sb("lnT", [128, 2, 32], F32)
        pA = k.ps("pA", [128, 512], F32)
        pB = k.ps("pB", [128, 512], F32)
        pC = k.ps("pC", [128, 512], F32)
        pD = k.ps("pD", [128, 512], F32)
        pT = k.ps("pT", [128, 1024], BF16)
        bpA, bpB, bpC, bpD, bpT = (k.pbuf(n) for n in ("pA", "pB", "pC", "pD", "pT"))
        for j in range(2):
            k.op("pe", lambda: nc.tensor.transpose(out=pA[:, j * 32:(j + 1) * 32], in_=lnrow[:, j, :],
                                                   identity=identf[0:32, 0:32]),
                 reads=[b_const, b_id], writes=[bpA])
        k.op("dve", lambda: nc.vector.tensor_copy(out=lnT[:].rearrange("p a b -> p (a b)"), in_=pA[:, 0:64]),
             reads=[bpA], writes=[b_const])
        wnat = [k.sb(f"wnat{i}", [128, 128], F32) for i in range(2)]
        b_wnat = [k.buf() for _ in range(2)]
        wposT = k.sb("wposT", [128, 16, 128], BF16)
        for g in range(16):
            pp = pA if g % 2 == 0 else pB
            bpp = bpA if g % 2 == 0 else bpB
            k.dma("sp", wnat[g % 2][:], w_s[g, :, :], writes=[b_wnat[g % 2]])
            k.op("pe", lambda: nc.tensor.transpose(out=pp[:, 0:128], in_=wnat[g % 2][:], identity=identf[:]),
                 reads=[b_wnat[g % 2], b_id], writes=[bpp])
            k.op("act", lambda: nc.scalar.copy(out=wposT[:, g, :], in_=pp[:, 0:128]), reads=[bpp], writes=[b_const])
        k.op("dve", lambda: nc.vector.memset(wposT[64:128, :, 0:64], 0.0), reads=[b_const], writes=[b_const])
        wposT_s = k.sb("wposT_s", [64, 16, 64], BF16)
        for g in range(16):
            pp = pA if g % 2 == 0 else pB
            bpp = bpA if g % 2 == 0 else bpB
            k.op("dve", lambda: nc.vector.memset(wnat[g % 2][:], 0.0), writes=[b_wnat[g % 2]])
            for b in range(4):
                k.dma("sp", wnat[g % 2][16 * b:16 * b + 16, 16 * b:16 * b + 16], w_s[g, 0:16, 0:16],
                      writes=[b_wnat[g % 2]])
            k.op("pe", lambda: nc.tensor.transpose(out=pp[0:64, 0:64], in_=wnat[g % 2][0:64, 0:64], identity=identf[0:64, 0:64]),
                 reads=[b_wnat[g % 2], b_id], writes=[bpp])
            k.op("act", lambda: nc.scalar.copy(out=wposT_s[:, g, :], in_=pp[0:64, 0:64]), reads=[bpp], writes=[b_const])
        bsbt = [k.sb(f"bsb{i}", [128, 128], F32) for i in range(2)]
        b_bsb = [k.buf() for _ in range(2)]
        biasT = k.sb("biasT", [128, 32, 128], F32)
        biasT_s = k.sb("biasT_s", [128, 32, 64], F32)
        for g in range(16):
            k.dma("sp", bsbt[g % 2][:], b_s[g, :].partition_broadcast(128), writes=[b_bsb[g % 2]])
            k.op("pe", lambda: nc.tensor.matmul(pC[:, 0:128], lhsT=ones_b[:], rhs=wposT[:, g, :], start=True, stop=True),
                 reads=[b_const], writes=[bpC])
            k.op("pe", lambda: nc.tensor.matmul(pC[:, 128:192], lhsT=ones_b[0:64, :], rhs=wposT_s[:, g, :], start=True, stop=True),
                 reads=[b_const], writes=[bpC])
            for j in range(2):
                ft = 2 * g + j
                k.op("dve", lambda: nc.vector.scalar_tensor_tensor(out=biasT[:, ft, :], in0=pC[:, 0:128],
                                                                   scalar=lnT[:, 1, ft:ft + 1], in1=bsbt[g % 2][:],
                                                                   op0=ALU.mult, op1=ALU.add),
                     reads=[bpC, b_const, b_bsb[g % 2]], writes=[b_const])
                for b in range(4):
                    k.op("dve", lambda: nc.vector.scalar_tensor_tensor(out=biasT_s[:, ft, 16 * b:16 * b + 16],
                                                                       in0=pC[:, 128 + 16 * b:128 + 16 * b + 16],
                                                                       scalar=lnT[:, 1, ft:ft + 1], in1=bsbt[g % 2][:, 0:16],
                                                                       op0=ALU.mult, op1=ALU.add),
                         reads=[bpC, b_const, b_bsb[g % 2]], writes=[b_const])

        TB = 512
        xt = k.sb("xt", [128, D], F32)
        hb = k.sb("hb", [128, D], BF16)
        hT = k.sb("hT", [128, 16, TB], BF16)
        gv = k.sb("gv", [128, 4, AW], BF16)
        yT = k.sb("yT", [128, 32, TB], BF16)
        wsl = [k.sb(f"wsl{i}", [128, 16, 512], BF16) for i in range(2)]
        wo = [k.sb(f"wo{i}", [128, 32, 256], BF16) for i in range(1)] * 2
        gvf = [k.sb(f"gvf{i}", [128, 512], F32) for i in range(2)]
        sqf = [k.sb(f"sqf{i}", [128, 512], F32) for i in range(2)]
        guf = gvf
        szf = sqf
        ssf = [k.sb(f"ssf{i}", [128, 512], F32) for i in range(2)]
        stats = k.sb("stats", [128, 4, 2, 8], F32)
        st2 = k.sb("st2", [128, 4, 8], F32)
        small = k.sb("small", [128, 8], F32)
        xc = [k.sb(f"xc{i}", [128, 256], F32) for i in range(2)]
        xn = [k.sb(f"xn{i}", [128, 256], F32) for i in range(2)]
        b_xt, b_hb, b_hT, b_gv, b_yT, b_stats, b_small = (k.buf(n) for n in
                                                          ("xt", "hb", "hT", "gv", "yT", "stats", "small"))
        b_wsl = [k.buf(f"wsl{i}") for i in range(2)]
        b_wo = [k.buf("wo0")] * 2
        b_gvf = [k.buf() for _ in range(2)]
        b_sqf = [k.buf() for _ in range(2)]
        b_guf = b_gvf
        b_szf = b_sqf
        b_ssf = [k.buf() for _ in range(2)]
        b_xc = [k.buf() for _ in range(2)]
        b_xn = [k.buf() for _ in range(2)]
        b_xo = k.buf("xo_dram")
        cnt = {"w": 0, "z": 0, "o": 0, "t": 0, "c": 0}

        def load_slab(dst, bdst, col0):
            for c4 in range(4):
                k.dma("poolq", dst[:, c4 * 4:(c4 + 1) * 4, :],
                      w_in[c4 * 512:(c4 + 1) * 512, col0:col0 + 512].rearrange("(kt p) c -> p kt c", p=128),
                      writes=[bdst])

        blocks = [(i * 512, 512) for i in range(TP // 512)] + [(TP, TS)]
        b_xsrc = k.buf("xsrc")
        if pre_oproj:
            for (t0, nt) in blocks:
                ntile = max(1, nt // 128)
                P = min(nt, 128)
                for c8 in range(2):
                    k.dma("poolq", yT[:, c8 * 8:(c8 + 1) * 8, 0:nt],
                          aT_in[c8 * 1024:(c8 + 1) * 1024, t0:t0 + nt].rearrange("(kt p) t -> p kt t", p=128), writes=[b_yT])
                for oc in range(8):
                    for c8 in range(2):
                        k.dma("poolq", wo[0][:, c8 * 8:(c8 + 1) * 8, :],
                              w_o[c8 * 1024:(c8 + 1) * 1024, oc * 256:(oc + 1) * 256].rearrange("(kt p) c -> p kt c", p=128),
                              writes=[b_wo[0]])
                    for ti in range(ntile):
                        r0 = t0 + ti * 128
                        j = cnt["t"] % 2
                        cnt["t"] += 1
                        pp, bpp = (pD, bpD) if j == 0 else (pB, bpB)
                        k.dma("sp", xc[j][:P, :], x[r0:r0 + P, oc * 256:(oc + 1) * 256], writes=[b_xc[j]])
                        for ft in range(16):
                            k.op("pe", lambda: nc.tensor.matmul(pp[:P, 0:256], lhsT=yT[:, ft, ti * 128:ti * 128 + P],
                                                                rhs=wo[0][:, ft, :], start=(ft == 0), stop=(ft == 15)),
                                 reads=[b_yT, b_wo[0]], writes=[bpp], sig=(ft == 15))
                        k.op("dve", lambda: nc.vector.tensor_tensor(out=xn[j][:P, :], in0=pp[:P, 0:256], in1=xc[j][:P, :],
                                                                   op=ALU.add), reads=[bpp, b_xc[j]], writes=[b_xn[j]])
                        k.dma("sp", x3[r0:r0 + P, oc * 256:(oc + 1) * 256], xn[j][:P, :], reads=[b_xn[j]], writes=[b_xsrc])
            x = x3
        for (t0, nt) in blocks:
            samp = nt == TS
            ntile = max(1, nt // 128)
            P = min(nt, 128)
            for ti in range(ntile):
                r0 = t0 + ti * 128
                k.dma("sp", xt[:P, :], x[r0:r0 + P, :], reads=[b_xsrc], writes=[b_xt])
                for q4 in range(4):
                    k.op("act", lambda: nc.scalar.activation(out=gvf[0][:P, :], in_=xt[:P, q4 * 512:(q4 + 1) * 512],
                                                             func=AF.Square), reads=[b_xt], writes=[b_gvf[0]])
                    k.op("dve", lambda: nc.vector.reduce_sum(out=small[:P, q4:q4 + 1], in_=gvf[0][:P, :], axis=AX.X),
                         reads=[b_gvf[0]], writes=[b_small])
                k.op("dve", lambda: nc.vector.reduce_sum(out=small[:P, 4:5], in_=small[:P, 0:4], axis=AX.X),
                     reads=[b_small], writes=[b_small])
                rstd_from_sumsq(k, nc, small[:P, 5:6], small[:P, 4:5], D, [b_small], [b_small])
                k.op("dve", lambda: nc.vector.scalar_tensor_tensor(out=hb[:P, :], in0=xt[:P, :], scalar=small[:P, 5:6],
                                                                   in1=normw_bc[:P, :], op0=ALU.mult, op1=ALU.mult),
                     reads=[b_xt, b_small, b_const], writes=[b_hb])
                for half in range(2):
                    for j in range(8):
                        kt = half * 8 + j
                        k.op("pe", lambda: nc.tensor.transpose(out=pT[:, j * 128:j * 128 + P], in_=hb[:P, kt * 128:(kt + 1) * 128],
                                                               identity=ident[:P, :P]),
                             reads=[b_hb, b_id], writes=[bpT])
                    k.op("act", lambda: nc.scalar.copy(
                        out=hT[:, half * 8:half * 8 + 8, ti * 128:ti * 128 + P],
                        in_=pT[:].rearrange("p (a b) -> p a b", b=128)[:, :, 0:P]),
                        reads=[bpT], writes=[b_hT])
            for vc in range(8):
                i = cnt["w"] % 2
                cnt["w"] += 1
                load_slab(wsl[i], b_wsl[i], AW + vc * 512)
                for ti in range(ntile):
                    pp, bpp = (pA, bpA) if (cnt["t"] % 2 == 0) else (pB, bpB)
                    j = cnt["t"] % 2
                    cnt["t"] += 1
                    for kt in range(16):
                        k.op("pe", lambda: nc.tensor.matmul(pp[:P, :], lhsT=hT[:, kt, ti * 128:ti * 128 + P],
                                                            rhs=wsl[i][:, kt, :], start=(kt == 0), stop=(kt == 15)),
                             reads=[b_hT, b_wsl[i]], writes=[bpp], sig=(kt == 15))
                    k.op("act", lambda: nc.scalar.activation(out=gvf[j][:P, :], in_=pp[:P, :], func=AF.Gelu),
                         reads=[bpp], writes=[b_gvf[j]])
                    k.op("act", lambda: nc.scalar.activation(out=sqf[j][:P, :], in_=gvf[j][:P, :], func=AF.Square),
                         reads=[b_gvf[j]], writes=[b_sqf[j]])
                    k.op("dve", lambda: nc.vector.reduce_sum(out=stats[:P, ti, 0, vc:vc + 1], in_=gvf[j][:P, :], axis=AX.X),
                         reads=[b_gvf[j]], writes=[b_stats])
                    k.op("dve", lambda: nc.vector.reduce_sum(out=stats[:P, ti, 1, vc:vc + 1], in_=sqf[j][:P, :], axis=AX.X),
                         reads=[b_sqf[j]], writes=[b_stats])
                    k.op("dve", lambda: nc.vector.tensor_copy(out=gv[:P, ti, vc * 512:(vc + 1) * 512], in_=gvf[j][:P, :]),
                         reads=[b_gvf[j]], writes=[b_gv])
            for ti in range(ntile):
                k.op("dve", lambda: nc.vector.reduce_sum(out=st2[:P, ti, 0:1], in_=stats[:P, ti, 0, :], axis=AX.X),
                     reads=[b_stats], writes=[b_stats])
                k.op("dve", lambda: nc.vector.reduce_sum(out=st2[:P, ti, 1:2], in_=stats[:P, ti, 1, :], axis=AX.X),
                     reads=[b_stats], writes=[b_stats])
                k.op("dve", lambda: nc.vector.tensor_scalar(out=st2[:P, ti, 2:3], in0=st2[:P, ti, 0:1], scalar1=1.0 / AW,
                                                           scalar2=None, op0=ALU.mult), reads=[b_stats], writes=[b_stats])
                k.op("dve", lambda: nc.vector.tensor_scalar(out=st2[:P, ti, 3:4], in0=st2[:P, ti, 1:2], scalar1=1.0 / AW,
                                                           scalar2=None, op0=ALU.mult), reads=[b_stats], writes=[b_stats])
                k.op("dve", lambda: nc.vector.tensor_tensor(out=st2[:P, ti, 4:5], in0=st2[:P, ti, 2:3], in1=st2[:P, ti, 2:3],
                                                           op=ALU.mult), reads=[b_stats], writes=[b_stats])
                k.op("dve", lambda: nc.vector.tensor_tensor(out=st2[:P, ti, 5:6], in0=st2[:P, ti, 3:4], in1=st2[:P, ti, 4:5],
                                                           op=ALU.subtract), reads=[b_stats], writes=[b_stats])
                k.op("dve", lambda: nc.vector.tensor_scalar(out=st2[:P, ti, 6:7], in0=st2[:P, ti, 5:6], scalar1=EPS,
                                                           scalar2=None, op0=ALU.add),
                     reads=[b_stats], writes=[b_stats])
                k.op("act", lambda: nc.scalar.activation(out=st2[:P, ti, 6:7], in_=st2[:P, ti, 6:7], func=AF.Sqrt),
                     reads=[b_stats], writes=[b_stats])
                k.op("dve", lambda: nc.vector.reciprocal(out=st2[:P, ti, 6:7], in_=st2[:P, ti, 6:7]),
                     reads=[b_stats], writes=[b_stats])
                k.op("dve", lambda: nc.vector.tensor_scalar(out=gv[:P, ti, :], in0=gv[:P, ti, :], scalar1=st2[:P, ti, 2:3],
                                                           scalar2=st2[:P, ti, 6:7], op0=ALU.subtract, op1=ALU.mult),
                     reads=[b_stats, b_gv], writes=[b_gv])
            if samp:
                for ft in range(32):
                    k.op("pe", lambda: nc.tensor.transpose(out=pT[:, 0:P], in_=gv[:P, 0, ft * 128:(ft + 1) * 128],
                                                           identity=ident[:P, :P]), reads=[b_gv, b_id], writes=[bpT])
                    k.op("act", lambda: nc.scalar.activation(out=ssf[ft % 2][:, 0:P], in_=pT[:, 0:P], func=AF.Identity,
                                                             bias=lnT[:, 1, ft:ft + 1], scale=lnT[:, 0, ft:ft + 1]),
                         reads=[bpT, b_const], writes=[b_ssf[ft % 2]])
                    k.dma("sp", vsT[ft * 128:(ft + 1) * 128, :], ssf[ft % 2][:, 0:P], reads=[b_ssf[ft % 2]], writes=[b_xo])
            for fc in range(8):
                load_slab(wsl[0], b_wsl[0], fc * 512)
                load_slab(wsl[1], b_wsl[1], 2 * AW + fc * 512)
                for fj in range(4):
                    ft = fc * 4 + fj
                    g = ft // 2
                    j = cnt["c"] % 2
                    cnt["c"] += 1
                    for kt in range(16):
                        k.op("pe", lambda: nc.tensor.matmul(pA[:, 0:nt], lhsT=wsl[0][:, kt, fj * 128:(fj + 1) * 128],
                                                            rhs=hT[:, kt, 0:nt], start=(kt == 0), stop=(kt == 15)),
                             reads=[b_hT, b_wsl[0]], writes=[bpA], sig=(kt == 15))
                    for kt in range(16):
                        k.op("pe", lambda: nc.tensor.matmul(pB[:, 0:nt], lhsT=wsl[1][:, kt, fj * 128:(fj + 1) * 128],
                                                            rhs=hT[:, kt, 0:nt], start=(kt == 0), stop=(kt == 15)),
                             reads=[b_hT, b_wsl[1]], writes=[bpB], sig=(kt == 15))
                    for ti in range(ntile):
                        if samp:
                            k.op("pe", lambda: nc.tensor.matmul(pC[:, 0:P], lhsT=gv[:P, 0, ft * 128:(ft + 1) * 128],
                                                                rhs=wposT_s[:, g, :], start=True, stop=True),
                                 reads=[b_gv, b_const], writes=[bpC])
                        else:
                            k.op("pe", lambda: nc.tensor.matmul(pC[:, ti * 128:(ti + 1) * 128],
                                                                lhsT=gv[:, ti, ft * 128:(ft + 1) * 128],
                                                                rhs=wposT[:, g, :], start=True, stop=True),
                                 reads=[b_gv, b_const], writes=[bpC])
                    k.op("act", lambda: nc.scalar.activation(out=guf[j][:, 0:nt], in_=pA[:, 0:nt], func=AF.Gelu),
                         reads=[bpA], writes=[b_guf[j]])
                    k.op("act", lambda: nc.scalar.activation(out=szf[j][:, 0:nt], in_=pB[:, 0:nt], func=AF.Silu),
                         reads=[bpB], writes=[b_szf[j]])
                    for ti in range(ntile):
                        bt = biasT_s[:, ft, :] if samp else biasT[:, ft, :]
                        k.op("dve", lambda: nc.vector.scalar_tensor_tensor(
                            out=ssf[j][:, ti * 128:ti * 128 + P], in0=pC[:, ti * 128:ti * 128 + P],
                            scalar=lnT[:, 0, ft:ft + 1], in1=bt, op0=ALU.mult, op1=ALU.add),
                            reads=[bpC, b_const], writes=[b_ssf[j]])
                    k.op("dve", lambda: nc.vector.tensor_tensor(out=ssf[j][:, 0:nt], in0=ssf[j][:, 0:nt], in1=guf[j][:, 0:nt],
                                                               op=ALU.mult), reads=[b_ssf[j], b_guf[j]], writes=[b_ssf[j]])
                    k.op("dve", lambda: nc.vector.tensor_tensor(out=yT[:, ft, 0:nt], in0=ssf[j][:, 0:nt], in1=szf[j][:, 0:nt],
                                                               op=ALU.mult), reads=[b_ssf[j], b_szf[j]], writes=[b_yT])
            for oc in range(8):
                i = cnt["o"] % 2
                cnt["o"] += 1
                for c4 in range(4):
                    k.dma("poolq", wo[i][:, c4 * 8:(c4 + 1) * 8, :],
                          w_out[c4 * 1024:(c4 + 1) * 1024, oc * 256:(oc + 1) * 256].rearrange("(kt p) c -> p kt c", p=128),
                          writes=[b_wo[i]])
                for ti in range(ntile):
                    r0 = t0 + ti * 128
                    j = cnt["t"] % 2
                    cnt["t"] += 1
                    pp, bpp = (pD, bpD) if j == 0 else (pB, bpB)
                    k.dma("sp", xc[j][:P, :], x[r0:r0 + P, oc * 256:(oc + 1) * 256], reads=[b_xsrc], writes=[b_xc[j]])
                    for ft in range(32):
                        k.op("pe", lambda: nc.tensor.matmul(pp[:P, 0:256], lhsT=yT[:, ft, ti * 128:ti * 128 + P],
                                                            rhs=wo[i][:, ft, :], start=(ft == 0), stop=(ft == 31)),
                             reads=[b_yT, b_wo[i]], writes=[bpp], sig=(ft == 31))
                    k.op("dve", lambda: nc.vector.tensor_tensor(out=xn[j][:P, :], in0=pp[:P, 0:256], in1=xc[j][:P, :],
                                                               op=ALU.add), reads=[bpp, b_xc[j]], writes=[b_xn[j]])
                    k.dma("sp", xo[r0:r0 + P, oc * 256:(oc + 1) * 256], xn[j][:P, :], reads=[b_xn[j]], writes=[b_xo])
            if final_norm:
                for ti in range(ntile):
                    r0 = t0 + ti * 128
                    k.dma("sp", hT[:P, :, :].rearrange("p a b -> p (a b)").bitcast(F32)[:, 0:D], xo[r0:r0 + P, :],
                          reads=[b_xo], writes=[b_hT])
                    xfl = hT[:P, :, :].rearrange("p a b -> p (a b)").bitcast(F32)
                    for q4 in range(4):
                        k.op("act", lambda: nc.scalar.activation(out=gvf[0][:P, :], in_=xfl[:, q4 * 512:(q4 + 1) * 512],
                                                                 func=AF.Square), reads=[b_hT], writes=[b_gvf[0]])
                        k.op("dve", lambda: nc.vector.reduce_sum(out=small[:P, q4:q4 + 1], in_=gvf[0][:P, :], axis=AX.X),
                             reads=[b_gvf[0]], writes=[b_small])
                    k.op("dve", lambda: nc.vector.reduce_sum(out=small[:P, 4:5], in_=small[:P, 0:4], axis=AX.X),
                         reads=[b_small], writes=[b_small])
                    rstd_from_sumsq(k, nc, small[:P, 5:6], small[:P, 4:5], D, [b_small], [b_small])
                    k.op("dve", lambda: nc.vector.scalar_tensor_tensor(out=xt[:P, :], in0=xfl[:, 0:D], scalar=small[:P, 5:6],
                                                                       in1=fnw_bc[:P, :], op0=ALU.mult, op1=ALU.mult),
                         reads=[b_hT, b_small, b_const], writes=[b_xt])
                    k.dma("sp", yo[r0:r0 + P, :], xt[:P, :], reads=[b_xt], writes=[b_xo])
        k.finish()
    return nc


def _run(nc, in_maps):
    res = run_bass_kernel_spmd(nc, in_maps, core_ids=list(range(NCORES)))
    return res.results


def tok_shard(xp, xs, c):
    return np.ascontiguousarray(np.concatenate(
        [xp[c * TP:(c + 1) * TP], xs[c * 4:(c + 1) * 4].reshape(TS, -1)], axis=0))


def run_gmlp(xp, xs, normw, w_in, ln_g, ln_b, w_s, b_s, w_out, fnw=None, pre=None):
    nc = build_gmlp(fnw is not None, pre is not None)
    in_maps = []
    for c in range(NCORES):
        m = {"x": tok_shard(xp, xs, c), "normw": normw, "w_in": w_in, "ln_g": ln_g, "ln_b": ln_b,
             "w_s": w_s, "b_s": b_s, "w_out": w_out}
        if fnw is not None:
            m["fnw"] = fnw
        if pre is not None:
            m["aT"] = np.ascontiguousarray(tok_shard(pre[0], pre[1], c).T)
            m["w_o"] = pre[2]
        in_maps.append(m)
    return _run(nc, in_maps)


NTOK = SEQ + NSB * NST
GC = 768


def build_mamba():
    nc = bass.Bass("TRN2", target_bir_lowering=False)
    xT = nc.dram_tensor("xT", [D, NTOK], F32, kind="ExternalInput").ap()
    normwT = nc.dram_tensor("normwT", [128, 16], F32, kind="ExternalInput").ap()
    w_z = nc.dram_tensor("w_z", [D, 512], F32, kind="ExternalInput").ap()
    w_xbc = nc.dram_tensor("w_xbc", [D, GC], F32, kind="ExternalInput").ap()
    w_dt = nc.dram_tensor("w_dt", [D, 8], F32, kind="ExternalInput").ap()
    convw = nc.dram_tensor("convw", [128, 6, 4], F32, kind="ExternalInput").ap()
    convb = nc.dram_tensor("convb", [128, 6], F32, kind="ExternalInput").ap()
    hp = nc.dram_tensor("hp", [3, 8], F32, kind="ExternalInput").ap()
    dexp = nc.dram_tensor("dexp", [512], F32, kind="ExternalInput").ap()
    nw = nc.dram_tensor("nw", [512], F32, kind="ExternalInput").ap()
    sT_in = nc.dram_tensor("sT_in", [NSB, 128, 512], F32, kind="ExternalInput").ap()
    cs_in = nc.dram_tensor("cs_in", [128, 6, NSB, 3], F32, kind="ExternalInput").ap()
    yn = nc.dram_tensor("yn", [NTOK, 512], F32, kind="ExternalOutput").ap()
    sT_p = nc.dram_tensor("sT_p", [128, 512], F32, kind="ExternalOutput").ap()
    cs_p = nc.dram_tensor("cs_p", [128, 6, 3], F32, kind="ExternalOutput").ap()
    sT_s = nc.dram_tensor("sT_s", [NSB, 128, 512], F32, kind="ExternalOutput").ap()
    cs_s = nc.dram_tensor("cs_s", [128, 6, NSB, 3], F32, kind="ExternalOutput").ap()

    with ExitStack() as st:
        k = K(nc, st)
        identf, ident, b_id = make_ident(k, nc)
        bc = k.buf("const")
        nwT = k.sb("nwT", [128, 16], F32)
        k.dma("sp", nwT[:], normwT[:, :], writes=[bc])
        cw = k.sb("cw", [128, 6, 4], F32)
        k.dma("sp", cw[:], convw[:, :, :], writes=[bc])
        cbias = k.sb("cbias", [128, 6], F32)
        k.dma("sp", cbias[:], convb[:, :], writes=[bc])
        hpb = k.sb("hpb", [128, 3, 8], F32)
        k.dma("sp", hpb[:].rearrange("p a b -> p (a b)"), hp.rearrange("a b -> (a b)").partition_broadcast(128), writes=[bc])
        a_bc = k.sb("a_bc", [128, 8], F32)
        k.op("act", lambda: nc.scalar.activation(out=a_bc[:], in_=hpb[:, 1, :], func=AF.Exp), reads=[bc], writes=[bc])
        k.op("dve", lambda: nc.vector.tensor_scalar(out=a_bc[:], in0=a_bc[:], scalar1=-1.0, scalar2=None, op0=ALU.mult),
             reads=[bc], writes=[bc])
        d_bc = k.sb("d_bc", [128, 512], F32)
        k.dma("sp", d_bc[:], dexp.partition_broadcast(128), writes=[bc])
        nw_bc = k.sb("nw_bc", [128, 512], F32)
        k.dma("sp", nw_bc[:], nw.partition_broadcast(128), writes=[bc])
        ones_b = k.sb("ones_b", [128, 128], BF16)
        k.op("dve", lambda: nc.vector.memset(ones_b[:], 1.0), writes=[bc])
        ones_f = k.sb("ones_f", [128, 128], F32)
        k.op("dve", lambda: nc.vector.memset(ones_f[:], 1.0), writes=[bc])
        tri_le = k.sb("tri_le", [128, 128], F32)
        k.op("pool", lambda: nc.gpsimd.affine_select(out=tri_le[:], in_=ones_f[:], pattern=[[1, 128]],
                                                     compare_op=ALU.is_ge, fill=0.0, base=0, channel_multiplier=-1),
             reads=[bc], writes=[bc])
        mgt = k.sb("mgt", [128, 128], F32)
        k.op("pool", lambda: nc.gpsimd.affine_select(out=mgt[:], in_=ones_f[:], pattern=[[-1, 128]],
                                                     compare_op=ALU.is_gt, fill=0.0, base=0, channel_multiplier=1),
             reads=[bc], writes=[bc])
        Wx = k.sb("Wx", [128, 16, GC], BF16)
        Wz = k.sb("Wz", [128, 16, 512], BF16)
        Wd = k.sb("Wd", [128, 16, 8], BF16)
        for c4 in range(4):
            k.dma("poolq", Wx[:, c4 * 4:(c4 + 1) * 4, :], w_xbc[c4 * 512:(c4 + 1) * 512, :].rearrange("(kt p) c -> p kt c", p=128), writes=[bc])
            k.dma("poolq", Wz[:, c4 * 4:(c4 + 1) * 4, :], w_z[c4 * 512:(c4 + 1) * 512, :].rearrange("(kt p) c -> p kt c", p=128), writes=[bc])
        k.dma("poolq", Wd[:], w_dt.rearrange("(kt p) c -> p kt c", p=128), writes=[bc])

        pX = k.ps("pX", [128, 1024], F32)
        pZ = k.ps("pZ", [128, 512], F32)
        pSeg = k.ps("pSeg", [128, 1024], F32)
        pY = k.ps("pY", [128, 512], F32)
        pYi = k.ps("pYi", [128, 512], F32)
        pM = k.ps("pM", [128, 512], F32)
        bpX, bpZ, bpSeg, bpY, bpYi, bpM = (k.pbuf(n) for n in ("pX", "pZ", "pSeg", "pY", "pYi", "pM"))
        pYi_b = pYi[:].bitcast(BF16)

        xTb = k.sb("xTb", [128, 16, 512], F32)
        sqb = [k.sb(f"sqb{i}", [128, 512], BF16) for i in range(2)]
        hT = k.sb("hT", [128, 16, 512], BF16)
        rstd = k.sb("rstd", [128, 512], F32)
        cbuf = [k.sb(f"cbuf{i}", [128, 6, 131], F32) for i in range(2)]
        cv = k.sb("cv", [128, 6, 128], F32)
        xbc = k.sb("xbc", [128, 6, 128], BF16)
        Et = k.sb("Et", [128, 8, 128], F32)
        MT = k.sb("MT", [128, 8, 128], BF16)
        lh = k.sb("lh", [128, 8, 128], F32)
        x_tok = k.sb("x_tok", [128, 512], BF16)
        B_tok = k.sb("B_tok", [128, 128], BF16)
        xdt = k.sb("xdt", [128, 512], BF16)
        xw = k.sb("xw", [128, 512], BF16)
        y_sb = k.sb("y_sb", [128, 512], F32)
        zs = k.sb("zs", [128, 512], F32)
        yg = k.sb("yg", [128, 512], F32)
        ysq = k.sb("ysq", [128, 512], F32)
        yo = [k.sb(f"yo{i}", [128, 512], F32) for i in range(2)]
        ST = k.sb("ST", [128, 512], F32)
        STb = k.sb("STb", [128, 512], BF16)
        dtt = k.sb("dtt", [128, 8], F32)
        da = k.sb("da", [128, 8], F32)
        ecum = k.sb("ecum", [128, 8], F32)
        dec = k.sb("dec", [128, 8], F32)
        cbm = k.sb("cbm", [128, 128], F32)
        sm = k.sb("sm", [128, 4], F32)
        cso = k.sb("cso", [128, 6, NSB, 3], F32)
        (b_xTb, b_hT, b_rstd, b_cv, b_xbc, b_Et, b_MT, b_lh, b_xtok, b_Btok, b_xdt, b_xw, b_ysb, b_zs, b_yg, b_ysq,
         b_ST, b_STb, b_dt, b_da, b_ecum, b_dec, b_cbm, b_sm, b_cso, b_out) = (k.buf() for _ in range(26))
        b_sqb = [k.buf() for _ in range(2)]
        b_cbuf = [k.buf() for _ in range(2)]
        b_yo = [k.buf() for _ in range(2)]
        k.dma("sp", cso[:], cs_in[:, :, :, :], writes=[b_cso])
        k.op("dve", lambda: nc.vector.memset(ST[:], 0.0), writes=[b_ST])
        k.op("dve", lambda: nc.vector.memset(STb[:], 0.0), writes=[b_STb])
        k.op("dve", lambda: nc.vector.memset(cbuf[0][:], 0.0), writes=[b_cbuf[0]])
        k.op("dve", lambda: nc.vector.memset(cbuf[1][:], 0.0), writes=[b_cbuf[1]])
        ucount = [0]

        def unit(u0, L, tok0, sample_b):
            ci = ucount[0] % 2
            ucount[0] += 1
            cb_, bcb = cbuf[ci], b_cbuf[ci]
            cbn, bcbn = cbuf[1 - ci], b_cbuf[1 - ci]
            if sample_b is not None:
                k.op("dve", lambda: nc.vector.tensor_copy(out=cb_[:, :, 0:3], in_=cso[:, :, sample_b, :]),
                     reads=[b_cso], writes=[bcb])
                k.dma("sp", ST[:], sT_in[sample_b, :, :], writes=[b_ST])
                k.op("act", lambda: nc.scalar.copy(out=STb[:], in_=ST[:]), reads=[b_ST], writes=[b_STb])
            for ct in range(6):
                for kt in range(16):
                    k.op("pe", lambda: nc.tensor.matmul(pX[:, ct * 128:ct * 128 + L], lhsT=Wx[:, kt, ct * 128:(ct + 1) * 128],
                                                        rhs=hT[:, kt, u0:u0 + L], start=(kt == 0), stop=(kt == 15)),
                         reads=[b_hT, bc], writes=[bpX], sig=(kt == 15))
            k.op("act", lambda: nc.scalar.copy(out=cb_[:, 0:4, 3:3 + L],
                                               in_=pX[:, 0:512].rearrange("p (a b) -> p a b", b=128)[:, :, 0:L]),
                 reads=[bpX], writes=[bcb])
            k.op("act", lambda: nc.scalar.copy(out=cb_[:, 4:6, 3:3 + L],
                                               in_=pX[:, 512:768].rearrange("p (a b) -> p a b", b=128)[:, :, 0:L]),
                 reads=[bpX], writes=[bcb])
            if sample_b is not None:
                k.op("dve", lambda: nc.vector.tensor_copy(out=cso[:, :, sample_b, :], in_=cb_[:, :, L:L + 3]),
                     reads=[bcb], writes=[b_cso])
            else:
                k.op("dve", lambda: nc.vector.tensor_copy(out=cbn[:, :, 0:3], in_=cb_[:, :, L:L + 3]),
                     reads=[bcb], writes=[bcbn])
            for ct in range(6):
                k.op("act", lambda: nc.scalar.activation(out=cv[:, ct, 0:L], in_=cb_[:, ct, 0:L], func=AF.Identity,
                                                         bias=cbias[:, ct:ct + 1], scale=cw[:, ct, 0:1]),
                     reads=[bcb, bc], writes=[b_cv])
                for kk in range(1, 4):
                    k.op("dve", lambda: nc.vector.scalar_tensor_tensor(out=cv[:, ct, 0:L], in0=cb_[:, ct, kk:kk + L],
                                                                       scalar=cw[:, ct, kk:kk + 1], in1=cv[:, ct, 0:L],
                                                                       op0=ALU.mult, op1=ALU.add),
                         reads=[bcb, bc, b_cv], writes=[b_cv])
            k.op("act", lambda: nc.scalar.activation(out=xbc[:, :, 0:L], in_=cv[:, :, 0:L], func=AF.Silu),
                 reads=[b_cv], writes=[b_xbc])
            for kt in range(16):
                k.op("pe", lambda: nc.tensor.matmul(pZ[:L, :], lhsT=hT[:, kt, u0:u0 + L], rhs=Wz[:, kt, :],
                                                    start=(kt == 0), stop=(kt == 15)),
                     reads=[b_hT, bc], writes=[bpZ], sig=(kt == 15))
            for kt in range(16):
                k.op("pe", lambda: nc.tensor.matmul(pM[:L, 0:8], lhsT=hT[:, kt, u0:u0 + L], rhs=Wd[:, kt, :],
                                                    start=(kt == 0), stop=(kt == 15)),
                     reads=[b_hT, bc], writes=[bpM], sig=(kt == 15))
            k.op("dve", lambda: nc.vector.tensor_tensor(out=dtt[:L, :], in0=pM[:L, 0:8], in1=hpb[:L, 0, :], op=ALU.add),
                 reads=[bpM, bc], writes=[b_dt])
            k.op("act", lambda: nc.scalar.activation(out=dtt[:L, :], in_=dtt[:L, :], func=AF.Exp), reads=[b_dt], writes=[b_dt])
            k.op("act", lambda: nc.scalar.activation(out=dtt[:L, :], in_=dtt[:L, :], func=AF.Ln, bias=1.0), reads=[b_dt], writes=[b_dt])
            k.op("dve", lambda: nc.vector.tensor_tensor(out=da[:L, :], in0=dtt[:L, :], in1=a_bc[:L, :], op=ALU.mult),
                 reads=[b_dt, bc], writes=[b_da])
            for ct in range(5):
                k.op("pe", lambda: nc.tensor.transpose(out=pYi_b[:L, ct * 128:(ct + 1) * 128], in_=xbc[:, ct, 0:L], identity=ident[:]),
                     reads=[b_xbc, b_id], writes=[bpYi])
            k.op("act", lambda: nc.scalar.copy(out=x_tok[:L, :], in_=pYi_b[:L, 0:512]), reads=[bpYi], writes=[b_xtok])
            k.op("act", lambda: nc.scalar.copy(out=B_tok[:L, :], in_=pYi_b[:L, 512:640]), reads=[bpYi], writes=[b_Btok])
            for h in range(8):
                k.op("dve", lambda: nc.vector.tensor_scalar(out=xdt[:L, h * 64:(h + 1) * 64], in0=x_tok[:L, h * 64:(h + 1) * 64],
                                                           scalar1=dtt[:L, h:h + 1], scalar2=None, op0=ALU.mult),
                     reads=[b_xtok, b_dt], writes=[b_xdt])
            for h in range(8):
                k.op("dve", lambda: nc.vector.tensor_scalar(out=lh[:L, h, 0:L], in0=mgt[:L, 0:L], scalar1=da[:L, h:h + 1],
                                                           scalar2=None, op0=ALU.mult), reads=[b_da, bc], writes=[b_lh])
            for h in range(8):
                k.op("pe", lambda: nc.tensor.matmul(pSeg[:L, h * 128:h * 128 + L], lhsT=lh[:L, h, 0:L], rhs=tri_le[:L, 0:L],
                                                    start=True, stop=True), reads=[b_lh, bc], writes=[bpSeg])
            for hb_ in range(2):
                k.op("act", lambda: nc.scalar.activation(
                    out=Et[:L, hb_ * 4:hb_ * 4 + 4, 0:L],
                    in_=pSeg[:L, hb_ * 512:(hb_ + 1) * 512].rearrange("p (a b) -> p a b", b=128)[:, :, 0:L], func=AF.Exp),
                    reads=[bpSeg], writes=[b_Et])
            k.op("pe", lambda: nc.tensor.matmul(pM[:L, 8:16], lhsT=tri_le[:L, 0:L], rhs=da[:L, :], start=True, stop=True),
                 reads=[b_da, bc], writes=[bpM])
            k.op("pe", lambda: nc.tensor.matmul(pM[:, 16:24], lhsT=ones_f[:L, :], rhs=da[:L, :], start=True, stop=True),
                 reads=[b_da, bc], writes=[bpM])
            k.op("act", lambda: nc.scalar.activation(out=ecum[:L, :], in_=pM[:L, 8:16], func=AF.Exp), reads=[bpM], writes=[b_ecum])
            k.op("act", lambda: nc.scalar.activation(out=dec[:, :], in_=pM[:, 16:24], func=AF.Exp), reads=[bpM], writes=[b_dec])
            k.op("pe", lambda: nc.tensor.matmul(pM[:L, 128:128 + L], lhsT=xbc[:, 4, 0:L], rhs=xbc[:, 5, 0:L], start=True, stop=True),
                 reads=[b_xbc], writes=[bpM])
            k.op("dve", lambda: nc.vector.tensor_tensor(out=cbm[:L, 0:L], in0=pM[:L, 128:128 + L], in1=tri_le[:L, 0:L], op=ALU.mult),
                 reads=[bpM, bc], writes=[b_cbm])
            for h in range(8):
                k.op("dve", lambda: nc.vector.tensor_tensor(out=MT[:L, h, 0:L], in0=Et[:L, h, 0:L], in1=cbm[:L, 0:L], op=ALU.mult),
                     reads=[b_Et, b_cbm], writes=[b_MT])
            for h in range(8):
                k.op("pe", lambda: nc.tensor.matmul(pY[:L, h * 64:(h + 1) * 64], lhsT=MT[:L, h, 0:L], rhs=xdt[:L, h * 64:(h + 1) * 64],
                                                    start=True, stop=True), reads=[b_MT, b_xdt], writes=[bpY])
            k.op("pe", lambda: nc.tensor.matmul(pYi[:L, :], lhsT=xbc[:, 5, 0:L], rhs=STb[:], start=True, stop=True),
                 reads=[b_xbc, b_STb], writes=[bpYi])
            k.op("act", lambda: nc.scalar.copy(out=y_sb[:L, :], in_=pY[:L, :]), reads=[bpY], writes=[b_ysb])
            for h in range(8):
                k.op("dve", lambda: nc.vector.scalar_tensor_tensor(out=y_sb[:L, h * 64:(h + 1) * 64], in0=pYi[:L, h * 64:(h + 1) * 64],
                                                                   scalar=ecum[:L, h:h + 1], in1=y_sb[:L, h * 64:(h + 1) * 64],
                                                                   op0=ALU.mult, op1=ALU.add),
                     reads=[bpYi, b_ecum, b_ysb], writes=[b_ysb])
            k.op("dve", lambda: nc.vector.tensor_tensor(out=yg[:L, :], in0=x_tok[:L, :], in1=d_bc[:L, :], op=ALU.mult),
                 reads=[b_xtok, bc], writes=[b_yg])
            k.op("dve", lambda: nc.vector.tensor_tensor(out=y_sb[:L, :], in0=y_sb[:L, :], in1=yg[:L, :], op=ALU.add),
                 reads=[b_ysb, b_yg], writes=[b_ysb])
            for h in range(8):
                k.op("dve", lambda: nc.vector.tensor_scalar(out=xw[:L, h * 64:(h + 1) * 64], in0=xdt[:L, h * 64:(h + 1) * 64],
                                                           scalar1=Et[:L, h, L - 1:L], scalar2=None, op0=ALU.mult),
                     reads=[b_xdt, b_Et], writes=[b_xw])
            k.op("pe", lambda: nc.tensor.matmul(pX[:, 0:512], lhsT=B_tok[:L, :], rhs=xw[:L, :], start=True, stop=True),
                 reads=[b_Btok, b_xw], writes=[bpX])
            for h in range(8):
                k.op("dve", lambda: nc.vector.scalar_tensor_tensor(out=ST[:, h * 64:(h + 1) * 64], in0=ST[:, h * 64:(h + 1) * 64],
                                                                   scalar=dec[:, h:h + 1], in1=pX[:, h * 64:(h + 1) * 64],
                                                                   op0=ALU.mult, op1=ALU.add),
                     reads=[b_ST, b_dec, bpX], writes=[b_ST])
            if sample_b is not None:
                k.dma("sp", sT_s[sample_b, :, :], ST[:], reads=[b_ST], writes=[b_out])
            else:
                k.op("act", lambda: nc.scalar.copy(out=STb[:], in_=ST[:]), reads=[b_ST], writes=[b_STb])
            k.op("act", lambda: nc.scalar.activation(out=zs[:L, :], in_=pZ[:L, :], func=AF.Silu), reads=[bpZ], writes=[b_zs])
            k.op("dve", lambda: nc.vector.tensor_tensor(out=yg[:L, :], in0=y_sb[:L, :], in1=zs[:L, :], op=ALU.mult),
                 reads=[b_ysb, b_zs], writes=[b_yg])
            k.op("act", lambda: nc.scalar.activation(out=ysq[:L, :], in_=yg[:L, :], func=AF.Square), reads=[b_yg], writes=[b_ysq])
            k.op("dve", lambda: nc.vector.reduce_sum(out=sm[:L, 0:1], in_=ysq[:L, :], axis=AX.X), reads=[b_ysq], writes=[b_sm])
            rstd_from_sumsq(k, nc, sm[:L, 1:2], sm[:L, 0:1], 512, [b_sm], [b_sm])
            oi = ucount[0] % 2
            k.op("dve", lambda: nc.vector.scalar_tensor_tensor(out=yo[oi][:L, :], in0=yg[:L, :], scalar=sm[:L, 1:2], in1=nw_bc[:L, :],
                                                               op0=ALU.mult, op1=ALU.mult),
                 reads=[b_yg, b_sm, bc], writes=[b_yo[oi]])
            k.dma("sp", yn[tok0:tok0 + L, :], yo[oi][:L, :], reads=[b_yo[oi]], writes=[b_out])
            return cbn, bcbn

        nblk = NTOK // 512
        last = None
        for bi in range(nblk):
            c0 = bi * 512
            for c4 in range(4):
                k.dma("sp", xTb[:, c4 * 4:(c4 + 1) * 4, :],
                      xT[c4 * 512:(c4 + 1) * 512, c0:c0 + 512].rearrange("(kt p) t -> p kt t", p=128), writes=[b_xTb])
            for kt in range(16):
                j = kt % 2
                k.op("act", lambda: nc.scalar.activation(out=sqb[j][:], in_=xTb[:, kt, :], func=AF.Square),
                     reads=[b_xTb], writes=[b_sqb[j]])
                k.op("pe", lambda: nc.tensor.matmul(pZ[:, :], lhsT=ones_b[:], rhs=sqb[j][:], start=(kt == 0), stop=(kt == 15)),
                     reads=[b_sqb[j], bc], writes=[bpZ])
            k.op("dve", lambda: nc.vector.tensor_scalar(out=rstd[:], in0=pZ[:, :], scalar1=1.0 / D, scalar2=EPS,
                                                       op0=ALU.mult, op1=ALU.add), reads=[bpZ], writes=[b_rstd])
            k.op("act", lambda: nc.scalar.activation(out=rstd[:], in_=rstd[:], func=AF.Sqrt), reads=[b_rstd], writes=[b_rstd])
            k.op("dve", lambda: nc.vector.reciprocal(out=rstd[:], in_=rstd[:]), reads=[b_rstd], writes=[b_rstd])
            for kt in range(16):
                k.op("dve", lambda: nc.vector.scalar_tensor_tensor(out=hT[:, kt, :], in0=xTb[:, kt, :], scalar=nwT[:, kt:kt + 1],
                                                                   in1=rstd[:], op0=ALU.mult, op1=ALU.mult),
                     reads=[b_xTb, b_rstd, bc], writes=[b_hT])
            if bi < SEQ // 512:
                for u in range(4):
                    last = unit(u * 128, 128, c0 + u * 128, None)
                if bi == SEQ // 512 - 1:
                    k.dma("sp", sT_p[:, :], ST[:], reads=[b_ST], writes=[b_out])
                    k.dma("sp", cs_p[:, :, :], last[0][:, :, 0:3], reads=[last[1]], writes=[b_out])
            else:
                for b in range(NSB):
                    unit(b * 16, 16, c0 + b * 16, b)
        k.dma("sp", cs_s[:, :, :, :], cso[:], reads=[b_cso], writes=[b_out])
        k.finish()
    return nc


PAST = 1024


def build_attn(dbg_heads=2, dbg_blocks=None, dbg_attend=True):
    nc = bass.Bass("TRN2", target_bir_lowering=False)
    xT = nc.dram_tensor("xT", [D, NTOK], F32, kind="ExternalInput").ap()
    normwT = nc.dram_tensor("normwT", [128, 16], F32, kind="ExternalInput").ap()
    w_att = nc.dram_tensor("w_att", [2, D, 512], F32, kind="ExternalInput").ap()
    ckT = nc.dram_tensor("ckT", [NSB, 2, 128, PAST], F32, kind="ExternalInput").ap()
    cv_ = nc.dram_tensor("cv", [NSB, 2, PAST, 128], F32, kind="ExternalInput").ap()
    ogT = nc.dram_tensor("ogT", [256, NTOK], F32, kind="ExternalOutput").ap()
    kTo = nc.dram_tensor("kTo", [2, 128, NTOK], F32, kind="ExternalOutput").ap()
    vo = nc.dram_tensor("vo", [2, NTOK, 128], F32, kind="ExternalOutput").ap()
    SCALE = 128 ** -0.5

    with ExitStack() as st:
        k = K(nc, st)
        identf, ident, b_id = make_ident(k, nc)
        bc = k.buf("const")
        nwT = k.sb("nwT", [128, 16], F32)
        k.dma("sp", nwT[:], normwT[:, :], writes=[bc])
        ones_b = k.sb("ones_b", [128, 128], BF16)
        k.op("dve", lambda: nc.vector.memset(ones_b[:], 1.0), writes=[bc])
        ones_f = k.sb("ones_f", [128, 128], F32)
        k.op("dve", lambda: nc.vector.memset(ones_f[:], 1.0), writes=[bc])
        ones_w = k.sb("ones_w", [128, 512], F32)
        k.op("dve", lambda: nc.vector.memset(ones_w[:], 1.0), writes=[bc])
        mgt = k.sb("mgt", [128, 128], F32)
        k.op("pool", lambda: nc.gpsimd.affine_select(out=mgt[:], in_=ones_f[:], pattern=[[-1, 128]],
                                                     compare_op=ALU.is_gt, fill=0.0, base=0, channel_multiplier=1),
             reads=[bc], writes=[bc])
        m01 = k.sb("m01", [128, 4, 512], F32)
        negm = k.sb("negm", [128, 4, 512], F32)
        for jl in range(4):
            k.op("pool", lambda: nc.gpsimd.affine_select(out=m01[:, jl, :], in_=ones_w[:], pattern=[[-1, 512]],
                                                         compare_op=ALU.is_gt, fill=0.0, base=jl * 128, channel_multiplier=1),
                 reads=[bc], writes=[bc])
        k.op("dve", lambda: nc.vector.tensor_scalar(out=negm[:].rearrange("p a b -> p (a b)"), in0=m01[:].rearrange("p a b -> p (a b)"),
                                                   scalar1=-1.0, scalar2=1e30, op0=ALU.add, op1=ALU.mult), reads=[bc], writes=[bc])

        pQ = k.ps("pQ", [128, 512], F32)
        pV = k.ps("pV", [128, 512], F32)
        pS = [k.ps(f"pS{i}", [128, 512], F32) for i in range(2)]
        pT = [k.ps(f"pT{i}", [128, 1024], BF16) for i in range(2)]
        pO = k.ps("pO", [128, 512], F32)
        bpQ, bpV, bpO = (k.pbuf(n) for n in ("pQ", "pV", "pO"))
        bpS = [k.pbuf() for _ in range(2)]
        bpT = [k.pbuf() for _ in range(2)]

        xTb = k.sb("xTb", [128, 16, 512], F32)
        sqb = [k.sb(f"sqb{i}", [128, 512], BF16) for i in range(2)]
        hT = k.sb("hT", [128, 16, 512], BF16)
        rstd = k.sb("rstd", [128, 512], F32)
        W = k.sb("W", [128, 16, 512], BF16)
        kT_all = k.sb("kT_all", [128, SEQ], BF16)
        v_all = k.sb("v_all", [128, SEQ // 128, 128], BF16)
        qT = k.sb("qT", [128, 512], BF16)
        kT_s = k.sb("kT_s", [128, 512], BF16)
        v_s = k.sb("v_s", [16, 128], BF16)
        szT = k.sb("szT", [128, 512], F32)
        kf = k.sb("kf", [128, 512], F32)
        vf = [k.sb(f"vf{i}", [128, 128], F32) for i in range(2)]
        kc = k.sb("kc", [128, PAST], BF16)
        vc = k.sb("vc", [128, PAST // 128, 128], BF16)
        ef = [k.sb(f"ef{i}", [128, 512], F32) for i in range(2)]
        spf = [k.sb(f"spf{i}", [128, 512], F32) for i in range(2)]
        t1 = [k.sb(f"t1{i}", [128, 512], F32) for i in range(2)]
        t2 = [k.sb(f"t2{i}", [128, 512], F32) for i in range(2)]
        wT = [k.sb(f"wT{i}", [128, 512], BF16) for i in range(2)]
        wTt = [k.sb(f"wTt{i}", [128, 512], BF16) for i in range(2)]
        b_wTt = [k.buf() for _ in range(2)]
        carry = [k.sb(f"carry{i}", [128, 2], F32) for i in range(2)]
        ogf = [k.sb(f"ogf{i}", [128, 512], F32) for i in range(2)]
        (b_xTb, b_hT, b_rstd, b_W, b_kT, b_v, b_qT, b_kTs, b_vs, b_szT, b_kf, b_kc, b_vc, b_out) = (k.buf() for _ in range(14))
        b_sqb = [k.buf() for _ in range(2)]
        b_vf = [k.buf() for _ in range(2)]
        b_ef = [k.buf() for _ in range(2)]
        b_spf = [k.buf() for _ in range(2)]
        b_t1 = [k.buf() for _ in range(2)]
        b_t2 = [k.buf() for _ in range(2)]
        b_wT = [k.buf() for _ in range(2)]
        b_carry = [k.buf() for _ in range(2)]
        b_ogf = [k.buf() for _ in range(2)]
        tc = [0]
        cc = [0]
        oc = [0]

        def attend(q_ap, T, tiles):
            nt_ = len(tiles)
            base = tc[0]
            tc[0] += nt_
            c_first = cc[0] % 2
            cc[0] += nt_
            k.op("pool", lambda: nc.gpsimd.memset(carry[c_first][:, :], 0.0), writes=[b_carry[c_first]])

            def stage_a(idx):
                k_ap, vs, J, m_ap, n_ap, rds = tiles[idx]
                i = (base + idx) % 2
                k.op("pe", lambda: nc.tensor.matmul(pS[i][:T, 0:J], lhsT=q_ap, rhs=k_ap, start=True, stop=True),
                     reads=rds + [b_qT], writes=[bpS[i]])
                k.op("act", lambda: nc.scalar.activation(out=ef[i][:T, 0:J], in_=pS[i][:T, 0:J], func=AF.Exp),
                     reads=[bpS[i]], writes=[b_ef[i]])
                k.op("act", lambda: nc.scalar.activation(out=spf[i][:T, 0:J], in_=ef[i][:T, 0:J], func=AF.Ln, bias=1.0),
                     reads=[b_ef[i]], writes=[b_spf[i]])
                if m_ap is not None:
                    k.op("pool", lambda: nc.gpsimd.tensor_tensor(out=spf[i][:T, 0:J], in0=spf[i][:T, 0:J], in1=m_ap, op=ALU.mult),
                         reads=[b_spf[i], bc], writes=[b_spf[i]])

            def stage_b(idx):
                k_ap, vs, J, m_ap, n_ap, rds = tiles[idx]
                i = (base + idx) % 2
                ci = (c_first + idx) % 2
                cn = 1 - ci
                k.op("dve", lambda: nc.vector.tensor_tensor_scan(out=t2[i][:T, 0:J], data0=ones_w[:T, 0:J], data1=spf[i][:T, 0:J],
                                                                 initial=0.0, op0=ALU.mult, op1=ALU.add),
                     reads=[b_spf[i], bc], writes=[b_t2[i]])
                k.op("dve", lambda: nc.vector.tensor_tensor(out=carry[cn][:T, 0:1], in0=carry[ci][:T, 0:1], in1=t2[i][:T, J - 1:J],
                                                           op=ALU.subtract), reads=[b_carry[ci], b_t2[i]], writes=[b_carry[cn]])
                k.op("dve", lambda: nc.vector.tensor_tensor(out=t1[i][:T, 0:J], in0=pS[i][:T, 0:J], in1=spf[i][:T, 0:J], op=ALU.subtract),
                     reads=[bpS[i], b_spf[i]], writes=[b_t1[i]])
                k.op("dve", lambda: nc.vector.tensor_tensor(out=t2[i][:T, 0:J], in0=t2[i][:T, 0:J], in1=t1[i][:T, 0:J], op=ALU.add),
                     reads=[b_t2[i], b_t1[i]], writes=[b_t2[i]])
                if n_ap is not None:
                    k.op("pool", lambda: nc.gpsimd.tensor_tensor(out=t2[i][:T, 0:J], in0=t2[i][:T, 0:J], in1=n_ap, op=ALU.add),
                         reads=[b_t2[i], bc], writes=[b_t2[i]])
                k.op("act", lambda: nc.scalar.activation(out=wT[i][:T, 0:J], in_=t2[i][:T, 0:J], func=AF.Exp,
                                                         bias=carry[cn][:T, 0:1]),
                     reads=[b_t2[i], b_carry[cn]], writes=[b_wT[i]])
                for jb, (v_ap, jn) in enumerate(vs):
                    k.op("pe", lambda: nc.tensor.transpose(out=pT[i][:jn, jb * 128:jb * 128 + T], in_=wT[i][:T, jb * 128:jb * 128 + jn],
                                                           identity=ident[:T, :T]),
                         reads=[b_wT[i], b_id], writes=[bpT[i]])

            def stage_c(idx):
                k_ap, vs, J, m_ap, n_ap, rds = tiles[idx]
                i = (base + idx) % 2
                jn0 = vs[0][1]
                nb_ = len(vs)
                if i == 0:
                    k.op("act", lambda: nc.scalar.copy(out=wTt[i][:jn0, 0:nb_ * 128], in_=pT[i][:jn0, 0:nb_ * 128]),
                         reads=[bpT[i]], writes=[b_wTt[i]])
                else:
                    k.op("dve", lambda: nc.vector.tensor_copy(out=wTt[i][:jn0, 0:nb_ * 128], in_=pT[i][:jn0, 0:nb_ * 128]),
                         reads=[bpT[i]], writes=[b_wTt[i]])
                for jb, (v_ap, jn) in enumerate(vs):
                    first = (idx == 0 and jb == 0)
                    last = (idx == nt_ - 1 and jb == nb_ - 1)
                    k.op("pe", lambda: nc.tensor.matmul(pO[:, 0:T], lhsT=v_ap, rhs=wTt[i][:jn, jb * 128:jb * 128 + T],
                                                        start=first, stop=last),
                         reads=rds + [b_wTt[i]], writes=[bpO])

            stage_a(0)
            for idx in range(nt_):
                if idx + 1 < nt_:
                    stage_a(idx + 1)
                stage_b(idx)
                if idx >= 1:
                    stage_c(idx - 1)
            stage_c(nt_ - 1)

        def emit_og(hh, N, col0, sz_ap):
            j = oc[0] % 2
            oc[0] += 1
            k.op("dve", lambda: nc.vector.tensor_tensor(out=ogf[j][:, 0:N], in0=pO[:, 0:N], in1=sz_ap, op=ALU.mult),
                 reads=[bpO, b_szT], writes=[b_ogf[j]])
            k.dma("sp", ogT[hh * 128:(hh + 1) * 128, col0:col0 + N], ogf[j][:, 0:N], reads=[b_ogf[j]], writes=[b_out])

        nblk = NTOK // 512
        for hh in range(dbg_heads):
            for c4 in range(4):
                k.dma("poolq", W[:, c4 * 4:(c4 + 1) * 4, :], w_att[hh, c4 * 512:(c4 + 1) * 512, :].rearrange("(kt p) c -> p kt c", p=128),
                      writes=[b_W])
            for bi in (range(nblk) if dbg_blocks is None else dbg_blocks):
                c0 = bi * 512
                samp = bi >= SEQ // 512
                for c4 in range(4):
                    k.dma("sp", xTb[:, c4 * 4:(c4 + 1) * 4, :],
                          xT[c4 * 512:(c4 + 1) * 512, c0:c0 + 512].rearrange("(kt p) t -> p kt t", p=128), writes=[b_xTb])
                for kt in range(16):
                    j = kt % 2
                    k.op("act", lambda: nc.scalar.activation(out=sqb[j][:], in_=xTb[:, kt, :], func=AF.Square),
                         reads=[b_xTb], writes=[b_sqb[j]])
                    k.op("pe", lambda: nc.tensor.matmul(pQ[:, :], lhsT=ones_b[:], rhs=sqb[j][:], start=(kt == 0), stop=(kt == 15)),
                         reads=[b_sqb[j], bc], writes=[bpQ])
                k.op("dve", lambda: nc.vector.tensor_scalar(out=rstd[:], in0=pQ[:, :], scalar1=1.0 / D, scalar2=EPS,
                                                           op0=ALU.mult, op1=ALU.add), reads=[bpQ], writes=[b_rstd])
                k.op("act", lambda: nc.scalar.activation(out=rstd[:], in_=rstd[:], func=AF.Sqrt), reads=[b_rstd], writes=[b_rstd])
                k.op("dve", lambda: nc.vector.reciprocal(out=rstd[:], in_=rstd[:]), reads=[b_rstd], writes=[b_rstd])
                for kt in range(16):
                    k.op("dve", lambda: nc.vector.scalar_tensor_tensor(out=hT[:, kt, :], in0=xTb[:, kt, :], scalar=nwT[:, kt:kt + 1],
                                                                       in1=rstd[:], op0=ALU.mult, op1=ALU.mult),
                         reads=[b_xTb, b_rstd, bc], writes=[b_hT])
                import os
                LVL = int(os.environ.get("ATT_LVL", "9"))
                if LVL < 2:
                    continue
                for kt in range(16):
                    k.op("pe", lambda: nc.tensor.matmul(pQ[:, :], lhsT=W[:, kt, 0:128], rhs=hT[:, kt, :], start=(kt == 0), stop=(kt == 15)),
                         reads=[b_W, b_hT], writes=[bpQ], sig=(kt == 15))
                k.op("act", lambda: nc.scalar.activation(out=qT[:], in_=pQ[:, :], func=AF.Identity, scale=SCALE), reads=[bpQ], writes=[b_qT])
                if LVL < 3:
                    continue
                for kt in range(16):
                    k.op("pe", lambda: nc.tensor.matmul(pV[:, :], lhsT=W[:, kt, 128:256], rhs=hT[:, kt, :], start=(kt == 0), stop=(kt == 15)),
                         reads=[b_W, b_hT], writes=[bpV], sig=(kt == 15))
                if samp:
                    k.op("act", lambda: nc.scalar.copy(out=kT_s[:], in_=pV[:, :]), reads=[bpV], writes=[b_kTs])
                else:
                    k.op("act", lambda: nc.scalar.copy(out=kT_all[:, c0:c0 + 512], in_=pV[:, :]), reads=[bpV], writes=[b_kT])
                k.op("dve", lambda: nc.vector.tensor_copy(out=kf[:], in_=pV[:, :]), reads=[bpV], writes=[b_kf])
                if os.environ.get("ATT_NOKDMA") is None:
                    k.dma("sp", kTo[hh, :, c0:c0 + 512], kf[:], reads=[b_kf], writes=[b_out])
                if LVL < 4:
                    continue
                for kt in range(16):
                    k.op("pe", lambda: nc.tensor.matmul(pQ[:, :], lhsT=W[:, kt, 384:512], rhs=hT[:, kt, :], start=(kt == 0), stop=(kt == 15)),
                         reads=[b_W, b_hT], writes=[bpQ], sig=(kt == 15))
                k.op("act", lambda: nc.scalar.activation(out=szT[:], in_=pQ[:, :], func=AF.Silu), reads=[bpQ], writes=[b_szT])
                if LVL < 5:
                    continue
                if not samp:
                    for ti in range(4):
                        j = ti % 2
                        for kt in range(16):
                            k.op("pe", lambda: nc.tensor.matmul(pV[:, 0:128], lhsT=hT[:, kt, ti * 128:(ti + 1) * 128], rhs=W[:, kt, 256:384],
                                                                start=(kt == 0), stop=(kt == 15)),
                                 reads=[b_W, b_hT], writes=[bpV], sig=(kt == 15))
                        k.op("act", lambda: nc.scalar.copy(out=v_all[:, bi * 4 + ti, :], in_=pV[:, 0:128]), reads=[bpV], writes=[b_v])
                        k.op("dve", lambda: nc.vector.tensor_copy(out=vf[j][:], in_=pV[:, 0:128]), reads=[bpV], writes=[b_vf[j]])
                        k.dma("sp", vo[hh, c0 + ti * 128:c0 + (ti + 1) * 128, :], vf[j][:], reads=[b_vf[j]], writes=[b_out])
                    for qi in range(4):
                        tiles = []
                        for g in range(bi, -1, -1):
                            vs = [(v_all[:, g * 4 + jb, :], 128) for jb in range(4)]
                            if g == bi:
                                tiles.append((kT_all[:, g * 512:(g + 1) * 512], vs, 512, m01[:, qi, :], negm[:, qi, :], [b_kT, b_v]))
                            else:
                                tiles.append((kT_all[:, g * 512:(g + 1) * 512], vs, 512, None, None, [b_kT, b_v]))
                        if dbg_attend:
                            attend(qT[:, qi * 128:(qi + 1) * 128], 128, tiles)
                            emit_og(hh, 128, c0 + qi * 128, szT[:, qi * 128:(qi + 1) * 128])
                else:
                    for b in range(NSB):
                        j = b % 2
                        for kt in range(16):
                            k.op("pe", lambda: nc.tensor.matmul(pV[:16, 0:128], lhsT=hT[:, kt, b * 16:(b + 1) * 16], rhs=W[:, kt, 256:384],
                                                                start=(kt == 0), stop=(kt == 15)),
                                 reads=[b_W, b_hT], writes=[bpV], sig=(kt == 15))
                        k.op("act", lambda: nc.scalar.copy(out=v_s[:, :], in_=pV[:16, 0:128]), reads=[bpV], writes=[b_vs])
                        k.op("dve", lambda: nc.vector.tensor_copy(out=vf[j][:16, :], in_=pV[:16, 0:128]), reads=[bpV], writes=[b_vf[j]])
                        k.dma("sp", vo[hh, c0 + b * 16:c0 + (b + 1) * 16, :], vf[j][:16, :], reads=[b_vf[j]], writes=[b_out])
                        k.dma("poolq", kc[:], ckT[b, hh, :, :], writes=[b_kc])
                        k.dma("poolq", vc[:], cv_[b, hh, :, :].rearrange("(a p) d -> p a d", p=128), writes=[b_vc])
                        tiles = [(kT_s[:, b * 16:(b + 1) * 16], [(v_s[:, :], 16)], 16, m01[:16, 0, 0:16], negm[:16, 0, 0:16], [b_kTs, b_vs])]
                        for g in range(PAST // 512 - 1, -1, -1):
                            tiles.append((kc[:, g * 512:(g + 1) * 512], [(vc[:, g * 4 + jb, :], 128) for jb in range(4)], 512,
                                          None, None, [b_kc, b_vc]))
                        if dbg_attend:
                            attend(qT[:, b * 16:(b + 1) * 16], 16, tiles)
                            emit_og(hh, 16, c0 + b * 16, szT[:, b * 16:(b + 1) * 16])
        k.finish()
    return nc


def build_oproj(KD):
    nc = bass.Bass("TRN2", target_bir_lowering=False)
    nk = KD // 128
    aT = nc.dram_tensor("aT", [KD, TT], F32, kind="ExternalInput").ap()
    w = nc.dram_tensor("w", [KD, D], F32, kind="ExternalInput").ap()
    x = nc.dram_tensor("x", [TT, D], F32, kind="ExternalInput").ap()
    xo = nc.dram_tensor("xo", [TT, D], F32, kind="ExternalOutput").ap()
    with ExitStack() as st:
        k = K(nc, st)
        pp = [k.ps(f"pp{i}", [128, 512], F32) for i in range(2)]
        bpp = [k.pbuf() for _ in range(2)]
        yT = k.sb("yT", [128, nk, 512], BF16)
        wo = [k.sb(f"wo{i}", [128, nk, 256], BF16) for i in range(2)]
        xc = [k.sb(f"xc{i}", [128, 256], F32) for i in range(2)]
        xn = [k.sb(f"xn{i}", [128, 256], F32) for i in range(2)]
        b_yT = k.buf()
        b_wo = [k.buf() for _ in range(2)]
        b_xc = [k.buf() for _ in range(2)]
        b_xn = [k.buf() for _ in range(2)]
        b_out = k.buf()
        cnt = [0, 0]
        blocks = [(i * 512, 512) for i in range(TP // 512)] + [(TP, TS)]
        for (t0, nt) in blocks:
            ntile = max(1, nt // 128)
            P = min(nt, 128)
            for c8 in range(nk // 8):
                k.dma("poolq", yT[:, c8 * 8:(c8 + 1) * 8, 0:nt],
                      aT[c8 * 1024:(c8 + 1) * 1024, t0:t0 + nt].rearrange("(kt p) t -> p kt t", p=128), writes=[b_yT])
            for oc in range(8):
                i = cnt[0] % 2
                cnt[0] += 1
                for c8 in range(nk // 8):
                    k.dma("poolq", wo[i][:, c8 * 8:(c8 + 1) * 8, :],
                          w[c8 * 1024:(c8 + 1) * 1024, oc * 256:(oc + 1) * 256].rearrange("(kt p) c -> p kt c", p=128),
                          writes=[b_wo[i]])
                for ti in range(ntile):
                    r0 = t0 + ti * 128
                    j = cnt[1] % 2
                    cnt[1] += 1
                    k.dma("sp", xc[j][:P, :], x[r0:r0 + P, oc * 256:(oc + 1) * 256], writes=[b_xc[j]])
                    for ft in range(nk):
                        k.op("pe", lambda: nc.tensor.matmul(pp[j][:P, 0:256], lhsT=yT[:, ft, ti * 128:ti * 128 + P],
                                                            rhs=wo[i][:, ft, :], start=(ft == 0), stop=(ft == nk - 1)),
                             reads=[b_yT, b_wo[i]], writes=[bpp[j]], sig=(ft == nk - 1))
                    k.op("dve", lambda: nc.vector.tensor_tensor(out=xn[j][:P, :], in0=pp[j][:P, 0:256], in1=xc[j][:P, :],
                                                               op=ALU.add), reads=[bpp[j], b_xc[j]], writes=[b_xn[j]])
                    k.dma("sp", xo[r0:r0 + P, oc * 256:(oc + 1) * 256], xn[j][:P, :], reads=[b_xn[j]], writes=[b_out])
        k.finish()
    return nc


def _unshard_tok(res, key):
    xp = np.concatenate([r[key][:TP] for r in res], axis=0)
    xs = np.concatenate([r[key][TP:] for r in res], axis=0)
    return xp, xs


def _featmajor_all(xp, xs):
    return np.ascontiguousarray(np.concatenate([xp, xs.reshape(NSB * NST, -1)], axis=0).T)


def kernel(x_prompt, x_sample, state_ssm, state_conv, cache_k, cache_v, norm_w, final_norm_w,
           a_w_in, a_ln_g, a_ln_b, a_w_s, a_b_s, a_w_out,
           b_w_in, b_conv_w, b_conv_b, b_dt_bias, b_a_log, b_d, b_norm_w, b_w_out,
           c_w_in, c_w_out):
    f = lambda a: np.ascontiguousarray(np.asarray(a, dtype=np.float32))
    xp = f(x_prompt)[0]
    xs = f(x_sample)
    norm_w = f(norm_w)
    res = run_gmlp(xp, xs, norm_w[0], f(a_w_in)[0], f(a_ln_g)[0], f(a_ln_b)[0], f(a_w_s)[0], f(a_b_s)[0], f(a_w_out)[0])
    x1p, x1s = _unshard_tok(res, "xo")
    v0 = np.concatenate([r["vsT"].T for r in res], axis=0).reshape(NSB, NST, AW)
    xT1 = _featmajor_all(x1p, x1s)
    bw = f(b_w_in)[0]
    cw_ = f(b_conv_w)[0]
    cb_ = f(b_conv_b)[0]
    sst = f(state_ssm)[0]
    scv = f(state_conv)[0]
    nwT1 = np.ascontiguousarray(norm_w[1].reshape(16, 128).T)
    in_maps = []
    for g in range(NCORES):
        xcols = np.arange(4096 + g * 512, 4096 + (g + 1) * 512)
        bcols = np.arange(4096 + 4096 + g * 128, 4096 + 4096 + (g + 1) * 128)
        ccols = np.arange(4096 + 4096 + 1024 + g * 128, 4096 + 4096 + 1024 + (g + 1) * 128)
        cols = np.concatenate([xcols, bcols, ccols])
        cch = cols - 4096
        hs = slice(g * 8, (g + 1) * 8)
        m = {
            "xT": xT1, "normwT": nwT1,
            "w_z": np.ascontiguousarray(bw[:, g * 512:(g + 1) * 512]),
            "w_xbc": np.ascontiguousarray(bw[:, cols]),
            "w_dt": np.ascontiguousarray(bw[:, 4096 + 6144 + g * 8:4096 + 6144 + (g + 1) * 8]),
            "convw": np.ascontiguousarray(cw_[:, cch].T.reshape(6, 128, 4).transpose(1, 0, 2)),
            "convb": np.ascontiguousarray(cb_[cch].reshape(6, 128).T),
            "hp": np.ascontiguousarray(np.stack([f(b_dt_bias)[0][hs], f(b_a_log)[0][hs], f(b_d)[0][hs]])),
            "dexp": np.ascontiguousarray(np.repeat(f(b_d)[0][hs], 64)),
            "nw": np.ascontiguousarray(f(b_norm_w)[0][g * 512:(g + 1) * 512]),
            "sT_in": np.ascontiguousarray(sst[:, hs].reshape(NSB, 512, 128).transpose(0, 2, 1)),
            "cs_in": np.ascontiguousarray(scv[:, :, cch].transpose(2, 0, 1).reshape(6, 128, NSB, 3).transpose(1, 0, 2, 3)),
        }
        in_maps.append(m)
    resm = _run(build_mamba(), in_maps)
    ssm_p = np.zeros((1, 1, 64, 64, 128), np.float32)
    conv_p = np.zeros((1, 1, 3, 6144), np.float32)
    ssm_s = np.zeros((1, NSB, 64, 64, 128), np.float32)
    conv_s = np.zeros((1, NSB, 3, 6144), np.float32)
    yn_all = np.zeros((NTOK, 4096), np.float32)
    for g in range(NCORES):
        r = resm[g]
        xcols = np.arange(g * 512, (g + 1) * 512)
        bcols = np.arange(4096 + g * 128, 4096 + (g + 1) * 128)
        ccols = np.arange(4096 + 1024 + g * 128, 4096 + 1024 + (g + 1) * 128)
        cch = np.concatenate([xcols, bcols, ccols])
        ssm_p[0, 0, g * 8:(g + 1) * 8] = r["sT_p"].T.reshape(8, 64, 128)
        conv_p[0, 0][:, cch] = r["cs_p"].transpose(1, 0, 2).reshape(768, 3).T
        ssm_s[0, :, g * 8:(g + 1) * 8] = r["sT_s"].transpose(0, 2, 1).reshape(NSB, 8, 64, 128)
        conv_s[0][:, :, cch] = r["cs_s"].transpose(1, 0, 2, 3).reshape(768, NSB, 3).transpose(1, 2, 0)
        yn_all[:, g * 512:(g + 1) * 512] = r["yn"]
    ynp, yns = yn_all[:SEQ], yn_all[SEQ:].reshape(NSB, NST, 4096)
    in_maps = [{"aT": np.ascontiguousarray(tok_shard(ynp, yns, c).T), "w": f(b_w_out)[0], "x": tok_shard(x1p, x1s.reshape(NSB, NST, D), c)}
               for c in range(NCORES)]
    res = _run(build_oproj(4096), in_maps)
    x2p, x2s = _unshard_tok(res, "xo")
    xT2 = _featmajor_all(x2p, x2s)
    nwT2 = np.ascontiguousarray(norm_w[2].reshape(16, 128).T)
    cwi = f(c_w_in)[0]
    ck = f(cache_k)[0]
    cvv = f(cache_v)[0]
    in_maps = []
    for c in range(NCORES):
        wh = []
        for hh in range(2):
            h = 2 * c + hh
            wh.append(np.concatenate([cwi[:, part * 2048 + h * 128: part * 2048 + (h + 1) * 128] for part in range(4)], axis=1))
        in_maps.append({"xT": xT2, "normwT": nwT2, "w_att": np.ascontiguousarray(np.stack(wh)),
                        "ckT": np.ascontiguousarray(ck[:, 2 * c:2 * c + 2].transpose(0, 1, 3, 2)),
                        "cv": np.ascontiguousarray(cvv[:, 2 * c:2 * c + 2])})
    resa = _run(build_attn(), in_maps)
    k_all = np.zeros((16, NTOK, 128), np.float32)
    v_all = np.zeros((16, NTOK, 128), np.float32)
    og_all = np.zeros((NTOK, 2048), np.float32)
    for c in range(NCORES):
        r = resa[c]
        k_all[2 * c:2 * c + 2] = r["kTo"].transpose(0, 2, 1)
        v_all[2 * c:2 * c + 2] = r["vo"]
        og_all[:, c * 256:(c + 1) * 256] = r["ogT"].T
    k_p = k_all[:, :SEQ][None, None]
    v_p = v_all[:, :SEQ][None, None]
    k_s = np.ascontiguousarray(k_all[:, SEQ:].reshape(16, NSB, NST, 128).transpose(1, 0, 2, 3))[None]
    v_s = np.ascontiguousarray(v_all[:, SEQ:].reshape(16, NSB, NST, 128).transpose(1, 0, 2, 3))[None]
    ogp, ogs = og_all[:SEQ], og_all[SEQ:].reshape(NSB, NST, 2048)
    res = run_gmlp(x2p, x2s.reshape(NSB, NST, D), norm_w[3], f(a_w_in)[1], f(a_ln_g)[1], f(a_ln_b)[1], f(a_w_s)[1], f(a_b_s)[1],
                   f(a_w_out)[1], f(final_norm_w), pre=(ogp, ogs, f(c_w_out)[0]))
    yp, ys = _unshard_tok(res, "yo")
    v1 = np.concatenate([r["vsT"].T for r in res], axis=0).reshape(NSB, NST, AW)
    return (np.ascontiguousarray(yp)[None], np.ascontiguousarray(ys).reshape(NSB, NST, D),
            np.ascontiguousarray(np.stack([v0, v1])), ssm_p, conv_p, ssm_s, conv_s,
            np.ascontiguousarray(k_p), np.ascontiguousarray(v_p), k_s, v_s)
```

```python
import numpy as np
from contextlib import ExitStack
import concourse.bass as bass
import concourse.mybir as mybir
from concourse.bass_utils import run_bass_kernel_spmd

F32 = mybir.dt.float32
BF16 = mybir.dt.bfloat16
AF = mybir.ActivationFunctionType
ALU = mybir.AluOpType
AX = mybir.AxisListType

NCORES = 8
D = 2048
SEQ = 16384
NSB = 32
NST = 16
TP = SEQ // NCORES
TS = NSB * NST // NCORES
TT = TP + TS
AW = 4096
EPS = 1e-6
DMA_RING = 12
EPOCH = 6000


class Buf:
    __slots__ = ("name", "w", "r", "excl")

    def __init__(self, name, excl=False):
        self.name = name
        self.w = None
        self.r = []
        self.excl = excl


class K:
    def __init__(self, nc, stack):
        self.nc = nc
        self.eng = {"pe": nc.tensor, "act": nc.scalar, "dve": nc.vector,
                    "pool": nc.gpsimd, "sp": nc.sync}
        self.sems = {}
        self.cnt = {}
        self.stack = stack
        self.epoch = {}
        for e in ("pe", "act", "dve", "pool"):
            self.sems[(e, 0)] = stack.enter_context(nc.semaphore("s_" + e + "_0"))
            self.cnt[e] = 0
            self.epoch[e] = 0
        self.dq = {}
        for q, e in (("sp", "sp"), ("poolq", "pool")):
            ring = [stack.enter_context(nc.semaphore(f"d_{q}_{i}")) for i in range(DMA_RING)]
            self.dq[q] = {"eng": e, "ring": ring, "n": 0}
            for i in range(DMA_RING):
                self.sems[(q, i)] = ring[i]
        self.waited = {}
        self.pending = {e: [] for e in ("pe", "act", "dve", "pool")}
        self.nb = 0
        self.ninstr = 0
        self.allbufs = []

    def sb(self, name, shape, dt):
        return self.nc.alloc_sbuf_tensor(name, list(shape), dt)

    def ps(self, name, shape, dt=F32):
        return self.nc.alloc_psum_tensor(name, list(shape), dt)

    def buf(self, name=None, excl=False):
        self.nb += 1
        b = Buf(name or f"b{self.nb}", excl)
        self.allbufs.append(b)
        return b

    def pbuf(self, name=None):
        return self.buf(name, excl=True)

    def _wait(self, engkey, ev):
        semkey, val, src = ev
        if src == engkey and engkey == "pe":
            return
        kk = (engkey, semkey)
        if self.waited.get(kk, 0) >= val:
            return
        self.waited[kk] = val
        self.eng[engkey].wait_ge(self.sems[semkey], val)
        self.ninstr += 1

    def _deps(self, engkey, reads, writes):
        for b in reads:
            if b.w is not None:
                self._wait(engkey, b.w)
            if b.excl:
                for ev in b.r:
                    if ev[2] != engkey:
                        self._wait(engkey, ev)
        for b in writes:
            if b.w is not None:
                self._wait(engkey, b.w)
            for ev in b.r:
                self._wait(engkey, ev)

    def _record(self, ev, reads, writes):
        for b in reads:
            b.r.append(ev)
            if len(b.r) > 16:
                d = {}
                for e in b.r:
                    if e[0] not in d or d[e[0]][1] < e[1]:
                        d[e[0]] = e
                b.r = list(d.values())
        for b in writes:
            b.w = ev
            b.r = []

    def op(self, engkey, fn, reads=(), writes=(), sig=True):
        reads = [b for b in reads if b is not None]
        writes = [b for b in writes if b is not None]
        self._deps(engkey, reads, writes)
        ins = fn()
        self.ninstr += 1
        if sig:
            if self.cnt[engkey] >= EPOCH:
                self.epoch[engkey] += 1
                self.cnt[engkey] = 0
                self.sems[(engkey, self.epoch[engkey])] = self.stack.enter_context(
                    self.nc.semaphore(f"s_{engkey}_{self.epoch[engkey]}"))
            self.cnt[engkey] += 1
            sk = (engkey, self.epoch[engkey])
            ins.then_inc(self.sems[sk], 1)
            ev = (sk, self.cnt[engkey], engkey)
            pr = self.pending[engkey]
            if pr:
                for b in pr:
                    b.r.append(ev)
                self.pending[engkey] = []
            self._record(ev, reads, writes)
        else:
            self.pending[engkey].extend(reads)
            self.pending[engkey].extend(writes)
        return ins

    def dma(self, q, out, in_, reads=(), writes=(), **kw):
        reads = [b for b in reads if b is not None]
        writes = [b for b in writes if b is not None]
        d = self.dq[q]
        engkey = d["eng"]
        i = d["n"]
        slot = i % DMA_RING
        rnd = i // DMA_RING
        if rnd > 0:
            self._wait(engkey, ((q, slot), 16 * rnd, q))
        self._deps(engkey, reads, writes)
        ins = self.eng[engkey].dma_start(out=out, in_=in_, **kw)
        ins.then_inc(d["ring"][slot], 16)
        self.ninstr += 1
        d["n"] = i + 1
        ev = ((q, slot), 16 * (rnd + 1), q)
        self._record(ev, reads, writes)
        return ev

    def finish(self):
        for b in self.allbufs:
            if b.w is not None:
                self._wait("sp", b.w)
            for ev in b.r:
                self._wait("sp", ev)
        for q, d in self.dq.items():
            n = d["n"]
            for i in range(max(0, n - DMA_RING), n):
                self._wait("sp", ((q, i % DMA_RING), 16 * (i // DMA_RING + 1), q))


def make_ident(k, nc):
    identf = k.sb("identf", [128, 128], F32)
    ident = k.sb("identb", [128, 128], BF16)
    bi = k.buf("ident")
    k.op("pool", lambda: nc.gpsimd.memset(identf[:], 0.0), writes=[bi])
    k.op("pool", lambda: nc.gpsimd.affine_select(out=identf[:], in_=identf[:], pattern=[[-1, 128]],
                                                 compare_op=ALU.not_equal, fill=1.0, base=0,
                                                 channel_multiplier=1), reads=[bi], writes=[bi])
    k.op("dve", lambda: nc.vector.tensor_copy(out=ident[:], in_=identf[:]), reads=[bi], writes=[bi])
    return identf, ident, bi


def rstd_from_sumsq(k, nc, out, ssq, n, rb, wb, np_=128):
    k.op("dve", lambda: nc.vector.tensor_scalar(out=out, in0=ssq, scalar1=1.0 / n, scalar2=EPS,
                                               op0=ALU.mult, op1=ALU.add), reads=rb, writes=wb)
    k.op("act", lambda: nc.scalar.activation(out=out, in_=out, func=AF.Sqrt), reads=wb, writes=wb)
    k.op("dve", lambda: nc.vector.reciprocal(out=out, in_=out), reads=wb, writes=wb)


def build_gmlp(final_norm, pre_oproj=False):
    nc = bass.Bass("TRN2", target_bir_lowering=False)
    x = nc.dram_tensor("x", [TT, D], F32, kind="ExternalInput").ap()
    if pre_oproj:
        aT_in = nc.dram_tensor("aT", [D, TT], F32, kind="ExternalInput").ap()
        w_o = nc.dram_tensor("w_o", [D, D], F32, kind="ExternalInput").ap()
        x3 = nc.dram_tensor("x3_scratch", [TT, D], F32).ap()
    normw = nc.dram_tensor("normw", [D], F32, kind="ExternalInput").ap()
    w_in = nc.dram_tensor("w_in", [D, 3 * AW], F32, kind="ExternalInput").ap()
    ln_g = nc.dram_tensor("ln_g", [AW], F32, kind="ExternalInput").ap()
    ln_b = nc.dram_tensor("ln_b", [AW], F32, kind="ExternalInput").ap()
    w_s = nc.dram_tensor("w_s", [16, 128, 128], F32, kind="ExternalInput").ap()
    b_s = nc.dram_tensor("b_s", [16, 128], F32, kind="ExternalInput").ap()
    w_out = nc.dram_tensor("w_out", [AW, D], F32, kind="ExternalInput").ap()
    xo = nc.dram_tensor("xo", [TT, D], F32, kind="ExternalOutput").ap()
    vsT = nc.dram_tensor("vsT", [AW, TS], F32, kind="ExternalOutput").ap()
    if final_norm:
        fnw = nc.dram_tensor("fnw", [D], F32, kind="ExternalInput").ap()
        yo = nc.dram_tensor("yo", [TT, D], F32, kind="ExternalOutput").ap()

    with ExitStack() as st:
        k = K(nc, st)
        identf, ident, b_id = make_ident(k, nc)
        normw_bc = k.sb("normw_bc", [128, D], F32)
        b_const = k.buf("const")
        k.dma("sp", normw_bc[:], normw.partition_broadcast(128), writes=[b_const])
        if final_norm:
            fnw_bc = k.sb("fnw_bc", [128, D], F32)
            k.dma("sp", fnw_bc[:], fnw.partition_broadcast(128), writes=[b_const])
        ones_b = k.sb("ones_b", [128, 128], BF16)
        k.op("dve", lambda: nc.vector.memset(ones_b[:], 1.0), writes=[b_const])
        lnrow = k.sb("lnrow", [32, 2, 128], F32)
        k.dma("sp", lnrow[:, 0, :], ln_g.rearrange("(a p) -> a p", p=128), writes=[b_const])
        k.dma("sp", lnrow[:, 1, :], ln_b.rearrange("(a p) -> a p", p=128), writes=[b_const])
        lnT = k.sb("lnT", [128, 2, 32], F32)
        pA = k.ps("pA", [128, 512], F32)
        pB = k.ps("pB", [128, 512], F32)
        pC = k.ps("pC", [128, 512], F32)
        pD = k.ps("pD", [128, 512], F32)
        pT = k.ps("pT", [128, 1024], BF16)
        bpA, bpB, bpC, bpD, bpT = (k.pbuf(n) for n in ("pA", "pB", "pC", "pD", "pT"))
        for j in range(2):
            k.op("pe", lambda: nc.tensor.transpose(out=pA[:, j * 32:(j + 1) * 32], in_=lnrow[:, j, :],
                                                   identity=identf[0:32, 0:32]),
                 reads=[b_const, b_id], writes=[bpA])
        k.op("dve", lambda: nc.vector.tensor_copy(out=lnT[:].rearrange("p a b -> p (a b)"), in_=pA[:, 0:64]),
             reads=[bpA], writes=[b_const])
        wnat = [k.sb(f"wnat{i}", [128, 128], F32) for i in range(2)]
        b_wnat = [k.buf() for _ in range(2)]
        wposT = k.sb("wposT", [128, 16, 128], BF16)
        for g in range(16):
            pp = pA if g % 2 == 0 else pB
            bpp = bpA if g % 2 == 0 else bpB
            k.dma("sp", wnat[g % 2][:], w_s[g, :, :], writes=[b_wnat[g % 2]])
            k.op("pe", lambda: nc.tensor.transpose(out=pp[:, 0:128], in_=wnat[g % 2][:], identity=identf[:]),
                 reads=[b_wnat[g % 2], b_id], writes=[bpp])
            k.op("act", lambda: nc.scalar.copy(out=wposT[:, g, :], in_=pp[:, 0:128]), reads=[bpp], writes=[b_const])
        k.op("dve", lambda: nc.vector.memset(wposT[64:128, :, 0:64], 0.0), reads=[b_const], writes=[b_const])
        wposT_s = k.sb("wposT_s", [64, 16, 64], BF16)
        for g in range(16):
            pp = pA if g % 2 == 0 else pB
            bpp = bpA if g % 2 == 0 else bpB
            k.op("dve", lambda: nc.vector.memset(wnat[g % 2][:], 0.0), writes=[b_wnat[g % 2]])
            for b in range(4):
                k.dma("sp", wnat[g % 2][16 * b:16 * b + 16, 16 * b:16 * b + 16], w_s[g, 0:16, 0:16],
                      writes=[b_wnat[g % 2]])
            k.op("pe", lambda: nc.tensor.transpose(out=pp[0:64, 0:64], in_=wnat[g % 2][0:64, 0:64], identity=identf[0:64, 0:64]),
                 reads=[b_wnat[g % 2], b_id], writes=[bpp])
            k.op("act", lambda: nc.scalar.copy(out=wposT_s[:, g, :], in_=pp[0:64, 0:64]), reads=[bpp], writes=[b_const])
        bsbt = [k.sb(f"bsb{i}", [128, 128], F32) for i in range(2)]
        b_bsb = [k.buf() for _ in range(2)]
        biasT = k.sb("biasT", [128, 32, 128], F32)
        biasT_s = k.sb("biasT_s", [128, 32, 64], F32)
        for g in range(16):
            k.dma("sp", bsbt[g % 2][:], b_s[g, :].partition_broadcast(128), writes=[b_bsb[g % 2]])
            k.op("pe", lambda: nc.tensor.matmul(pC[:, 0:128], lhsT=ones_b[:], rhs=wposT[:, g, :], start=True, stop=True),
                 reads=[b_const], writes=[bpC])
            k.op("pe", lambda: nc.tensor.matmul(pC[:, 128:192], lhsT=ones_b[0:64, :], rhs=wposT_s[:, g, :], start=True, stop=True),
                 reads=[b_const], writes=[bpC])
            for j in range(2):
                ft = 2 * g + j
                k.op("dve", lambda: nc.vector.scalar_tensor_tensor(out=biasT[:, ft, :], in0=pC[:, 0:128],
                                                                   scalar=lnT[:, 1, ft:ft + 1], in1=bsbt[g % 2][:],
                                                                   op0=ALU.mult, op1=ALU.add),
                     reads=[bpC, b_const, b_bsb[g % 2]], writes=[b_const])
                for b in range(4):
                    k.op("dve", lambda: nc.vector.scalar_tensor_tensor(out=biasT_s[:, ft, 16 * b:16 * b + 16],
                                                                       in0=pC[:, 128 + 16 * b:128 + 16 * b + 16],
                                                                       scalar=lnT[:, 1, ft:ft + 1], in1=bsbt[g % 2][:, 0:16],
                                                                       op0=ALU.mult, op1=ALU.add),
                         reads=[bpC, b_const, b_bsb[g % 2]], writes=[b_const])

        TB = 512
        xt = k.sb("xt", [128, D], F32)
        hb = k.sb("hb", [128, D], BF16)
        hT = k.sb("hT", [128, 16, TB], BF16)
        gv = k.sb("gv", [128, 4, AW], BF16)
        yT = k.sb("yT", [128, 32, TB], BF16)
        wsl = [k.sb(f"wsl{i}", [128, 16, 512], BF16) for i in range(2)]
        wo = [k.sb(f"wo{i}", [128, 32, 256], BF16) for i in range(1)] * 2
        gvf = [k.sb(f"gvf{i}", [128, 512], F32) for i in range(2)]
        sqf = [k.sb(f"sqf{i}", [128, 512], F32) for i in range(2)]
        guf = gvf
        szf = sqf
        ssf = [k.sb(f"ssf{i}", [128, 512], F32) for i in range(2)]
        stats = k.sb("stats", [128, 4, 2, 8], F32)
        st2 = k.sb("st2", [128, 4, 8], F32)
        small = k.sb("small", [128, 8], F32)
        xc = [k.sb(f"xc{i}", [128, 256], F32) for i in range(2)]
        xn = [k.sb(f"xn{i}", [128, 256], F32) for i in range(2)]
        b_xt, b_hb, b_hT, b_gv, b_yT, b_stats, b_small = (k.buf(n) for n in
                                                          ("xt", "hb", "hT", "gv", "yT", "stats", "small"))
        b_wsl = [k.buf(f"wsl{i}") for i in range(2)]
        b_wo = [k.buf("wo0")] * 2
        b_gvf = [k.buf() for _ in range(2)]
        b_sqf = [k.buf() for _ in range(2)]
        b_guf = b_gvf
        b_szf = b_sqf
        b_ssf = [k.buf() for _ in range(2)]
        b_xc = [k.buf() for _ in range(2)]
        b_xn = [k.buf() for _ in range(2)]
        b_xo = k.buf("xo_dram")
        cnt = {"w": 0, "z": 0, "o": 0, "t": 0, "c": 0}

        wc_in = nc.dram_tensor("wc_in", [24, 128, 16 * 512], BF16).ap()
        wc_out = nc.dram_tensor("wc_out", [8, 128, 32 * 256], BF16).ap()
        b_wcin = [k.buf() for _ in range(24)]
        b_wcout = [k.buf() for _ in range(8)]
        cur_blk = [0]

        def load_slab(dst, bdst, col0):
            si = col0 // 512
            flat = dst[:].rearrange("p a b -> p (a b)")
            if cur_blk[0] == 0:
                for c4 in range(4):
                    k.dma("poolq", dst[:, c4 * 4:(c4 + 1) * 4, :],
                          w_in[c4 * 512:(c4 + 1) * 512, col0:col0 + 512].rearrange("(kt p) c -> p kt c", p=128),
                          writes=[bdst])
                k.dma("sp", wc_in[si, :, :], flat, reads=[bdst], writes=[b_wcin[si]])
            else:
                k.dma("sp", flat, wc_in[si, :, :], reads=[b_wcin[si]], writes=[bdst])

        blocks = [(i * 512, 512) for i in range(TP // 512)] + [(TP, TS)]
        b_xsrc = k.buf("xsrc")
        if pre_oproj:
            for (t0, nt) in blocks:
                ntile = max(1, nt // 128)
                P = min(nt, 128)
                for c8 in range(2):
                    k.dma("poolq", yT[:, c8 * 8:(c8 + 1) * 8, 0:nt],
                          aT_in[c8 * 1024:(c8 + 1) * 1024, t0:t0 + nt].rearrange("(kt p) t -> p kt t", p=128), writes=[b_yT])
                for oc in range(8):
                    for c8 in range(2):
                        k.dma("poolq", wo[0][:, c8 * 8:(c8 + 1) * 8, :],
                              w_o[c8 * 1024:(c8 + 1) * 1024, oc * 256:(oc + 1) * 256].rearrange("(kt p) c -> p kt c", p=128),
                              writes=[b_wo[0]])
                    for ti in range(ntile):
                        r0 = t0 + ti * 128
                        j = cnt["t"] % 2
                        cnt["t"] += 1
                        pp, bpp = (pD, bpD) if j == 0 else (pB, bpB)
                        k.dma("sp", xc[j][:P, :], x[r0:r0 + P, oc * 256:(oc + 1) * 256], writes=[b_xc[j]])
                        for ft in range(16):
                            k.op("pe", lambda: nc.tensor.matmul(pp[:P, 0:256], lhsT=yT[:, ft, ti * 128:ti * 128 + P],
                                                                rhs=wo[0][:, ft, :], start=(ft == 0), stop=(ft == 15)),
                                 reads=[b_yT, b_wo[0]], writes=[bpp], sig=(ft == 15))
                        k.op("dve", lambda: nc.vector.tensor_tensor(out=xn[j][:P, :], in0=pp[:P, 0:256], in1=xc[j][:P, :],
                                                                   op=ALU.add), reads=[bpp, b_xc[j]], writes=[b_xn[j]])
                        k.dma("sp", x3[r0:r0 + P, oc * 256:(oc + 1) * 256], xn[j][:P, :], reads=[b_xn[j]], writes=[b_xsrc])
            x = x3
        for bidx, (t0, nt) in enumerate(blocks):
            cur_blk[0] = bidx
            samp = nt == TS
            ntile = max(1, nt // 128)
            P = min(nt, 128)
            for ti in range(ntile):
                r0 = t0 + ti * 128
                k.dma("sp", xt[:P, :], x[r0:r0 + P, :], reads=[b_xsrc], writes=[b_xt])
                for q4 in range(4):
                    k.op("act", lambda: nc.scalar.activation(out=gvf[0][:P, :], in_=xt[:P, q4 * 512:(q4 + 1) * 512],
                                                             func=AF.Square), reads=[b_xt], writes=[b_gvf[0]])
                    k.op("dve", lambda: nc.vector.reduce_sum(out=small[:P, q4:q4 + 1], in_=gvf[0][:P, :], axis=AX.X),
                         reads=[b_gvf[0]], writes=[b_small])
                k.op("dve", lambda: nc.vector.reduce_sum(out=small[:P, 4:5], in_=small[:P, 0:4], axis=AX.X),
                     reads=[b_small], writes=[b_small])
                rstd_from_sumsq(k, nc, small[:P, 5:6], small[:P, 4:5], D, [b_small], [b_small])
                k.op("dve", lambda: nc.vector.scalar_tensor_tensor(out=hb[:P, :], in0=xt[:P, :], scalar=small[:P, 5:6],
                                                                   in1=normw_bc[:P, :], op0=ALU.mult, op1=ALU.mult),
                     reads=[b_xt, b_small, b_const], writes=[b_hb])
                for half in range(2):
                    for j in range(8):
                        kt = half * 8 + j
                        k.op("pe", lambda: nc.tensor.transpose(out=pT[:, j * 128:j * 128 + P], in_=hb[:P, kt * 128:(kt + 1) * 128],
                                                               identity=ident[:P, :P]),
                             reads=[b_hb, b_id], writes=[bpT])
                    k.op("act", lambda: nc.scalar.copy(
                        out=hT[:, half * 8:half * 8 + 8, ti * 128:ti * 128 + P],
                        in_=pT[:].rearrange("p (a b) -> p a b", b=128)[:, :, 0:P]),
                        reads=[bpT], writes=[b_hT])
            for vc in range(8):
                i = cnt["w"] % 2
                cnt["w"] += 1
                load_slab(wsl[i], b_wsl[i], AW + vc * 512)
                for ti in range(ntile):
                    pp, bpp = (pA, bpA) if (cnt["t"] % 2 == 0) else (pB, bpB)
                    j = cnt["t"] % 2
                    cnt["t"] += 1
                    for kt in range(16):
                        k.op("pe", lambda: nc.tensor.matmul(pp[:P, :], lhsT=hT[:, kt, ti * 128:ti * 128 + P],
                                                            rhs=wsl[i][:, kt, :], start=(kt == 0), stop=(kt == 15)),
                             reads=[b_hT, b_wsl[i]], writes=[bpp], sig=(kt == 15))
                    k.op("act", lambda: nc.scalar.activation(out=gvf[j][:P, :], in_=pp[:P, :], func=AF.Gelu),
                         reads=[bpp], writes=[b_gvf[j]])
                    k.op("act", lambda: nc.scalar.activation(out=sqf[j][:P, :], in_=gvf[j][:P, :], func=AF.Square),
                         reads=[b_gvf[j]], writes=[b_sqf[j]])
                    k.op("dve", lambda: nc.vector.reduce_sum(out=stats[:P, ti, 0, vc:vc + 1], in_=gvf[j][:P, :], axis=AX.X),
                         reads=[b_gvf[j]], writes=[b_stats])
                    k.op("dve", lambda: nc.vector.reduce_sum(out=stats[:P, ti, 1, vc:vc + 1], in_=sqf[j][:P, :], axis=AX.X),
                         reads=[b_sqf[j]], writes=[b_stats])
                    k.op("dve", lambda: nc.vector.tensor_copy(out=gv[:P, ti, vc * 512:(vc + 1) * 512], in_=gvf[j][:P, :]),
                         reads=[b_gvf[j]], writes=[b_gv])
            for ti in range(ntile):
                k.op("dve", lambda: nc.vector.reduce_sum(out=st2[:P, ti, 0:1], in_=stats[:P, ti, 0, :], axis=AX.X),
                     reads=[b_stats], writes=[b_stats])
                k.op("dve", lambda: nc.vector.reduce_sum(out=st2[:P, ti, 1:2], in_=stats[:P, ti, 1, :], axis=AX.X),
                     reads=[b_stats], writes=[b_stats])
                k.op("dve", lambda: nc.vector.tensor_scalar(out=st2[:P, ti, 2:3], in0=st2[:P, ti, 0:1], scalar1=1.0 / AW,
                                                           scalar2=None, op0=ALU.mult), reads=[b_stats], writes=[b_stats])
                k.op("dve", lambda: nc.vector.tensor_scalar(out=st2[:P, ti, 3:4], in0=st2[:P, ti, 1:2], scalar1=1.0 / AW,
                                                           scalar2=None, op0=ALU.mult), reads=[b_stats], writes=[b_stats])
                k.op("dve", lambda: nc.vector.tensor_tensor(out=st2[:P, ti, 4:5], in0=st2[:P, ti, 2:3], in1=st2[:P, ti, 2:3],
                                                           op=ALU.mult), reads=[b_stats], writes=[b_stats])
                k.op("dve", lambda: nc.vector.tensor_tensor(out=st2[:P, ti, 5:6], in0=st2[:P, ti, 3:4], in1=st2[:P, ti, 4:5],
                                                           op=ALU.subtract), reads=[b_stats], writes=[b_stats])
                k.op("dve", lambda: nc.vector.tensor_scalar(out=st2[:P, ti, 6:7], in0=st2[:P, ti, 5:6], scalar1=EPS,
                                                           scalar2=None, op0=ALU.add),
                     reads=[b_stats], writes=[b_stats])
                k.op("act", lambda: nc.scalar.activation(out=st2[:P, ti, 6:7], in_=st2[:P, ti, 6:7], func=AF.Sqrt),
                     reads=[b_stats], writes=[b_stats])
                k.op("dve", lambda: nc.vector.reciprocal(out=st2[:P, ti, 6:7], in_=st2[:P, ti, 6:7]),
                     reads=[b_stats], writes=[b_stats])
                k.op("dve", lambda: nc.vector.tensor_scalar(out=gv[:P, ti, :], in0=gv[:P, ti, :], scalar1=st2[:P, ti, 2:3],
                                                           scalar2=st2[:P, ti, 6:7], op0=ALU.subtract, op1=ALU.mult),
                     reads=[b_stats, b_gv], writes=[b_gv])
            if samp:
                for ft in range(32):
                    k.op("pe", lambda: nc.tensor.transpose(out=pT[:, 0:P], in_=gv[:P, 0, ft * 128:(ft + 1) * 128],
                                                           identity=ident[:P, :P]), reads=[b_gv, b_id], writes=[bpT])
                    k.op("act", lambda: nc.scalar.activation(out=ssf[ft % 2][:, 0:P], in_=pT[:, 0:P], func=AF.Identity,
                                                             bias=lnT[:, 1, ft:ft + 1], scale=lnT[:, 0, ft:ft + 1]),
                         reads=[bpT, b_const], writes=[b_ssf[ft % 2]])
                    k.dma("sp", vsT[ft * 128:(ft + 1) * 128, :], ssf[ft % 2][:, 0:P], reads=[b_ssf[ft % 2]], writes=[b_xo])
            for fc in range(8):
                load_slab(wsl[0], b_wsl[0], fc * 512)
                load_slab(wsl[1], b_wsl[1], 2 * AW + fc * 512)
                for fj in range(4):
                    ft = fc * 4 + fj
                    g = ft // 2
                    j = cnt["c"] % 2
                    cnt["c"] += 1
                    for kt in range(16):
                        k.op("pe", lambda: nc.tensor.matmul(pA[:, 0:nt], lhsT=wsl[0][:, kt, fj * 128:(fj + 1) * 128],
                                                            rhs=hT[:, kt, 0:nt], start=(kt == 0), stop=(kt == 15)),
                             reads=[b_hT, b_wsl[0]], writes=[bpA], sig=(kt == 15))
                    for kt in range(16):
                        k.op("pe", lambda: nc.tensor.matmul(pB[:, 0:nt], lhsT=wsl[1][:, kt, fj * 128:(fj + 1) * 128],
                                                            rhs=hT[:, kt, 0:nt], start=(kt == 0), stop=(kt == 15)),
                             reads=[b_hT, b_wsl[1]], writes=[bpB], sig=(kt == 15))
                    for ti in range(ntile):
                        if samp:
                            k.op("pe", lambda: nc.tensor.matmul(pC[:, 0:P], lhsT=gv[:P, 0, ft * 128:(ft + 1) * 128],
                                                                rhs=wposT_s[:, g, :], start=True, stop=True),
                                 reads=[b_gv, b_const], writes=[bpC])
                        else:
                            k.op("pe", lambda: nc.tensor.matmul(pC[:, ti * 128:(ti + 1) * 128],
                                                                lhsT=gv[:, ti, ft * 128:(ft + 1) * 128],
                                                                rhs=wposT[:, g, :], start=True, stop=True),
                                 reads=[b_gv, b_const], writes=[bpC])
                    k.op("act", lambda: nc.scalar.activation(out=guf[j][:, 0:nt], in_=pA[:, 0:nt], func=AF.Gelu),
                         reads=[bpA], writes=[b_guf[j]])
                    k.op("act", lambda: nc.scalar.activation(out=szf[j][:, 0:nt], in_=pB[:, 0:nt], func=AF.Silu),
                         reads=[bpB], writes=[b_szf[j]])
                    for ti in range(ntile):
                        bt = biasT_s[:, ft, :] if samp else biasT[:, ft, :]
                        k.op("dve", lambda: nc.vector.scalar_tensor_tensor(
                            out=ssf[j][:, ti * 128:ti * 128 + P], in0=pC[:, ti * 128:ti * 128 + P],
                            scalar=lnT[:, 0, ft:ft + 1], in1=bt, op0=ALU.mult, op1=ALU.add),
                            reads=[bpC, b_const], writes=[b_ssf[j]])
                    k.op("dve", lambda: nc.vector.tensor_tensor(out=ssf[j][:, 0:nt], in0=ssf[j][:, 0:nt], in1=guf[j][:, 0:nt],
                                                               op=ALU.mult), reads=[b_ssf[j], b_guf[j]], writes=[b_ssf[j]])
                    k.op("dve", lambda: nc.vector.tensor_tensor(out=yT[:, ft, 0:nt], in0=ssf[j][:, 0:nt], in1=szf[j][:, 0:nt],
                                                               op=ALU.mult), reads=[b_ssf[j], b_szf[j]], writes=[b_yT])
            for oc in range(8):
                i = cnt["o"] % 2
                cnt["o"] += 1
                wflat = wo[i][:].rearrange("p a b -> p (a b)")
                if bidx == 0:
                    for c4 in range(4):
                        k.dma("poolq", wo[i][:, c4 * 8:(c4 + 1) * 8, :],
                              w_out[c4 * 1024:(c4 + 1) * 1024, oc * 256:(oc + 1) * 256].rearrange("(kt p) c -> p kt c", p=128),
                              writes=[b_wo[i]])
                    k.dma("sp", wc_out[oc, :, :], wflat, reads=[b_wo[i]], writes=[b_wcout[oc]])
                else:
                    k.dma("sp", wflat, wc_out[oc, :, :], reads=[b_wcout[oc]], writes=[b_wo[i]])
                for ti in range(ntile):
                    r0 = t0 + ti * 128
                    j = cnt["t"] % 2
                    cnt["t"] += 1
                    pp, bpp = (pD, bpD) if j == 0 else (pB, bpB)
                    k.dma("sp", xc[j][:P, :], x[r0:r0 + P, oc * 256:(oc + 1) * 256], reads=[b_xsrc], writes=[b_xc[j]])
                    for ft in range(32):
                        k.op("pe", lambda: nc.tensor.matmul(pp[:P, 0:256], lhsT=yT[:, ft, ti * 128:ti * 128 + P],
                                                            rhs=wo[i][:, ft, :], start=(ft == 0), stop=(ft == 31)),
                             reads=[b_yT, b_wo[i]], writes=[bpp], sig=(ft == 31))
                    k.op("dve", lambda: nc.vector.tensor_tensor(out=xn[j][:P, :], in0=pp[:P, 0:256], in1=xc[j][:P, :],
                                                               op=ALU.add), reads=[bpp, b_xc[j]], writes=[b_xn[j]])
                    k.dma("sp", xo[r0:r0 + P, oc * 256:(oc + 1) * 256], xn[j][:P, :], reads=[b_xn[j]], writes=[b_xo])
            if final_norm:
                for ti in range(ntile):
                    r0 = t0 + ti * 128
                    k.dma("sp", hT[:P, :, :].rearrange("p a b -> p (a b)").bitcast(F32)[:, 0:D], xo[r0:r0 + P, :],
                          reads=[b_xo], writes=[b_hT])
                    xfl = hT[:P, :, :].rearrange("p a b -> p (a b)").bitcast(F32)
                    for q4 in range(4):
                        k.op("act", lambda: nc.scalar.activation(out=gvf[0][:P, :], in_=xfl[:, q4 * 512:(q4 + 1) * 512],
                                                                 func=AF.Square), reads=[b_hT], writes=[b_gvf[0]])
                        k.op("dve", lambda: nc.vector.reduce_sum(out=small[:P, q4:q4 + 1], in_=gvf[0][:P, :], axis=AX.X),
                             reads=[b_gvf[0]], writes=[b_small])
                    k.op("dve", lambda: nc.vector.reduce_sum(out=small[:P, 4:5], in_=small[:P, 0:4], axis=AX.X),
                         reads=[b_small], writes=[b_small])
                    rstd_from_sumsq(k, nc, small[:P, 5:6], small[:P, 4:5], D, [b_small], [b_small])
                    k.op("dve", lambda: nc.vector.scalar_tensor_tensor(out=xt[:P, :], in0=xfl[:, 0:D], scalar=small[:P, 5:6],
                                                                       in1=fnw_bc[:P, :], op0=ALU.mult, op1=ALU.mult),
                         reads=[b_hT, b_small, b_const], writes=[b_xt])
                    k.dma("sp", yo[r0:r0 + P, :], xt[:P, :], reads=[b_xt], writes=[b_xo])
        k.finish()
    return nc


def _run(nc, in_maps):
    res = run_bass_kernel_spmd(nc, in_maps, core_ids=list(range(NCORES)))
    return res.results


def tok_shard(xp, xs, c):
    return np.ascontiguousarray(np.concatenate(
        [xp[c * TP:(c + 1) * TP], xs[c * 4:(c + 1) * 4].reshape(TS, -1)], axis=0))


def run_gmlp(xp, xs, normw, w_in, ln_g, ln_b, w_s, b_s, w_out, fnw=None, pre=None):
    nc = build_gmlp(fnw is not None, pre is not None)
    in_maps = []
    for c in range(NCORES):
        m = {"x": tok_shard(xp, xs, c), "normw": normw, "w_in": w_in, "ln_g": ln_g, "ln_b": ln_b,
             "w_s": w_s, "b_s": b_s, "w_out": w_out}
        if fnw is not None:
            m["fnw"] = fnw
        if pre is not None:
            m["aT"] = np.ascontiguousarray(tok_shard(pre[0], pre[1], c).T)
            m["w_o"] = pre[2]
        in_maps.append(m)
    return _run(nc, in_maps)


NTOK = SEQ + NSB * NST
GC = 768


def build_mamba():
    nc = bass.Bass("TRN2", target_bir_lowering=False)
    xT = nc.dram_tensor("xT", [D, NTOK], F32, kind="ExternalInput").ap()
    normwT = nc.dram_tensor("normwT", [128, 16], F32, kind="ExternalInput").ap()
    w_z = nc.dram_tensor("w_z", [D, 512], F32, kind="ExternalInput").ap()
    w_xbc = nc.dram_tensor("w_xbc", [D, GC], F32, kind="ExternalInput").ap()
    w_dt = nc.dram_tensor("w_dt", [D, 8], F32, kind="ExternalInput").ap()
    convw = nc.dram_tensor("convw", [128, 6, 4], F32, kind="ExternalInput").ap()
    convb = nc.dram_tensor("convb", [128, 6], F32, kind="ExternalInput").ap()
    hp = nc.dram_tensor("hp", [3, 8], F32, kind="ExternalInput").ap()
    dexp = nc.dram_tensor("dexp", [512], F32, kind="ExternalInput").ap()
    nw = nc.dram_tensor("nw", [512], F32, kind="ExternalInput").ap()
    sT_in = nc.dram_tensor("sT_in", [NSB, 128, 512], F32, kind="ExternalInput").ap()
    cs_in = nc.dram_tensor("cs_in", [128, 6, NSB, 3], F32, kind="ExternalInput").ap()
    yn = nc.dram_tensor("yn", [NTOK, 512], F32, kind="ExternalOutput").ap()
    sT_p = nc.dram_tensor("sT_p", [128, 512], F32, kind="ExternalOutput").ap()
    cs_p = nc.dram_tensor("cs_p", [128, 6, 3], F32, kind="ExternalOutput").ap()
    sT_s = nc.dram_tensor("sT_s", [NSB, 128, 512], F32, kind="ExternalOutput").ap()
    cs_s = nc.dram_tensor("cs_s", [128, 6, NSB, 3], F32, kind="ExternalOutput").ap()

    with ExitStack() as st:
        k = K(nc, st)
        identf, ident, b_id = make_ident(k, nc)
        bc = k.buf("const")
        nwT = k.sb("nwT", [128, 16], F32)
        k.dma("sp", nwT[:], normwT[:, :], writes=[bc])
        cw = k.sb("cw", [128, 6, 4], F32)
        k.dma("sp", cw[:], convw[:, :, :], writes=[bc])
        cbias = k.sb("cbias", [128, 6], F32)
        k.dma("sp", cbias[:], convb[:, :], writes=[bc])
        hpb = k.sb("hpb", [128, 3, 8], F32)
        k.dma("sp", hpb[:].rearrange("p a b -> p (a b)"), hp.rearrange("a b -> (a b)").partition_broadcast(128), writes=[bc])
        a_bc = k.sb("a_bc", [128, 8], F32)
        k.op("act", lambda: nc.scalar.activation(out=a_bc[:], in_=hpb[:, 1, :], func=AF.Exp), reads=[bc], writes=[bc])
        k.op("dve", lambda: nc.vector.tensor_scalar(out=a_bc[:], in0=a_bc[:], scalar1=-1.0, scalar2=None, op0=ALU.mult),
             reads=[bc], writes=[bc])
        d_bc = k.sb("d_bc", [128, 512], F32)
        k.dma("sp", d_bc[:], dexp.partition_broadcast(128), writes=[bc])
        nw_bc = k.sb("nw_bc", [128, 512], F32)
        k.dma("sp", nw_bc[:], nw.partition_broadcast(128), writes=[bc])
        ones_b = k.sb("ones_b", [128, 128], BF16)
        k.op("dve", lambda: nc.vector.memset(ones_b[:], 1.0), writes=[bc])
        ones_f = k.sb("ones_f", [128, 128], F32)
        k.op("dve", lambda: nc.vector.memset(ones_f[:], 1.0), writes=[bc])
        tri_le = k.sb("tri_le", [128, 128], F32)
        k.op("pool", lambda: nc.gpsimd.affine_select(out=tri_le[:], in_=ones_f[:], pattern=[[1, 128]],
                                                     compare_op=ALU.is_ge, fill=0.0, base=0, channel_multiplier=-1),
             reads=[bc], writes=[bc])
        mgt = k.sb("mgt", [128, 128], F32)
        k.op("pool", lambda: nc.gpsimd.affine_select(out=mgt[:], in_=ones_f[:], pattern=[[-1, 128]],
                                                     compare_op=ALU.is_gt, fill=0.0, base=0, channel_multiplier=1),
             reads=[bc], writes=[bc])
        Wx = k.sb("Wx", [128, 16, GC], BF16)
        Wz = k.sb("Wz", [128, 16, 512], BF16)
        Wd = k.sb("Wd", [128, 16, 8], BF16)
        for c4 in range(4):
            k.dma("poolq", Wx[:, c4 * 4:(c4 + 1) * 4, :], w_xbc[c4 * 512:(c4 + 1) * 512, :].rearrange("(kt p) c -> p kt c", p=128), writes=[bc])
            k.dma("poolq", Wz[:, c4 * 4:(c4 + 1) * 4, :], w_z[c4 * 512:(c4 + 1) * 512, :].rearrange("(kt p) c -> p kt c", p=128), writes=[bc])
        k.dma("poolq", Wd[:], w_dt.rearrange("(kt p) c -> p kt c", p=128), writes=[bc])

        pX = k.ps("pX", [128, 1024], F32)
        pZ = k.ps("pZ", [128, 512], F32)
        pSeg = k.ps("pSeg", [128, 1024], F32)
        pY = k.ps("pY", [128, 512], F32)
        pYi = k.ps("pYi", [128, 512], F32)
        pM = k.ps("pM", [128, 512], F32)
        bpX, bpZ, bpSeg, bpY, bpYi, bpM = (k.pbuf(n) for n in ("pX", "pZ", "pSeg", "pY", "pYi", "pM"))
        pYi_b = pYi[:].bitcast(BF16)

        xTb = k.sb("xTb", [128, 16, 512], F32)
        sqb = [k.sb(f"sqb{i}", [128, 512], BF16) for i in range(2)]
        hT = k.sb("hT", [128, 16, 512], BF16)
        rstd = k.sb("rstd", [128, 512], F32)
        cbuf = [k.sb(f"cbuf{i}", [128, 6, 131], F32) for i in range(2)]
        cv = k.sb("cv", [128, 6, 128], F32)
        xbc = k.sb("xbc", [128, 6, 128], BF16)
        Et = k.sb("Et", [128, 8, 128], F32)
        MT = k.sb("MT", [128, 8, 128], BF16)
        lh = k.sb("lh", [128, 8, 128], F32)
        x_tok = k.sb("x_tok", [128, 512], BF16)
        B_tok = k.sb("B_tok", [128, 128], BF16)
        xdt = k.sb("xdt", [128, 512], BF16)
        xw = k.sb("xw", [128, 512], BF16)
        y_sb = k.sb("y_sb", [128, 512], F32)
        zs = k.sb("zs", [128, 512], F32)
        yg = k.sb("yg", [128, 512], F32)
        ysq = k.sb("ysq", [128, 512], F32)
        yo = [k.sb(f"yo{i}", [128, 512], F32) for i in range(2)]
        ST = k.sb("ST", [128, 512], F32)
        STb = k.sb("STb", [128, 512], BF16)
        dtt = k.sb("dtt", [128, 8], F32)
        da = k.sb("da", [128, 8], F32)
        ecum = k.sb("ecum", [128, 8], F32)
        dec = k.sb("dec", [128, 8], F32)
        cbm = k.sb("cbm", [128, 128], F32)
        sm = k.sb("sm", [128, 4], F32)
        cso = k.sb("cso", [128, 6, NSB, 3], F32)
        (b_xTb, b_hT, b_rstd, b_cv, b_xbc, b_Et, b_MT, b_lh, b_xtok, b_Btok, b_xdt, b_xw, b_ysb, b_zs, b_yg, b_ysq,
         b_ST, b_STb, b_dt, b_da, b_ecum, b_dec, b_cbm, b_sm, b_cso, b_out) = (k.buf() for _ in range(26))
        b_sqb = [k.buf() for _ in range(2)]
        b_cbuf = [k.buf() for _ in range(2)]
        b_yo = [k.buf() for _ in range(2)]
        k.dma("sp", cso[:], cs_in[:, :, :, :], writes=[b_cso])
        k.op("dve", lambda: nc.vector.memset(ST[:], 0.0), writes=[b_ST])
        k.op("dve", lambda: nc.vector.memset(STb[:], 0.0), writes=[b_STb])
        k.op("dve", lambda: nc.vector.memset(cbuf[0][:], 0.0), writes=[b_cbuf[0]])
        k.op("dve", lambda: nc.vector.memset(cbuf[1][:], 0.0), writes=[b_cbuf[1]])
        ucount = [0]

        def unit(u0, L, tok0, sample_b):
            ci = ucount[0] % 2
            ucount[0] += 1
            cb_, bcb = cbuf[ci], b_cbuf[ci]
            cbn, bcbn = cbuf[1 - ci], b_cbuf[1 - ci]
            if sample_b is not None:
                k.op("dve", lambda: nc.vector.tensor_copy(out=cb_[:, :, 0:3], in_=cso[:, :, sample_b, :]),
                     reads=[b_cso], writes=[bcb])
                k.dma("sp", ST[:], sT_in[sample_b, :, :], writes=[b_ST])
                k.op("act", lambda: nc.scalar.copy(out=STb[:], in_=ST[:]), reads=[b_ST], writes=[b_STb])
            for ct in range(6):
                for kt in range(16):
                    k.op("pe", lambda: nc.tensor.matmul(pX[:, ct * 128:ct * 128 + L], lhsT=Wx[:, kt, ct * 128:(ct + 1) * 128],
                                                        rhs=hT[:, kt, u0:u0 + L], start=(kt == 0), stop=(kt == 15)),
                         reads=[b_hT, bc], writes=[bpX], sig=(kt == 15))
            k.op("act", lambda: nc.scalar.copy(out=cb_[:, 0:4, 3:3 + L],
                                               in_=pX[:, 0:512].rearrange("p (a b) -> p a b", b=128)[:, :, 0:L]),
                 reads=[bpX], writes=[bcb])
            k.op("act", lambda: nc.scalar.copy(out=cb_[:, 4:6, 3:3 + L],
                                               in_=pX[:, 512:768].rearrange("p (a b) -> p a b", b=128)[:, :, 0:L]),
                 reads=[bpX], writes=[bcb])
            if sample_b is not None:
                k.op("dve", lambda: nc.vector.tensor_copy(out=cso[:, :, sample_b, :], in_=cb_[:, :, L:L + 3]),
                     reads=[bcb], writes=[b_cso])
            else:
                k.op("dve", lambda: nc.vector.tensor_copy(out=cbn[:, :, 0:3], in_=cb_[:, :, L:L + 3]),
                     reads=[bcb], writes=[bcbn])
            for ct in range(6):
                k.op("act", lambda: nc.scalar.activation(out=cv[:, ct, 0:L], in_=cb_[:, ct, 0:L], func=AF.Identity,
                                                         bias=cbias[:, ct:ct + 1], scale=cw[:, ct, 0:1]),
                     reads=[bcb, bc], writes=[b_cv])
                for kk in range(1, 4):
                    k.op("dve", lambda: nc.vector.scalar_tensor_tensor(out=cv[:, ct, 0:L], in0=cb_[:, ct, kk:kk + L],
                                                                       scalar=cw[:, ct, kk:kk + 1], in1=cv[:, ct, 0:L],
                                                                       op0=ALU.mult, op1=ALU.add),
                         reads=[bcb, bc, b_cv], writes=[b_cv])
            k.op("act", lambda: nc.scalar.activation(out=xbc[:, :, 0:L], in_=cv[:, :, 0:L], func=AF.Silu),
                 reads=[b_cv], writes=[b_xbc])
            for kt in range(16):
                k.op("pe", lambda: nc.tensor.matmul(pZ[:L, :], lhsT=hT[:, kt, u0:u0 + L], rhs=Wz[:, kt, :],
                                                    start=(kt == 0), stop=(kt == 15)),
                     reads=[b_hT, bc], writes=[bpZ], sig=(kt == 15))
            for kt in range(16):
                k.op("pe", lambda: nc.tensor.matmul(pM[:L, 0:8], lhsT=hT[:, kt, u0:u0 + L], rhs=Wd[:, kt, :],
                                                    start=(kt == 0), stop=(kt == 15)),
                     reads=[b_hT, bc], writes=[bpM], sig=(kt == 15))
            k.op("dve", lambda: nc.vector.tensor_tensor(out=dtt[:L, :], in0=pM[:L, 0:8], in1=hpb[:L, 0, :], op=ALU.add),
                 reads=[bpM, bc], writes=[b_dt])
            k.op("act", lambda: nc.scalar.activation(out=dtt[:L, :], in_=dtt[:L, :], func=AF.Exp), reads=[b_dt], writes=[b_dt])
            k.op("act", lambda: nc.scalar.activation(out=dtt[:L, :], in_=dtt[:L, :], func=AF.Ln, bias=1.0), reads=[b_dt], writes=[b_dt])
            k.op("dve", lambda: nc.vector.tensor_tensor(out=da[:L, :], in0=dtt[:L, :], in1=a_bc[:L, :], op=ALU.mult),
                 reads=[b_dt, bc], writes=[b_da])
            for ct in range(5):
                k.op("pe", lambda: nc.tensor.transpose(out=pYi_b[:L, ct * 128:(ct + 1) * 128], in_=xbc[:, ct, 0:L], identity=ident[:]),
                     reads=[b_xbc, b_id], writes=[bpYi])
            k.op("act", lambda: nc.scalar.copy(out=x_tok[:L, :], in_=pYi_b[:L, 0:512]), reads=[bpYi], writes=[b_xtok])
            k.op("act", lambda: nc.scalar.copy(out=B_tok[:L, :], in_=pYi_b[:L, 512:640]), reads=[bpYi], writes=[b_Btok])
            for h in range(8):
                k.op("dve", lambda: nc.vector.tensor_scalar(out=xdt[:L, h * 64:(h + 1) * 64], in0=x_tok[:L, h * 64:(h + 1) * 64],
                                                           scalar1=dtt[:L, h:h + 1], scalar2=None, op0=ALU.mult),
                     reads=[b_xtok, b_dt], writes=[b_xdt])
            for h in range(8):
                k.op("dve", lambda: nc.vector.tensor_scalar(out=lh[:L, h, 0:L], in0=mgt[:L, 0:L], scalar1=da[:L, h:h + 1],
                                                           scalar2=None, op0=ALU.mult), reads=[b_da, bc], writes=[b_lh])
            for h in range(8):
                k.op("pe", lambda: nc.tensor.matmul(pSeg[:L, h * 128:h * 128 + L], lhsT=lh[:L, h, 0:L], rhs=tri_le[:L, 0:L],
                                                    start=True, stop=True), reads=[b_lh, bc], writes=[bpSeg])
            for hb_ in range(2):
                k.op("act", lambda: nc.scalar.activation(
                    out=Et[:L, hb_ * 4:hb_ * 4 + 4, 0:L],
                    in_=pSeg[:L, hb_ * 512:(hb_ + 1) * 512].rearrange("p (a b) -> p a b", b=128)[:, :, 0:L], func=AF.Exp),
                    reads=[bpSeg], writes=[b_Et])
            k.op("pe", lambda: nc.tensor.matmul(pM[:L, 8:16], lhsT=tri_le[:L, 0:L], rhs=da[:L, :], start=True, stop=True),
                 reads=[b_da, bc], writes=[bpM])
            k.op("pe", lambda: nc.tensor.matmul(pM[:, 16:24], lhsT=ones_f[:L, :], rhs=da[:L, :], start=True, stop=True),
                 reads=[b_da, bc], writes=[bpM])
            k.op("act", lambda: nc.scalar.activation(out=ecum[:L, :], in_=pM[:L, 8:16], func=AF.Exp), reads=[bpM], writes=[b_ecum])
            k.op("act", lambda: nc.scalar.activation(out=dec[:, :], in_=pM[:, 16:24], func=AF.Exp), reads=[bpM], writes=[b_dec])
            k.op("pe", lambda: nc.tensor.matmul(pM[:L, 128:128 + L], lhsT=xbc[:, 4, 0:L], rhs=xbc[:, 5, 0:L], start=True, stop=True),
                 reads=[b_xbc], writes=[bpM])
            k.op("dve", lambda: nc.vector.tensor_tensor(out=cbm[:L, 0:L], in0=pM[:L, 128:128 + L], in1=tri_le[:L, 0:L], op=ALU.mult),
                 reads=[bpM, bc], writes=[b_cbm])
            for h in range(8):
                k.op("dve", lambda: nc.vector.tensor_tensor(out=MT[:L, h, 0:L], in0=Et[:L, h, 0:L], in1=cbm[:L, 0:L], op=ALU.mult),
                     reads=[b_Et, b_cbm], writes=[b_MT])
            for h in range(8):
                k.op("pe", lambda: nc.tensor.matmul(pY[:L, h * 64:(h + 1) * 64], lhsT=MT[:L, h, 0:L], rhs=xdt[:L, h * 64:(h + 1) * 64],
                                                    start=True, stop=True), reads=[b_MT, b_xdt], writes=[bpY])
            k.op("pe", lambda: nc.tensor.matmul(pYi[:L, :], lhsT=xbc[:, 5, 0:L], rhs=STb[:], start=True, stop=True),
                 reads=[b_xbc, b_STb], writes=[bpYi])
            k.op("act", lambda: nc.scalar.copy(out=y_sb[:L, :], in_=pY[:L, :]), reads=[bpY], writes=[b_ysb])
            for h in range(8):
                k.op("dve", lambda: nc.vector.scalar_tensor_tensor(out=y_sb[:L, h * 64:(h + 1) * 64], in0=pYi[:L, h * 64:(h + 1) * 64],
                                                                   scalar=ecum[:L, h:h + 1], in1=y_sb[:L, h * 64:(h + 1) * 64],
                                                                   op0=ALU.mult, op1=ALU.add),
                     reads=[bpYi, b_ecum, b_ysb], writes=[b_ysb])
            k.op("dve", lambda: nc.vector.tensor_tensor(out=yg[:L, :], in0=x_tok[:L, :], in1=d_bc[:L, :], op=ALU.mult),
                 reads=[b_xtok, bc], writes=[b_yg])
            k.op("dve", lambda: nc.vector.tensor_tensor(out=y_sb[:L, :], in0=y_sb[:L, :], in1=yg[:L, :], op=ALU.add),
                 reads=[b_ysb, b_yg], writes=[b_ysb])
            for h in range(8):
                k.op("dve", lambda: nc.vector.tensor_scalar(out=xw[:L, h * 64:(h + 1) * 64], in0=xdt[:L, h * 64:(h + 1) * 64],
                                                           scalar1=Et[:L, h, L - 1:L], scalar2=None, op0=ALU.mult),
                     reads=[b_xdt, b_Et], writes=[b_xw])
            k.op("pe", lambda: nc.tensor.matmul(pX[:, 0:512], lhsT=B_tok[:L, :], rhs=xw[:L, :], start=True, stop=True),
                 reads=[b_Btok, b_xw], writes=[bpX])
            for h in range(8):
                k.op("dve", lambda: nc.vector.scalar_tensor_tensor(out=ST[:, h * 64:(h + 1) * 64], in0=ST[:, h * 64:(h + 1) * 64],
                                                                   scalar=dec[:, h:h + 1], in1=pX[:, h * 64:(h + 1) * 64],
                                                                   op0=ALU.mult, op1=ALU.add),
                     reads=[b_ST, b_dec, bpX], writes=[b_ST])
            if sample_b is not None:
                k.dma("sp", sT_s[sample_b, :, :], ST[:], reads=[b_ST], writes=[b_out])
            else:
                k.op("act", lambda: nc.scalar.copy(out=STb[:], in_=ST[:]), reads=[b_ST], writes=[b_STb])
            k.op("act", lambda: nc.scalar.activation(out=zs[:L, :], in_=pZ[:L, :], func=AF.Silu), reads=[bpZ], writes=[b_zs])
            k.op("dve", lambda: nc.vector.tensor_tensor(out=yg[:L, :], in0=y_sb[:L, :], in1=zs[:L, :], op=ALU.mult),
                 reads=[b_ysb, b_zs], writes=[b_yg])
            k.op("act", lambda: nc.scalar.activation(out=ysq[:L, :], in_=yg[:L, :], func=AF.Square), reads=[b_yg], writes=[b_ysq])
            k.op("dve", lambda: nc.vector.reduce_sum(out=sm[:L, 0:1], in_=ysq[:L, :], axis=AX.X), reads=[b_ysq], writes=[b_sm])
            rstd_from_sumsq(k, nc, sm[:L, 1:2], sm[:L, 0:1], 512, [b_sm], [b_sm])
            oi = ucount[0] % 2
            k.op("dve", lambda: nc.vector.scalar_tensor_tensor(out=yo[oi][:L, :], in0=yg[:L, :], scalar=sm[:L, 1:2], in1=nw_bc[:L, :],
                                                               op0=ALU.mult, op1=ALU.mult),
                 reads=[b_yg, b_sm, bc], writes=[b_yo[oi]])
            k.dma("sp", yn[tok0:tok0 + L, :], yo[oi][:L, :], reads=[b_yo[oi]], writes=[b_out])
            return cbn, bcbn

        nblk = NTOK // 512
        last = None
        for bi in range(nblk):
            c0 = bi * 512
            for c4 in range(4):
                k.dma("sp", xTb[:, c4 * 4:(c4 + 1) * 4, :],
                      xT[c4 * 512:(c4 + 1) * 512, c0:c0 + 512].rearrange("(kt p) t -> p kt t", p=128), writes=[b_xTb])
            for kt in range(16):
                j = kt % 2
                k.op("act", lambda: nc.scalar.activation(out=sqb[j][:], in_=xTb[:, kt, :], func=AF.Square),
                     reads=[b_xTb], writes=[b_sqb[j]])
                k.op("pe", lambda: nc.tensor.matmul(pZ[:, :], lhsT=ones_b[:], rhs=sqb[j][:], start=(kt == 0), stop=(kt == 15)),
                     reads=[b_sqb[j], bc], writes=[bpZ])
            k.op("dve", lambda: nc.vector.tensor_scalar(out=rstd[:], in0=pZ[:, :], scalar1=1.0 / D, scalar2=EPS,
                                                       op0=ALU.mult, op1=ALU.add), reads=[bpZ], writes=[b_rstd])
            k.op("act", lambda: nc.scalar.activation(out=rstd[:], in_=rstd[:], func=AF.Sqrt), reads=[b_rstd], writes=[b_rstd])
            k.op("dve", lambda: nc.vector.reciprocal(out=rstd[:], in_=rstd[:]), reads=[b_rstd], writes=[b_rstd])
            for kt in range(16):
                k.op("dve", lambda: nc.vector.scalar_tensor_tensor(out=hT[:, kt, :], in0=xTb[:, kt, :], scalar=nwT[:, kt:kt + 1],
                                                                   in1=rstd[:], op0=ALU.mult, op1=ALU.mult),
                     reads=[b_xTb, b_rstd, bc], writes=[b_hT])
            if bi < SEQ // 512:
                for u in range(4):
                    last = unit(u * 128, 128, c0 + u * 128, None)
                if bi == SEQ // 512 - 1:
                    k.dma("sp", sT_p[:, :], ST[:], reads=[b_ST], writes=[b_out])
                    k.dma("sp", cs_p[:, :, :], last[0][:, :, 0:3], reads=[last[1]], writes=[b_out])
            else:
                for b in range(NSB):
                    unit(b * 16, 16, c0 + b * 16, b)
        k.dma("sp", cs_s[:, :, :, :], cso[:], reads=[b_cso], writes=[b_out])
        k.finish()
    return nc


PAST = 1024


def build_attn(dbg_heads=2, dbg_blocks=None, dbg_attend=True):
    nc = bass.Bass("TRN2", target_bir_lowering=False)
    xT = nc.dram_tensor("xT", [D, NTOK], F32, kind="ExternalInput").ap()
    normwT = nc.dram_tensor("normwT", [128, 16], F32, kind="ExternalInput").ap()
    w_att = nc.dram_tensor("w_att", [2, D, 512], F32, kind="ExternalInput").ap()
    ckT = nc.dram_tensor("ckT", [NSB, 2, 128, PAST], F32, kind="ExternalInput").ap()
    cv_ = nc.dram_tensor("cv", [NSB, 2, PAST, 128], F32, kind="ExternalInput").ap()
    ogT = nc.dram_tensor("ogT", [256, NTOK], F32, kind="ExternalOutput").ap()
    kTo = nc.dram_tensor("kTo", [2, 128, NTOK], F32, kind="ExternalOutput").ap()
    vo = nc.dram_tensor("vo", [2, NTOK, 128], F32, kind="ExternalOutput").ap()
    SCALE = 128 ** -0.5

    with ExitStack() as st:
        k = K(nc, st)
        identf, ident, b_id = make_ident(k, nc)
        bc = k.buf("const")
        nwT = k.sb("nwT", [128, 16], F32)
        k.dma("sp", nwT[:], normwT[:, :], writes=[bc])
        ones_b = k.sb("ones_b", [128, 128], BF16)
        k.op("dve", lambda: nc.vector.memset(ones_b[:], 1.0), writes=[bc])
        ones_f = k.sb("ones_f", [128, 128], F32)
        k.op("dve", lambda: nc.vector.memset(ones_f[:], 1.0), writes=[bc])
        ones_w = k.sb("ones_w", [128, 512], F32)
        k.op("dve", lambda: nc.vector.memset(ones_w[:], 1.0), writes=[bc])
        mgt = k.sb("mgt", [128, 128], F32)
        k.op("pool", lambda: nc.gpsimd.affine_select(out=mgt[:], in_=ones_f[:], pattern=[[-1, 128]],
                                                     compare_op=ALU.is_gt, fill=0.0, base=0, channel_multiplier=1),
             reads=[bc], writes=[bc])
        m01 = k.sb("m01", [128, 4, 512], F32)
        negm = k.sb("negm", [128, 4, 512], F32)
        for jl in range(4):
            k.op("pool", lambda: nc.gpsimd.affine_select(out=m01[:, jl, :], in_=ones_w[:], pattern=[[-1, 512]],
                                                         compare_op=ALU.is_gt, fill=0.0, base=jl * 128, channel_multiplier=1),
                 reads=[bc], writes=[bc])
        k.op("dve", lambda: nc.vector.tensor_scalar(out=negm[:].rearrange("p a b -> p (a b)"), in0=m01[:].rearrange("p a b -> p (a b)"),
                                                   scalar1=-1.0, scalar2=1e30, op0=ALU.add, op1=ALU.mult), reads=[bc], writes=[bc])

        pQ = k.ps("pQ", [128, 512], F32)
        pV = k.ps("pV", [128, 512], F32)
        pS = [k.ps(f"pS{i}", [128, 512], F32) for i in range(2)]
        pT = [k.ps(f"pT{i}", [128, 1024], BF16) for i in range(2)]
        pO = k.ps("pO", [128, 512], F32)
        bpQ, bpV, bpO = (k.pbuf(n) for n in ("pQ", "pV", "pO"))
        bpS = [k.pbuf() for _ in range(2)]
        bpT = [k.pbuf() for _ in range(2)]

        xTb = k.sb("xTb", [128, 16, 512], F32)
        sqb = [k.sb(f"sqb{i}", [128, 512], BF16) for i in range(2)]
        hT = k.sb("hT", [128, 16, 512], BF16)
        rstd = k.sb("rstd", [128, 512], F32)
        W = k.sb("W", [128, 16, 512], BF16)
        kT_all = k.sb("kT_all", [128, SEQ], BF16)
        v_all = k.sb("v_all", [128, SEQ // 128, 128], BF16)
        qT = k.sb("qT", [128, 512], BF16)
        kT_s = k.sb("kT_s", [128, 512], BF16)
        v_s = k.sb("v_s", [16, 128], BF16)
        szT = k.sb("szT", [128, 512], F32)
        kf = k.sb("kf", [128, 512], F32)
        vf = [k.sb(f"vf{i}", [128, 128], F32) for i in range(2)]
        kc = k.sb("kc", [128, PAST], BF16)
        vc = k.sb("vc", [128, PAST // 128, 128], BF16)
        ef = [k.sb(f"ef{i}", [128, 512], F32) for i in range(2)]
        spf = [k.sb(f"spf{i}", [128, 512], F32) for i in range(2)]
        t1 = [k.sb(f"t1{i}", [128, 512], F32) for i in range(2)]
        t2 = [k.sb(f"t2{i}", [128, 512], F32) for i in range(2)]
        wT = [k.sb(f"wT{i}", [128, 512], BF16) for i in range(2)]
        wTt = [k.sb(f"wTt{i}", [128, 512], BF16) for i in range(2)]
        b_wTt = [k.buf() for _ in range(2)]
        carry = [k.sb(f"carry{i}", [128, 2], F32) for i in range(2)]
        ogf = [k.sb(f"ogf{i}", [128, 512], F32) for i in range(2)]
        (b_xTb, b_hT, b_rstd, b_W, b_kT, b_v, b_qT, b_kTs, b_vs, b_szT, b_kf, b_kc, b_vc, b_out) = (k.buf() for _ in range(14))
        b_sqb = [k.buf() for _ in range(2)]
        b_vf = [k.buf() for _ in range(2)]
        b_ef = [k.buf() for _ in range(2)]
        b_spf = [k.buf() for _ in range(2)]
        b_t1 = [k.buf() for _ in range(2)]
        b_t2 = [k.buf() for _ in range(2)]
        b_wT = [k.buf() for _ in range(2)]
        b_carry = [k.buf() for _ in range(2)]
        b_ogf = [k.buf() for _ in range(2)]
        tc = [0]
        cc = [0]
        oc = [0]

        def attend(q_ap, T, tiles):
            nt_ = len(tiles)
            base = tc[0]
            tc[0] += nt_
            c_first = cc[0] % 2
            cc[0] += nt_
            k.op("pool", lambda: nc.gpsimd.memset(carry[c_first][:, :], 0.0), writes=[b_carry[c_first]])

            def stage_a(idx):
                k_ap, vs, J, m_ap, n_ap, rds = tiles[idx]
                i = (base + idx) % 2
                k.op("pe", lambda: nc.tensor.matmul(pS[i][:T, 0:J], lhsT=q_ap, rhs=k_ap, start=True, stop=True),
                     reads=rds + [b_qT], writes=[bpS[i]])
                k.op("act", lambda: nc.scalar.activation(out=ef[i][:T, 0:J], in_=pS[i][:T, 0:J], func=AF.Exp),
                     reads=[bpS[i]], writes=[b_ef[i]])
                k.op("act", lambda: nc.scalar.activation(out=spf[i][:T, 0:J], in_=ef[i][:T, 0:J], func=AF.Ln, bias=1.0),
                     reads=[b_ef[i]], writes=[b_spf[i]])
                if m_ap is not None:
                    k.op("pool", lambda: nc.gpsimd.tensor_tensor(out=spf[i][:T, 0:J], in0=spf[i][:T, 0:J], in1=m_ap, op=ALU.mult),
                         reads=[b_spf[i], bc], writes=[b_spf[i]])

            def stage_b(idx):
                k_ap, vs, J, m_ap, n_ap, rds = tiles[idx]
                i = (base + idx) % 2
                ci = (c_first + idx) % 2
                cn = 1 - ci
                k.op("dve", lambda: nc.vector.tensor_tensor_scan(out=t2[i][:T, 0:J], data0=ones_w[:T, 0:J], data1=spf[i][:T, 0:J],
                                                                 initial=0.0, op0=ALU.mult, op1=ALU.add),
                     reads=[b_spf[i], bc], writes=[b_t2[i]])
                k.op("dve", lambda: nc.vector.tensor_tensor(out=carry[cn][:T, 0:1], in0=carry[ci][:T, 0:1], in1=t2[i][:T, J - 1:J],
                                                           op=ALU.subtract), reads=[b_carry[ci], b_t2[i]], writes=[b_carry[cn]])
                k.op("dve", lambda: nc.vector.tensor_tensor(out=t1[i][:T, 0:J], in0=pS[i][:T, 0:J], in1=spf[i][:T, 0:J], op=ALU.subtract),
                     reads=[bpS[i], b_spf[i]], writes=[b_t1[i]])
                k.op("dve", lambda: nc.vector.tensor_tensor(out=t2[i][:T, 0:J], in0=t2[i][:T, 0:J], in1=t1[i][:T, 0:J], op=ALU.add),
                     reads=[b_t2[i], b_t1[i]], writes=[b_t2[i]])
                if n_ap is not None:
                    k.op("pool", lambda: nc.gpsimd.tensor_tensor(out=t2[i][:T, 0:J], in0=t2[i][:T, 0:J], in1=n_ap, op=ALU.add),
                         reads=[b_t2[i], bc], writes=[b_t2[i]])
                k.op("act", lambda: nc.scalar.activation(out=wT[i][:T, 0:J], in_=t2[i][:T, 0:J], func=AF.Exp,
                                                         bias=carry[cn][:T, 0:1]),
                     reads=[b_t2[i], b_carry[cn]], writes=[b_wT[i]])
                for jb, (v_ap, jn) in enumerate(vs):
                    k.op("pe", lambda: nc.tensor.transpose(out=pT[i][:jn, jb * 128:jb * 128 + T], in_=wT[i][:T, jb * 128:jb * 128 + jn],
                                                           identity=ident[:T, :T]),
                         reads=[b_wT[i], b_id], writes=[bpT[i]])

            def stage_c(idx):
                k_ap, vs, J, m_ap, n_ap, rds = tiles[idx]
                i = (base + idx) % 2
                jn0 = vs[0][1]
                nb_ = len(vs)
                if i == 0:
                    k.op("act", lambda: nc.scalar.copy(out=wTt[i][:jn0, 0:nb_ * 128], in_=pT[i][:jn0, 0:nb_ * 128]),
                         reads=[bpT[i]], writes=[b_wTt[i]])
                else:
                    k.op("dve", lambda: nc.vector.tensor_copy(out=wTt[i][:jn0, 0:nb_ * 128], in_=pT[i][:jn0, 0:nb_ * 128]),
                         reads=[bpT[i]], writes=[b_wTt[i]])
                for jb, (v_ap, jn) in enumerate(vs):
                    first = (idx == 0 and jb == 0)
                    last = (idx == nt_ - 1 and jb == nb_ - 1)
                    k.op("pe", lambda: nc.tensor.matmul(pO[:, 0:T], lhsT=v_ap, rhs=wTt[i][:jn, jb * 128:jb * 128 + T],
                                                        start=first, stop=last),
                         reads=rds + [b_wTt[i]], writes=[bpO])

            stage_a(0)
            for idx in range(nt_):
                if idx + 1 < nt_:
                    stage_a(idx + 1)
                stage_b(idx)
                if idx >= 1:
                    stage_c(idx - 1)
            stage_c(nt_ - 1)

        def emit_og(hh, N, col0, sz_ap):
            j = oc[0] % 2
            oc[0] += 1
            k.op("dve", lambda: nc.vector.tensor_tensor(out=ogf[j][:, 0:N], in0=pO[:, 0:N], in1=sz_ap, op=ALU.mult),
                 reads=[bpO, b_szT], writes=[b_ogf[j]])
            k.dma("sp", ogT[hh * 128:(hh + 1) * 128, col0:col0 + N], ogf[j][:, 0:N], reads=[b_ogf[j]], writes=[b_out])

        nblk = NTOK // 512
        for hh in range(dbg_heads):
            for c4 in range(4):
                k.dma("poolq", W[:, c4 * 4:(c4 + 1) * 4, :], w_att[hh, c4 * 512:(c4 + 1) * 512, :].rearrange("(kt p) c -> p kt c", p=128),
                      writes=[b_W])
            for bi in (range(nblk) if dbg_blocks is None else dbg_blocks):
                c0 = bi * 512
                samp = bi >= SEQ // 512
                for c4 in range(4):
                    k.dma("sp", xTb[:, c4 * 4:(c4 + 1) * 4, :],
                          xT[c4 * 512:(c4 + 1) * 512, c0:c0 + 512].rearrange("(kt p) t -> p kt t", p=128), writes=[b_xTb])
                for kt in range(16):
                    j = kt % 2
                    k.op("act", lambda: nc.scalar.activation(out=sqb[j][:], in_=xTb[:, kt, :], func=AF.Square),
                         reads=[b_xTb], writes=[b_sqb[j]])
                    k.op("pe", lambda: nc.tensor.matmul(pQ[:, :], lhsT=ones_b[:], rhs=sqb[j][:], start=(kt == 0), stop=(kt == 15)),
                         reads=[b_sqb[j], bc], writes=[bpQ])
                k.op("dve", lambda: nc.vector.tensor_scalar(out=rstd[:], in0=pQ[:, :], scalar1=1.0 / D, scalar2=EPS,
                                                           op0=ALU.mult, op1=ALU.add), reads=[bpQ], writes=[b_rstd])
                k.op("act", lambda: nc.scalar.activation(out=rstd[:], in_=rstd[:], func=AF.Sqrt), reads=[b_rstd], writes=[b_rstd])
                k.op("dve", lambda: nc.vector.reciprocal(out=rstd[:], in_=rstd[:]), reads=[b_rstd], writes=[b_rstd])
                for kt in range(16):
                    k.op("dve", lambda: nc.vector.scalar_tensor_tensor(out=hT[:, kt, :], in0=xTb[:, kt, :], scalar=nwT[:, kt:kt + 1],
                                                                       in1=rstd[:], op0=ALU.mult, op1=ALU.mult),
                         reads=[b_xTb, b_rstd, bc], writes=[b_hT])
                import os
                LVL = int(os.environ.get("ATT_LVL", "9"))
                if LVL < 2:
                    continue
                for kt in range(16):
                    k.op("pe", lambda: nc.tensor.matmul(pQ[:, :], lhsT=W[:, kt, 0:128], rhs=hT[:, kt, :], start=(kt == 0), stop=(kt == 15)),
                         reads=[b_W, b_hT], writes=[bpQ], sig=(kt == 15))
                k.op("act", lambda: nc.scalar.activation(out=qT[:], in_=pQ[:, :], func=AF.Identity, scale=SCALE), reads=[bpQ], writes=[b_qT])
                if LVL < 3:
                    continue
                for kt in range(16):
                    k.op("pe", lambda: nc.tensor.matmul(pV[:, :], lhsT=W[:, kt, 128:256], rhs=hT[:, kt, :], start=(kt == 0), stop=(kt == 15)),
                         reads=[b_W, b_hT], writes=[bpV], sig=(kt == 15))
                if samp:
                    k.op("act", lambda: nc.scalar.copy(out=kT_s[:], in_=pV[:, :]), reads=[bpV], writes=[b_kTs])
                else:
                    k.op("act", lambda: nc.scalar.copy(out=kT_all[:, c0:c0 + 512], in_=pV[:, :]), reads=[bpV], writes=[b_kT])
                k.op("dve", lambda: nc.vector.tensor_copy(out=kf[:], in_=pV[:, :]), reads=[bpV], writes=[b_kf])
                if os.environ.get("ATT_NOKDMA") is None:
                    k.dma("sp", kTo[hh, :, c0:c0 + 512], kf[:], reads=[b_kf], writes=[b_out])
                if LVL < 4:
                    continue
                for kt in range(16):
                    k.op("pe", lambda: nc.tensor.matmul(pQ[:, :], lhsT=W[:, kt, 384:512], rhs=hT[:, kt, :], start=(kt == 0), stop=(kt == 15)),
                         reads=[b_W, b_hT], writes=[bpQ], sig=(kt == 15))
                k.op("act", lambda: nc.scalar.activation(out=szT[:], in_=pQ[:, :], func=AF.Silu), reads=[bpQ], writes=[b_szT])
                if LVL < 5:
                    continue
                if not samp:
                    for ti in range(4):
                        j = ti % 2
                        for kt in range(16):
                            k.op("pe", lambda: nc.tensor.matmul(pV[:, 0:128], lhsT=hT[:, kt, ti * 128:(ti + 1) * 128], rhs=W[:, kt, 256:384],
                                                                start=(kt == 0), stop=(kt == 15)),
                                 reads=[b_W, b_hT], writes=[bpV], sig=(kt == 15))
                        k.op("act", lambda: nc.scalar.copy(out=v_all[:, bi * 4 + ti, :], in_=pV[:, 0:128]), reads=[bpV], writes=[b_v])
                        k.op("dve", lambda: nc.vector.tensor_copy(out=vf[j][:], in_=pV[:, 0:128]), reads=[bpV], writes=[b_vf[j]])
                        k.dma("sp", vo[hh, c0 + ti * 128:c0 + (ti + 1) * 128, :], vf[j][:], reads=[b_vf[j]], writes=[b_out])
                    for qi in range(4):
                        tiles = []
                        for g in range(bi, -1, -1):
                            vs = [(v_all[:, g * 4 + jb, :], 128) for jb in range(4)]
                            if g == bi:
                                tiles.append((kT_all[:, g * 512:(g + 1) * 512], vs, 512, m01[:, qi, :], negm[:, qi, :], [b_kT, b_v]))
                            else:
                                tiles.append((kT_all[:, g * 512:(g + 1) * 512], vs, 512, None, None, [b_kT, b_v]))
                        if dbg_attend:
                            attend(qT[:, qi * 128:(qi + 1) * 128], 128, tiles)
                            emit_og(hh, 128, c0 + qi * 128, szT[:, qi * 128:(qi + 1) * 128])
                else:
                    for b in range(NSB):
                        j = b % 2
                        for kt in range(16):
                            k.op("pe", lambda: nc.tensor.matmul(pV[:16, 0:128], lhsT=hT[:, kt, b * 16:(b + 1) * 16], rhs=W[:, kt, 256:384],
                                                                start=(kt == 0), stop=(kt == 15)),
                                 reads=[b_W, b_hT], writes=[bpV], sig=(kt == 15))
                        k.op("act", lambda: nc.scalar.copy(out=v_s[:, :], in_=pV[:16, 0:128]), reads=[bpV], writes=[b_vs])
                        k.op("dve", lambda: nc.vector.tensor_copy(out=vf[j][:16, :], in_=pV[:16, 0:128]), reads=[bpV], writes=[b_vf[j]])
                        k.dma("sp", vo[hh, c0 + b * 16:c0 + (b + 1) * 16, :], vf[j][:16, :], reads=[b_vf[j]], writes=[b_out])
                        k.dma("poolq", kc[:], ckT[b, hh, :, :], writes=[b_kc])
                        k.dma("poolq", vc[:], cv_[b, hh, :, :].rearrange("(a p) d -> p a d", p=128), writes=[b_vc])
                        tiles = [(kT_s[:, b * 16:(b + 1) * 16], [(v_s[:, :], 16)], 16, m01[:16, 0, 0:16], negm[:16, 0, 0:16], [b_kTs, b_vs])]
                        for g in range(PAST // 512 - 1, -1, -1):
                            tiles.append((kc[:, g * 512:(g + 1) * 512], [(vc[:, g * 4 + jb, :], 128) for jb in range(4)], 512,
                                          None, None, [b_kc, b_vc]))
                        if dbg_attend:
                            attend(qT[:, b * 16:(b + 1) * 16], 16, tiles)
                            emit_og(hh, 16, c0 + b * 16, szT[:, b * 16:(b + 1) * 16])
        k.finish()
    return nc


def build_oproj(KD):
    nc = bass.Bass("TRN2", target_bir_lowering=False)
    nk = KD // 128
    aT = nc.dram_tensor("aT", [KD, TT], F32, kind="ExternalInput").ap()
    w = nc.dram_tensor("w", [KD, D], F32, kind="ExternalInput").ap()
    x = nc.dram_tensor("x", [TT, D], F32, kind="ExternalInput").ap()
    xo = nc.dram_tensor("xo", [TT, D], F32, kind="ExternalOutput").ap()
    with ExitStack() as st:
        k = K(nc, st)
        pp = [k.ps(f"pp{i}", [128, 512], F32) for i in range(2)]
        bpp = [k.pbuf() for _ in range(2)]
        yT = k.sb("yT", [128, nk, 512], BF16)
        wo = [k.sb(f"wo{i}", [128, nk, 256], BF16) for i in range(2)]
        xc = [k.sb(f"xc{i}", [128, 256], F32) for i in range(2)]
        xn = [k.sb(f"xn{i}", [128, 256], F32) for i in range(2)]
        b_yT = k.buf()
        b_wo = [k.buf() for _ in range(2)]
        b_xc = [k.buf() for _ in range(2)]
        b_xn = [k.buf() for _ in range(2)]
        b_out = k.buf()
        cnt = [0, 0]
        blocks = [(i * 512, 512) for i in range(TP // 512)] + [(TP, TS)]
        for (t0, nt) in blocks:
            ntile = max(1, nt // 128)
            P = min(nt, 128)
            for c8 in range(nk // 8):
                k.dma("poolq", yT[:, c8 * 8:(c8 + 1) * 8, 0:nt],
                      aT[c8 * 1024:(c8 + 1) * 1024, t0:t0 + nt].rearrange("(kt p) t -> p kt t", p=128), writes=[b_yT])
            for oc in range(8):
                i = cnt[0] % 2
                cnt[0] += 1
                for c8 in range(nk // 8):
                    k.dma("poolq", wo[i][:, c8 * 8:(c8 + 1) * 8, :],
                          w[c8 * 1024:(c8 + 1) * 1024, oc * 256:(oc + 1) * 256].rearrange("(kt p) c -> p kt c", p=128),
                          writes=[b_wo[i]])
                for ti in range(ntile):
                    r0 = t0 + ti * 128
                    j = cnt[1] % 2
                    cnt[1] += 1
                    k.dma("sp", xc[j][:P, :], x[r0:r0 + P, oc * 256:(oc + 1) * 256], writes=[b_xc[j]])
                    for ft in range(nk):
                        k.op("pe", lambda: nc.tensor.matmul(pp[j][:P, 0:256], lhsT=yT[:, ft, ti * 128:ti * 128 + P],
                                                            rhs=wo[i][:, ft, :], start=(ft == 0), stop=(ft == nk - 1)),
                             reads=[b_yT, b_wo[i]], writes=[bpp[j]], sig=(ft == nk - 1))
                    k.op("dve", lambda: nc.vector.tensor_tensor(out=xn[j][:P, :], in0=pp[j][:P, 0:256], in1=xc[j][:P, :],
                                                               op=ALU.add), reads=[bpp[j], b_xc[j]], writes=[b_xn[j]])
                    k.dma("sp", xo[r0:r0 + P, oc * 256:(oc + 1) * 256], xn[j][:P, :], reads=[b_xn[j]], writes=[b_out])
        k.finish()
    return nc


def _unshard_tok(res, key):
    xp = np.concatenate([r[key][:TP] for r in res], axis=0)
    xs = np.concatenate([r[key][TP:] for r in res], axis=0)
    return xp, xs


def _featmajor_all(xp, xs):
    return np.ascontiguousarray(np.concatenate([xp, xs.reshape(NSB * NST, -1)], axis=0).T)


def kernel(x_prompt, x_sample, state_ssm, state_conv, cache_k, cache_v, norm_w, final_norm_w,
           a_w_in, a_ln_g, a_ln_b, a_w_s, a_b_s, a_w_out,
           b_w_in, b_conv_w, b_conv_b, b_dt_bias, b_a_log, b_d, b_norm_w, b_w_out,
           c_w_in, c_w_out):
    f = lambda a: np.ascontiguousarray(np.asarray(a, dtype=np.float32))
    xp = f(x_prompt)[0]
    xs = f(x_sample)
    norm_w = f(norm_w)
    res = run_gmlp(xp, xs, norm_w[0], f(a_w_in)[0], f(a_ln_g)[0], f(a_ln_b)[0], f(a_w_s)[0], f(a_b_s)[0], f(a_w_out)[0])
    x1p, x1s = _unshard_tok(res, "xo")
    v0 = np.concatenate([r["vsT"].T for r in res], axis=0).reshape(NSB, NST, AW)
    xT1 = _featmajor_all(x1p, x1s)
    bw = f(b_w_in)[0]
    cw_ = f(b_conv_w)[0]
    cb_ = f(b_conv_b)[0]
    sst = f(state_ssm)[0]
    scv = f(state_conv)[0]
    nwT1 = np.ascontiguousarray(norm_w[1].reshape(16, 128).T)
    in_maps = []
    for g in range(NCORES):
        xcols = np.arange(4096 + g * 512, 4096 + (g + 1) * 512)
        bcols = np.arange(4096 + 4096 + g * 128, 4096 + 4096 + (g + 1) * 128)
        ccols = np.arange(4096 + 4096 + 1024 + g * 128, 4096 + 4096 + 1024 + (g + 1) * 128)
        cols = np.concatenate([xcols, bcols, ccols])
        cch = cols - 4096
        hs = slice(g * 8, (g + 1) * 8)
        m = {
            "xT": xT1, "normwT": nwT1,
            "w_z": np.ascontiguousarray(bw[:, g * 512:(g + 1) * 512]),
            "w_xbc": np.ascontiguousarray(bw[:, cols]),
            "w_dt": np.ascontiguousarray(bw[:, 4096 + 6144 + g * 8:4096 + 6144 + (g + 1) * 8]),
            "convw": np.ascontiguousarray(cw_[:, cch].T.reshape(6, 128, 4).transpose(1, 0, 2)),
            "convb": np.ascontiguousarray(cb_[cch].reshape(6, 128).T),
            "hp": np.ascontiguousarray(np.stack([f(b_dt_bias)[0][hs], f(b_a_log)[0][hs], f(b_d)[0][hs]])),
            "dexp": np.ascontiguousarray(np.repeat(f(b_d)[0][hs], 64)),
            "nw": np.ascontiguousarray(f(b_norm_w)[0][g * 512:(g + 1) * 512]),
            "sT_in": np.ascontiguousarray(sst[:, hs].reshape(NSB, 512, 128).transpose(0, 2, 1)),
            "cs_in": np.ascontiguousarray(scv[:, :, cch].transpose(2, 0, 1).reshape(6, 128, NSB, 3).transpose(1, 0, 2, 3)),
        }
        in_maps.append(m)
    resm = _run(build_mamba(), in_maps)
    ssm_p = np.zeros((1, 1, 64, 64, 128), np.float32)
    conv_p = np.zeros((1, 1, 3, 6144), np.float32)
    ssm_s = np.zeros((1, NSB, 64, 64, 128), np.float32)
    conv_s = np.zeros((1, NSB, 3, 6144), np.float32)
    yn_all = np.zeros((NTOK, 4096), np.float32)
    for g in range(NCORES):
        r = resm[g]
        xcols = np.arange(g * 512, (g + 1) * 512)
        bcols = np.arange(4096 + g * 128, 4096 + (g + 1) * 128)
        ccols = np.arange(4096 + 1024 + g * 128, 4096 + 1024 + (g + 1) * 128)
        cch = np.concatenate([xcols, bcols, ccols])
        ssm_p[0, 0, g * 8:(g + 1) * 8] = r["sT_p"].T.reshape(8, 64, 128)
        conv_p[0, 0][:, cch] = r["cs_p"].transpose(1, 0, 2).reshape(768, 3).T
        ssm_s[0, :, g * 8:(g + 1) * 8] = r["sT_s"].transpose(0, 2, 1).reshape(NSB, 8, 64, 128)
        conv_s[0][:, :, cch] = r["cs_s"].transpose(1, 0, 2, 3).reshape(768, NSB, 3).transpose(1, 2, 0)
        yn_all[:, g * 512:(g + 1) * 512] = r["yn"]
    ynp, yns = yn_all[:SEQ], yn_all[SEQ:].reshape(NSB, NST, 4096)
    in_maps = [{"aT": np.ascontiguousarray(tok_shard(ynp, yns, c).T), "w": f(b_w_out)[0], "x": tok_shard(x1p, x1s.reshape(NSB, NST, D), c)}
               for c in range(NCORES)]
    res = _run(build_oproj(4096), in_maps)
    x2p, x2s = _unshard_tok(res, "xo")
    xT2 = _featmajor_all(x2p, x2s)
    nwT2 = np.ascontiguousarray(norm_w[2].reshape(16, 128).T)
    cwi = f(c_w_in)[0]
    ck = f(cache_k)[0]
    cvv = f(cache_v)[0]
    in_maps = []
    for c in range(NCORES):
        wh = []
        for hh in range(2):
            h = 2 * c + hh
            wh.append(np.concatenate([cwi[:, part * 2048 + h * 128: part * 2048 + (h + 1) * 128] for part in range(4)], axis=1))
        in_maps.append({"xT": xT2, "normwT": nwT2, "w_att": np.ascontiguousarray(np.stack(wh)),
                        "ckT": np.ascontiguousarray(ck[:, 2 * c:2 * c + 2].transpose(0, 1, 3, 2)),
                        "cv": np.ascontiguousarray(cvv[:, 2 * c:2 * c + 2])})
    resa = _run(build_attn(), in_maps)
    k_all = np.zeros((16, NTOK, 128), np.float32)
    v_all = np.zeros((16, NTOK, 128), np.float32)
    og_all = np.zeros((NTOK, 2048), np.float32)
    for c in range(NCORES):
        r = resa[c]
        k_all[2 * c:2 * c + 2] = r["kTo"].transpose(0, 2, 1)
        v_all[2 * c:2 * c + 2] = r["vo"]
        og_all[:, c * 256:(c + 1) * 256] = r["ogT"].T
    k_p = k_all[:, :SEQ][None, None]
    v_p = v_all[:, :SEQ][None, None]
    k_s = np.ascontiguousarray(k_all[:, SEQ:].reshape(16, NSB, NST, 128).transpose(1, 0, 2, 3))[None]
    v_s = np.ascontiguousarray(v_all[:, SEQ:].reshape(16, NSB, NST, 128).transpose(1, 0, 2, 3))[None]
    ogp, ogs = og_all[:SEQ], og_all[SEQ:].reshape(NSB, NST, 2048)
    res = run_gmlp(x2p, x2s.reshape(NSB, NST, D), norm_w[3], f(a_w_in)[1], f(a_ln_g)[1], f(a_ln_b)[1], f(a_w_s)[1], f(a_b_s)[1],
                   f(a_w_out)[1], f(final_norm_w), pre=(ogp, ogs, f(c_w_out)[0]))
    yp, ys = _unshard_tok(res, "yo")
    v1 = np.concatenate([r["vsT"].T for r in res], axis=0).reshape(NSB, NST, AW)
    return (np.ascontiguousarray(yp)[None], np.ascontiguousarray(ys).reshape(NSB, NST, D),
            np.ascontiguousarray(np.stack([v0, v1])), ssm_p, conv_p, ssm_s, conv_s,
            np.ascontiguousarray(k_p), np.ascontiguousarray(v_p), k_s, v_s)
```
